# Optimizing a Trainium2 kernel written in Bass

```python
import math
import jax, jax.numpy as jnp
from jax import lax
import numpy as np

D_MODEL = 1024
BATCH = 8
SEQ = 8192
DEPTH = 1

N_META = 16
GRID_W = 64
SSM_GROUP = 16
SSM_STATE = 64
SSM_WIDTH = D_MODEL // 2
SSM_GROUPS = SSM_WIDTH // SSM_GROUP
HEAD_DIM = 64
N_HEADS = D_MODEL // HEAD_DIM
N_KV_HEADS = N_HEADS // 4
Q_WIDTH = N_HEADS * HEAD_DIM
KV_WIDTH = N_KV_HEADS * HEAD_DIM
IN_WIDTH = SSM_WIDTH + Q_WIDTH + 2 * KV_WIDTH + 2 * D_MODEL
D_FF = 4 * D_MODEL
Q_BLOCK = 128
ROPE_THETA = 10000.0
NORM_EPS = 1e-6
DT_MIN = 1e-3
DT_MAX = 1e-1
EIG_RE_MAX = -1e-4

kernel_name = "hybrid_s5_gqa_axial_gated_encoder"


def rms_norm(x, g):
    xf = x.astype(jnp.float32)
    y = xf * lax.rsqrt(jnp.mean(xf * xf, axis=-1, keepdims=True) + NORM_EPS)
    return (y * g.astype(jnp.float32)).astype(x.dtype)


def axial_rope_tables(n_total):
    n_real = n_total - N_META
    rows = n_real // GRID_W
    row_id = jnp.repeat(jnp.arange(rows, dtype=jnp.float32), GRID_W)
    col_id = jnp.tile(jnp.arange(GRID_W, dtype=jnp.float32), rows)
    pairs_per_axis = HEAD_DIM // 4
    inv_freq = ROPE_THETA ** (-jnp.arange(pairs_per_axis, dtype=jnp.float32) / pairs_per_axis)
    ang = jnp.concatenate([row_id[:, None] * inv_freq, col_id[:, None] * inv_freq], axis=-1)
    ang = jnp.concatenate([jnp.zeros((N_META, HEAD_DIM // 2), jnp.float32), ang], axis=0)
    return jnp.cos(ang), jnp.sin(ang)


def apply_rope(t, cos, sin):
    tf = t.astype(jnp.float32).reshape(t.shape[:-1] + (HEAD_DIM // 2, 2))
    t0, t1 = tf[..., 0], tf[..., 1]
    c = cos[:, None, :]
    s = sin[:, None, :]
    out = jnp.stack([t0 * c - t1 * s, t0 * s + t1 * c], axis=-1)
    return out.reshape(t.shape)


def gqa_attention(q, k, v):
    b, l = q.shape[0], q.shape[1]
    rep = N_HEADS // N_KV_HEADS
    scale = HEAD_DIM ** -0.5
    qg = q.reshape(b, l, N_KV_HEADS, rep, HEAD_DIM)

    def attend(qb):
        s = jnp.einsum('bqgrd,bkgd->bgrqk', qb, k) * scale
        p = jax.nn.softmax(s, axis=-1)
        return jnp.einsum('bgrqk,bkgd->bqgrd', p, v)

    out_meta = attend(qg[:, :N_META]).reshape(b, N_META, Q_WIDTH)
    n_real = l - N_META
    n_blk = n_real // Q_BLOCK
    q_blocks = qg[:, N_META:].reshape(b, n_blk, Q_BLOCK, N_KV_HEADS, rep, HEAD_DIM)
    q_blocks = jnp.transpose(q_blocks, (1, 0, 2, 3, 4, 5))
    out_real = lax.map(attend, q_blocks)
    out_real = jnp.transpose(out_real, (1, 0, 2, 3, 4, 5)).reshape(b, n_real, Q_WIDTH)
    return jnp.concatenate([out_meta, out_real], axis=1)


def _complex_scan_combine(e1, e2):
    a1r, a1i, b1r, b1i = e1
    a2r, a2i, b2r, b2i = e2
    return (a1r * a2r - a1i * a2i,
            a1r * a2i + a1i * a2r,
            a2r * b1r - a2i * b1i + b2r,
            a2r * b1i + a2i * b1r + b2i)


def s5_direction(uf, a_re, a_im, log_dt, b_re, b_im, c_re, c_im, reverse):
    l = uf.shape[1]
    f32 = jnp.float32
    dt = jnp.exp(log_dt.astype(f32))[:, None]
    lam_re = jnp.minimum(a_re.astype(f32), EIG_RE_MAX)
    lam_im = a_im.astype(f32)
    mag = jnp.exp(lam_re * dt)
    ang = lam_im * dt
    lb_re = mag * jnp.cos(ang)
    lb_im = mag * jnp.sin(ang)
    num_re = lb_re - 1.0
    num_im = lb_im
    den = lam_re * lam_re + lam_im * lam_im
    f_re = (num_re * lam_re + num_im * lam_im) / den
    f_im = (num_im * lam_re - num_re * lam_im) / den
    br = b_re.astype(f32)
    bi = b_im.astype(f32)
    bb_re = f_re[..., None] * br - f_im[..., None] * bi
    bb_im = f_re[..., None] * bi + f_im[..., None] * br
    bu_re = jnp.einsum('blgp,gnp->blgn', uf, bb_re)
    bu_im = jnp.einsum('blgp,gnp->blgn', uf, bb_im)
    shape_a = (1, l) + lb_re.shape
    a_seq_re = jnp.broadcast_to(lb_re, shape_a)
    a_seq_im = jnp.broadcast_to(lb_im, shape_a)
    _, _, x_re, x_im = lax.associative_scan(
        _complex_scan_combine, (a_seq_re, a_seq_im, bu_re, bu_im), reverse=reverse, axis=1)
    return (jnp.einsum('blgn,gpn->blgp', x_re, c_re.astype(f32))
            - jnp.einsum('blgn,gpn->blgp', x_im, c_im.astype(f32)))


def s5_bidirectional(u, a_re, a_im, log_dt, b_re, b_im, c_re, c_im, d):
    bsz, l = u.shape[0], u.shape[1]
    uf = u.astype(jnp.float32).reshape(bsz, l, SSM_GROUPS, SSM_GROUP)
    y = uf * d.astype(jnp.float32).reshape(SSM_GROUPS, SSM_GROUP)
    for direction in range(2):
        y = y + s5_direction(uf, a_re[direction], a_im[direction], log_dt[direction],
                             b_re[direction], b_im[direction], c_re[direction], c_im[direction],
                             reverse=(direction == 1))
    return y.reshape(bsz, l, SSM_WIDTH)


def setup_inputs(seed: int = 0) -> dict:
    key = jax.random.key(seed)
    ks = jax.random.split(key, 24)
    f32 = jnp.float32

    def nrm(k, shape, scale):
        return jax.random.normal(k, shape, f32) * scale

    def gain(k, shape):
        return 1.0 + 0.02 * jax.random.normal(k, shape, f32)

    n_idx = jnp.arange(SSM_STATE, dtype=f32)
    ssm_shape = (DEPTH, 2, SSM_GROUPS, SSM_STATE)
    return {
        "x": jax.random.normal(ks[0], (BATCH, SEQ, D_MODEL), f32),
        "meta_tokens": nrm(ks[1], (N_META, D_MODEL), 1.0),
        "norm_mix_g": gain(ks[2], (DEPTH, D_MODEL)),
        "w_in": nrm(ks[3], (DEPTH, D_MODEL, IN_WIDTH), D_MODEL ** -0.5),
        "ssm_a_re": -0.5 + 0.01 * jax.random.normal(ks[4], ssm_shape, f32),
        "ssm_a_im": jnp.pi * n_idx + 0.01 * jax.random.normal(ks[5], ssm_shape, f32),
        "ssm_log_dt": jax.random.uniform(ks[6], (DEPTH, 2, SSM_GROUPS), f32,
                                         minval=math.log(DT_MIN), maxval=math.log(DT_MAX)),
        "ssm_b_re": nrm(ks[7], (DEPTH, 2, SSM_GROUPS, SSM_STATE, SSM_GROUP), (2 * SSM_GROUP) ** -0.5),
        "ssm_b_im": nrm(ks[8], (DEPTH, 2, SSM_GROUPS, SSM_STATE, SSM_GROUP), (2 * SSM_GROUP) ** -0.5),
        "ssm_c_re": nrm(ks[9], (DEPTH, 2, SSM_GROUPS, SSM_GROUP, SSM_STATE), SSM_STATE ** -0.5),
        "ssm_c_im": nrm(ks[10], (DEPTH, 2, SSM_GROUPS, SSM_GROUP, SSM_STATE), SSM_STATE ** -0.5),
        "ssm_d": nrm(ks[11], (DEPTH, SSM_WIDTH), 1.0),
        "w_glu": nrm(ks[12], (DEPTH, SSM_WIDTH, SSM_WIDTH), SSM_WIDTH ** -0.5),
        "b_glu": nrm(ks[13], (DEPTH, SSM_WIDTH), 0.02),
        "q_norm_g": gain(ks[14], (DEPTH, HEAD_DIM)),
        "k_norm_g": gain(ks[15], (DEPTH, HEAD_DIM)),
        "w_ssm_proj": nrm(ks[16], (DEPTH, SSM_WIDTH, D_MODEL), SSM_WIDTH ** -0.5),
        "w_attn_proj": nrm(ks[17], (DEPTH, Q_WIDTH, D_MODEL), Q_WIDTH ** -0.5),
        "w_out": nrm(ks[18], (DEPTH, D_MODEL, D_MODEL), D_MODEL ** -0.5),
        "norm_mlp_g": gain(ks[19], (DEPTH, D_MODEL)),
        "w_mlp_in": nrm(ks[20], (DEPTH, D_MODEL, D_FF), D_MODEL ** -0.5),
        "w_mlp_out": nrm(ks[21], (DEPTH, D_FF, D_MODEL), D_FF ** -0.5),
        "norm_final_g": gain(ks[22], (D_MODEL,)),
    }


def reference(x, meta_tokens, norm_mix_g, w_in, ssm_a_re, ssm_a_im, ssm_log_dt,
              ssm_b_re, ssm_b_im, ssm_c_re, ssm_c_im, ssm_d, w_glu, b_glu,
              q_norm_g, k_norm_g, w_ssm_proj, w_attn_proj, w_out,
              norm_mlp_g, w_mlp_in, w_mlp_out, norm_final_g):
    dtype = x.dtype
    bsz = x.shape[0]
    meta = jnp.broadcast_to(meta_tokens.astype(dtype)[None], (bsz, N_META, D_MODEL))
    h_res = jnp.concatenate([meta, x], axis=1)
    l = h_res.shape[1]
    cos, sin = axial_rope_tables(l)
    split_at = [SSM_WIDTH, SSM_WIDTH + Q_WIDTH, SSM_WIDTH + Q_WIDTH + KV_WIDTH,
                SSM_WIDTH + Q_WIDTH + 2 * KV_WIDTH, SSM_WIDTH + Q_WIDTH + 2 * KV_WIDTH + D_MODEL]

    for i in range(DEPTH):
        h = rms_norm(h_res, norm_mix_g[i])
        proj = h @ w_in[i]
        u, q, k, v, g_ssm, g_attn = jnp.split(proj, split_at, axis=-1)

        y = s5_bidirectional(u, ssm_a_re[i], ssm_a_im[i], ssm_log_dt[i], ssm_b_re[i], ssm_b_im[i],
                             ssm_c_re[i], ssm_c_im[i], ssm_d[i])
        z = jax.nn.gelu(y, approximate=False)
        y_ssm = z * jax.nn.sigmoid(z @ w_glu[i].astype(jnp.float32) + b_glu[i].astype(jnp.float32))

        q = rms_norm(q.reshape(bsz, l, N_HEADS, HEAD_DIM), q_norm_g[i])
        k = rms_norm(k.reshape(bsz, l, N_KV_HEADS, HEAD_DIM), k_norm_g[i])
        q = apply_rope(q, cos, sin)
        k = apply_rope(k, cos, sin)
        v = v.reshape(bsz, l, N_KV_HEADS, HEAD_DIM).astype(jnp.float32)
        y_attn = gqa_attention(q, k, v)

        merged = (jax.nn.sigmoid(g_ssm.astype(jnp.float32)) * (y_ssm.astype(dtype) @ w_ssm_proj[i])
                  + jax.nn.sigmoid(g_attn.astype(jnp.float32)) * (y_attn.astype(dtype) @ w_attn_proj[i]))
        h_res = h_res + (merged.astype(dtype) @ w_out[i]).astype(dtype)

        h2 = rms_norm(h_res, norm_mlp_g[i])
        h_res = h_res + (jnp.square(jax.nn.relu(h2 @ w_mlp_in[i])) @ w_mlp_out[i]).astype(dtype)

    out = rms_norm(h_res, norm_final_g)
    return out[:, N_META:]
```

```python
import numpy as np
from contextlib import ExitStack
import concourse.bass as bass
import concourse.mybir as mybir
from concourse.bass_utils import run_bass_kernel_spmd

F32 = mybir.dt.float32
BF16 = mybir.dt.bfloat16
AF = mybir.ActivationFunctionType
ALU = mybir.AluOpType

D = 1024
DFF = 4096
EPS = 1e-6
GV_MIX, GV_MLP, GV_FIN, GV_QG, GV_KG, GV_BGLU, GV_SSMD, GV_MASK, NG = 0, 8, 16, 24, 25, 26, 30, 34, 35


class Ev:
    __slots__ = ("sem", "val", "key")

    def __init__(s, sem, val, key):
        s.sem, s.val, s.key = sem, val, key


class Eng:
    def __init__(s, name, h, sem):
        s.name, s.h, s.sem = name, h, sem
        s.cnt = 0
        s.seen = {}
        s.last = None
        s.selfsync = True

    def wait(s, evs):
        if evs is None:
            return
        if isinstance(evs, Ev):
            evs = [evs]
        for ev in evs:
            if ev is None:
                continue
            if isinstance(ev, (list, tuple)):
                s.wait(ev)
                continue
            if ev.key == s.name and (s.name == "pe" or not s.selfsync):
                continue
            if s.seen.get(ev.key, 0) >= ev.val:
                continue
            s.h.wait_ge(ev.sem, ev.val)
            s.seen[ev.key] = ev.val

    def mark(s, inst):
        s.cnt += 1
        inst.then_inc(s.sem, 1)
        s.last = Ev(s.sem, s.cnt, s.name)
        return s.last


class Slot:
    def __init__(s, sem, key):
        s.sem, s.key = sem, key
        s.cnt = 0
        s.last = None


class Buf:
    def __init__(s, t, slot=None):
        s.t = t
        s.slot = slot
        s.wr = None
        s.rd = {}


class Ring:
    def __init__(s, bufs):
        s.bufs = bufs
        s.i = 0

    def next(s):
        b = s.bufs[s.i % len(s.bufs)]
        s.i += 1
        return b


def build(S, dbg=False):
    LP = S + 128
    NT = LP // 128
    NCH = LP // 8
    nc = bass.Bass("TRN2", target_bir_lowering=False)

    def din(name, shape, dt=F32):
        return nc.dram_tensor(name, shape, dt, kind="ExternalInput").ap()

    def dscr(name, shape, dt):
        return nc.dram_tensor(name, shape, dt, kind=("ExternalOutput" if dbg else "Internal")).ap()

    xT = din("xT", [D, LP])
    w_in = din("w_in", [D, 4096])
    w_glu = din("w_glu", [512, 512])
    w_sp = din("w_sp", [512, D])
    w_ap = din("w_ap", [D, D])
    w_out = din("w_out", [D, D])
    w1 = din("w1", [D, DFF])
    w2 = din("w2", [DFF, D])
    gv_d = din("gv", [128, NG])
    cmat_d = din("cmat", [128, 4 * 128])
    ropeC = din("ropeC", [128, LP])
    ropeS = din("ropeS", [128, LP])
    ssmA = din("ssmA", [128, 96])
    bz = din("bz", [32, 128, 512])
    outT = nc.dram_tensor("outT", [D, S], F32, kind="ExternalOutput").ap()

    winb = dscr("winb", [D, 4096], BF16)
    wglub = dscr("wglub", [512, 512], BF16)
    wspb = dscr("wspb", [512, D], BF16)
    wapb = dscr("wapb", [D, D], BF16)
    woutb = dscr("woutb", [D, D], BF16)
    w1b = dscr("w1b", [D, DFF], BF16)
    w2b = dscr("w2b", [DFF, D], BF16)
    qT_d = dscr("qT_d", [D, LP], BF16)
    zT_d = dscr("zT_d", [512, LP], BF16)
    yaT_d = dscr("yaT_d", [D, S], BF16)
    h1T_d = dscr("h1T_d", [D, S], F32)

    _uid = [0]

    def _sb(name, shape, dt):
        _uid[0] += 1
        return nc.sbuf_tensor(f"{name}_{_uid[0]}", shape, dt)

    def kview(ap):
        return ap.rearrange("(k p) l -> p k l", p=128)

    top = ExitStack()
    with top:
        sems = [top.enter_context(nc.semaphore(f"sem{i}")) for i in range(64)]
        PE = Eng("pe", nc.tensor, sems[0])
        ACT = Eng("act", nc.scalar, sems[1])
        DVE = Eng("dve", nc.vector, sems[2])
        POOL = Eng("pool", nc.gpsimd, sems[3])
        SP = Eng("sp", nc.sync, None)
        engines = [PE, ACT, DVE, POOL]
        slots = [Slot(sems[4 + i], f"dma{i}") for i in range(60)]
        slot_ptr = [0]

        def new_slot():
            s = slots[slot_ptr[0]]
            slot_ptr[0] += 1
            return s

        def op(E, fn, R=(), W=(), mark=True):
            for b in R:
                E.wait(b.wr)
            for b in W:
                E.wait(b.wr)
                E.wait([e for k_, e in b.rd.items() if k_ != E.name])
            inst = fn()
            if mark:
                ev = E.mark(inst)
                for b in R:
                    b.rd[E.name] = ev
                for b in W:
                    b.wr = ev
                    b.rd = {}
                return ev
            return None

        def dma(slot, out, in_, R=(), W=(), upd=()):
            for b in R:
                SP.wait(b.wr)
            for b in W:
                SP.wait(b.wr)
                SP.wait(list(b.rd.values()))
            inst = nc.sync.dma_start(out=out, in_=in_)
            slot.cnt += 16
            inst.then_inc(slot.sem, 16)
            ev = Ev(slot.sem, slot.cnt, slot.key)
            slot.last = ev
            for b in R:
                b.rd[slot.key] = ev
            for b in list(W) + list(upd):
                b.wr = ev
                b.rd = {}
            return ev

        def mm_group(bank, mms, R=()):
            for b in R:
                PE.wait(b.wr)
            PE.wait(bank.wr)
            PE.wait(list(bank.rd.values()))
            n = len(mms)
            inst = None
            for i, m in enumerate(mms):
                if len(m) == 3:
                    st, sp_ = (i == 0), (i == n - 1)
                else:
                    st, sp_ = m[3], m[4]
                inst = nc.tensor.matmul(m[0], lhsT=m[1], rhs=m[2], start=st, stop=sp_)
            ev = PE.mark(inst)
            for b in R:
                b.rd[PE.name] = ev
            bank.wr = ev
            bank.rd = {}
            return ev

        def barrier():
            evs = [E.last for E in engines if E.last is not None]
            evs += [s.last for s in slots if s.last is not None]
            for E in engines + [SP]:
                E.wait(evs)

        def colblocks(n, w=512):
            return [(c, min(w, n - c)) for c in range(0, n, w)]

        gv_t = top.enter_context(_sb("gv", [128, NG], F32))
        cm_f = top.enter_context(_sb("cm_f", [128, 512], F32))
        cm_b = top.enter_context(_sb("cm_b", [128, 512], BF16))
        PSA = top.enter_context(nc.psum_tensor("psa", [128, 7, 512], F32))
        PSB = top.enter_context(nc.psum_tensor("psb", [128, 1024], BF16))
        gv = Buf(gv_t, new_slot())
        cmf = Buf(cm_f, new_slot())
        cmb = Buf(cm_b)
        dma(gv.slot, gv_t[:], gv_d, W=[gv])
        dma(cmf.slot, cm_f[:], cmat_d, W=[cmf])
        op(DVE, lambda: nc.vector.tensor_copy(out=cm_b[:], in_=cm_f[:]), R=[cmf], W=[cmb])
        ones_b = cm_b[:, 0:128]
        blk_b = cm_b[:, 128:256]
        rot_b = cm_b[:, 256:384]
        id_b = cm_b[:, 384:512]
        id_f = cm_f[:, 384:512]
        ones_f = cm_f[:, 0:128]
        DVE.wait(gv.wr)
        ACT.wait(gv.wr)
        POOL.wait(gv.wr)
        banks = [Buf(PSA[:, i, :]) for i in range(7)]
        bankB = Buf(PSB)
        base_slot_ptr = slot_ptr[0]

        def gcol(c):
            return gv_t[:, c:c + 1]

        with ExitStack() as ph:
            slot_ptr[0] = base_slot_ptr
            stg = Ring([Buf(ph.enter_context(_sb(f"wst{i}", [128, 2048], F32)), new_slot()) for i in range(3)])
            ost = Ring([Buf(ph.enter_context(_sb(f"wso{i}", [128, 2048], BF16)), new_slot()) for i in range(3)])
            rr = [0]

            def convert(src, dst, rows, cols, gain0=None):
                for k in range(rows // 128):
                    for c0 in range(0, cols, 2048):
                        n = min(2048, cols - c0)
                        st = stg.next()
                        dma(st.slot, st.t[:, :n], src[k * 128:(k + 1) * 128, c0:c0 + n], W=[st])
                        ob = ost.next()
                        which = rr[0] % 3
                        rr[0] += 1
                        if gain0 is None:
                            if which == 0:
                                op(DVE, lambda: nc.vector.tensor_copy(out=ob.t[:, :n], in_=st.t[:, :n]), R=[st], W=[ob])
                            elif which == 1:
                                op(POOL, lambda: nc.gpsimd.tensor_copy(out=ob.t[:, :n], in_=st.t[:, :n]), R=[st], W=[ob])
                            else:
                                op(ACT, lambda: nc.scalar.copy(out=ob.t[:, :n], in_=st.t[:, :n]), R=[st], W=[ob])
                        else:
                            g = gcol(gain0 + k)
                            if which == 0:
                                op(DVE, lambda: nc.vector.tensor_scalar(out=ob.t[:, :n], in0=st.t[:, :n], scalar1=g, scalar2=None, op0=ALU.mult), R=[st], W=[ob])
                            elif which == 1:
                                op(POOL, lambda: nc.gpsimd.tensor_scalar(out=ob.t[:, :n], in0=st.t[:, :n], scalar1=g, scalar2=None, op0=ALU.mult), R=[st], W=[ob])
                            else:
                                op(ACT, lambda: nc.scalar.activation(out=ob.t[:, :n], in_=st.t[:, :n], func=AF.Copy, scale=g), R=[st], W=[ob])
                        dma(ob.slot, dst[k * 128:(k + 1) * 128, c0:c0 + n], ob.t[:, :n], R=[ob])

            convert(w_in, winb, D, 4096, GV_MIX)
            convert(w_glu, wglub, 512, 512)
            convert(w_sp, wspb, 512, D)
            convert(w_ap, wapb, D, D)
            convert(w_out, woutb, D, D)
            convert(w1, w1b, D, DFF, GV_MLP)
            convert(w2, w2b, DFF, D)
            barrier()

        def in_proj(ph, chunks, wcol0, wcols, uTb=None, kTb=None, Vxb=None):
            W_t = ph.enter_context(_sb("Wp", [128, 8, wcols], BF16))
            Wb = Buf(W_t, new_slot())
            dma(Wb.slot, W_t[:], kview(winb)[:, :, wcol0:wcol0 + wcols], W=[Wb])
            xring = Ring([Buf(ph.enter_context(_sb(f"xt{i}", [128, 8, 512], F32)), new_slot()) for i in range(2)])
            sqr = Ring([Buf(ph.enter_context(_sb(f"sq{i}", [128, 8, 512], BF16))) for i in range(1)])
            xbr = Ring([Buf(ph.enter_context(_sb(f"xb{i}", [128, 8, 512], BF16))) for i in range(2)])
            need_rope = any(c[0] in "qk" for c in chunks)
            if need_rope:
                csr = Ring([Buf(ph.enter_context(_sb(f"cs{i}", [128, 2, 512], F32)), new_slot()) for i in range(2)])
                qor = Ring([Buf(ph.enter_context(_sb(f"qo{i}", [128, 512], BF16)), new_slot()) for i in range(3)])
            f32r = Ring([Buf(ph.enter_context(_sb(f"f32r{i}", [128, 512], F32))) for i in range(8)])
            bfr = Ring([Buf(ph.enter_context(_sb(f"bfr{i}", [128, 512], BF16))) for i in range(4)])
            rstdr = Ring([Buf(ph.enter_context(_sb(f"rstd{i}", [128, 512], F32))) for i in range(2)])
            mainr = Ring(banks[0:3])
            ssb = banks[3]
            hsr = Ring(banks[4:5])
            rotr = Ring(banks[5:7])
            trb = bankB
            xTv = kview(xT)
            for (c0, N) in colblocks(LP):
                xt = xring.next()
                dma(xt.slot, xt.t[:, :, :N], xTv[:, :, c0:c0 + N], W=[xt])
                if need_rope:
                    cs = csr.next()
                    dma(cs.slot, cs.t[:, 0, :N], ropeC[:, c0:c0 + N], W=[cs])
                    dma(cs.slot, cs.t[:, 1, :N], ropeS[:, c0:c0 + N], upd=[cs])
                sq = sqr.next()
                op(ACT, lambda: nc.scalar.activation(out=sq.t[:, :, :N], in_=xt.t[:, :, :N], func=AF.Square), R=[xt], W=[sq])
                xb = xbr.next()
                op(DVE, lambda: nc.vector.tensor_copy(out=xb.t[:, :, :N], in_=xt.t[:, :, :N]), R=[xt], W=[xb])
                mm_group(ssb, [(ssb.t[:, :N], ones_b, sq.t[:, k, :N]) for k in range(8)], R=[sq, cmb])
                rt = f32r.next()
                op(ACT, lambda: nc.scalar.activation(out=rt.t[:, :N], in_=ssb.t[:, :N], func=AF.Sqrt, scale=1.0 / D, bias=eps_c), R=[ssb], W=[rt])
                rstd = rstdr.next()
                op(DVE, lambda: nc.vector.reciprocal(out=rstd.t[:, :N], in_=rt.t[:, :N]), R=[rt], W=[rstd])
                for (kind, idx, wc) in chunks:
                    ps = mainr.next()
                    mm_group(ps, [(ps.t[:, :N], W_t[:, k, wc:wc + 128], xb.t[:, k, :N]) for k in range(8)], R=[Wb, xb])
                    if kind == "u":
                        op(DVE, lambda: nc.vector.tensor_tensor(out=uTb.t[:, idx, c0:c0 + N], in0=ps.t[:, :N], in1=rstd.t[:, :N], op=ALU.mult), R=[ps, rstd], W=[uTb])
                    elif kind == "v":
                        vt = bfr.next()
                        op(DVE, lambda: nc.vector.tensor_tensor(out=vt.t[:, :N], in0=ps.t[:, :N], in1=rstd.t[:, :N], op=ALU.mult), R=[ps, rstd], W=[vt])
                        for s_ in range(N // 128):
                            tt = c0 // 128 + s_
                            op(PE, lambda: nc.tensor.transpose(out=trb.t[:, s_ * 128:(s_ + 1) * 128], in_=vt.t[:, s_ * 128:(s_ + 1) * 128], identity=id_b), R=[vt, cmb], W=[trb])
                            op(ACT, lambda: nc.scalar.copy(out=Vxb.t[:, tt, 2 * idx:2 * idx + 2, 0:64],
                                                           in_=trb.t[:, s_ * 128:(s_ + 1) * 128].rearrange("p (g d) -> p g d", g=2)), R=[trb], W=[Vxb])
                    else:
                        tq = f32r.next()
                        op(DVE, lambda: nc.vector.tensor_tensor(out=tq.t[:, :N], in0=ps.t[:, :N], in1=rstd.t[:, :N], op=ALU.mult), R=[ps, rstd], W=[tq])
                        sq2 = bfr.next()
                        op(ACT, lambda: nc.scalar.activation(out=sq2.t[:, :N], in_=tq.t[:, :N], func=AF.Square), R=[tq], W=[sq2])
                        hs = hsr.next()
                        mm_group(hs, [(hs.t[:, :N], blk_b, sq2.t[:, :N])], R=[sq2, cmb])
                        rt2 = f32r.next()
                        op(ACT, lambda: nc.scalar.activation(out=rt2.t[:, :N], in_=hs.t[:, :N], func=AF.Sqrt, scale=1.0 / 64, bias=eps_c), R=[hs], W=[rt2])
                        rq = f32r.next()
                        op(DVE, lambda: nc.vector.reciprocal(out=rq.t[:, :N], in_=rt2.t[:, :N]), R=[rt2], W=[rq])
                        tq2 = bfr.next()
                        gc_ = gcol(GV_QG if kind == "q" else GV_KG)
                        op(DVE, lambda: nc.vector.scalar_tensor_tensor(out=tq2.t[:, :N], in0=tq.t[:, :N], scalar=gc_, in1=rq.t[:, :N], op0=ALU.mult, op1=ALU.mult), R=[tq, rq], W=[tq2])
                        rot = rotr.next()
                        mm_group(rot, [(rot.t[:, :N], rot_b, tq2.t[:, :N])], R=[tq2, cmb])
                        t1 = f32r.next()
                        op(POOL, lambda: nc.gpsimd.tensor_tensor(out=t1.t[:, :N], in0=tq2.t[:, :N], in1=cs.t[:, 0, :N], op=ALU.mult), R=[tq2, cs], W=[t1])
                        t2 = f32r.next()
                        op(DVE, lambda: nc.vector.tensor_tensor(out=t2.t[:, :N], in0=rot.t[:, :N], in1=cs.t[:, 1, :N], op=ALU.mult), R=[rot, cs], W=[t2])
                        if kind == "k":
                            op(POOL, lambda: nc.gpsimd.tensor_tensor(out=kTb.t[:, idx, c0:c0 + N], in0=t1.t[:, :N], in1=t2.t[:, :N], op=ALU.add), R=[t1, t2], W=[kTb])
                        else:
                            qo = qor.next()
                            op(POOL, lambda: nc.gpsimd.tensor_tensor(out=qo.t[:, :N], in0=t1.t[:, :N], in1=t2.t[:, :N], op=ALU.add), R=[t1, t2], W=[qo])
                            dma(qo.slot, qT_d[idx * 128:(idx + 1) * 128, c0:c0 + N], qo.t[:, :N], R=[qo])

        eps_t = top.enter_context(_sb("eps", [128, 1], F32))
        epsb = Buf(eps_t)
        op(DVE, lambda: nc.vector.memset(eps_t[:], EPS), W=[epsb])
        ACT.wait(epsb.wr)
        eps_c = eps_t[:, 0:1]

        with ExitStack() as ph_prm:
            ph = ph_prm
            slot_ptr[0] = base_slot_ptr
            NPR = 120
            PR_t = ph.enter_context(_sb("PR", [128, NPR * 32], F32))
            PRb = Buf(PR_t, new_slot())
            pr_i = [0]

            def col():
                i = pr_i[0]
                pr_i[0] += 1
                assert i < NPR
                return PR_t[:, i * 32:(i + 1) * 32]

            A_re, A_im, LDT = col(), col(), col()
            dma(PRb.slot, PR_t[:, 0:96], ssmA, W=[PRb])

            def V(fn):
                return op(DVE, fn, R=[PRb], W=[PRb])

            def A(fn):
                return op(ACT, fn, R=[PRb], W=[PRb])

            def TT(o, a, b, o_):
                V(lambda: nc.vector.tensor_tensor(out=o, in0=a, in1=b, op=o_))

            def TS(o, a, s1, o1, s2=None, o2=None):
                if o2 is None:
                    V(lambda: nc.vector.tensor_scalar(out=o, in0=a, scalar1=s1, scalar2=None, op0=o1))
                else:
                    V(lambda: nc.vector.tensor_scalar(out=o, in0=a, scalar1=s1, scalar2=s2, op0=o1, op1=o2))

            t1, t2 = col(), col()

            def cmul(orr, oi, ar, ai, br, bi):
                TT(t1, ar, br, ALU.mult)
                TT(t2, ai, bi, ALU.mult)
                TT(orr, t1, t2, ALU.subtract)
                TT(t1, ar, bi, ALU.mult)
                TT(t2, ai, br, ALU.mult)
                TT(oi, t1, t2, ALU.add)

            dt_, lre, mag, ang, cs_, sn_, c2, s2 = col(), col(), col(), col(), col(), col(), col(), col()
            pio2 = col()
            V(lambda: nc.vector.memset(pio2, float(np.pi / 2)))
            A(lambda: nc.scalar.activation(out=dt_, in_=LDT, func=AF.Exp))
            TS(lre, A_re, -1e-4, ALU.min)
            TT(t1, lre, dt_, ALU.mult)
            A(lambda: nc.scalar.activation(out=mag, in_=t1, func=AF.Exp))
            TT(ang, A_im, dt_, ALU.mult)
            ki_t = ph.enter_context(_sb("ki", [128, 32], mybir.dt.int32))
            kf, rr_, mm_ = col(), col(), col()
            TS(kf, ang, float(1.0 / (2 * np.pi)), ALU.mult)
            V(lambda: nc.vector.tensor_copy(out=ki_t[:], in_=kf))
            V(lambda: nc.vector.tensor_copy(out=kf, in_=ki_t[:]))
            V(lambda: nc.vector.scalar_tensor_tensor(out=rr_, in0=kf, scalar=float(-2 * np.pi), in1=ang, op0=ALU.mult, op1=ALU.add))
            pi_c, npi_c = col(), col()
            V(lambda: nc.vector.memset(pi_c, float(np.pi)))
            V(lambda: nc.vector.memset(npi_c, float(-np.pi)))
            TT(mm_, rr_, pi_c, ALU.is_gt)
            V(lambda: nc.vector.scalar_tensor_tensor(out=rr_, in0=mm_, scalar=float(-2 * np.pi), in1=rr_, op0=ALU.mult, op1=ALU.add))
            TT(mm_, rr_, npi_c, ALU.is_lt)
            V(lambda: nc.vector.scalar_tensor_tensor(out=rr_, in0=mm_, scalar=float(2 * np.pi), in1=rr_, op0=ALU.mult, op1=ALU.add))
            TS(s2, rr_, -1.0, ALU.mult)
            TT(c2, rr_, s2, ALU.max)
            A(lambda: nc.scalar.activation(out=sn_, in_=rr_, func=AF.Sin))
            A(lambda: nc.scalar.activation(out=cs_, in_=c2, func=AF.Sin, scale=-1.0, bias=pio2[:, 0:1]))
            lbr, lbi = col(), col()
            TT(lbr, mag, cs_, ALU.mult)
            TT(lbi, mag, sn_, ALU.mult)
            numr, den, rden, fre, fim = col(), col(), col(), col(), col()
            TS(numr, lbr, -1.0, ALU.add)
            TT(t1, lre, lre, ALU.mult)
            TT(t2, A_im, A_im, ALU.mult)
            TT(den, t1, t2, ALU.add)
            V(lambda: nc.vector.reciprocal(out=rden, in_=den))
            TT(t1, numr, lre, ALU.mult)
            TT(t2, lbi, A_im, ALU.mult)
            TT(fre, t1, t2, ALU.add)
            TT(fre, fre, rden, ALU.mult)
            TT(t1, lbi, lre, ALU.mult)
            TT(t2, numr, A_im, ALU.mult)
            TT(fim, t1, t2, ALU.subtract)
            TT(fim, fim, rden, ALU.mult)
            Pre = [col() for _ in range(9)]
            Pim = [col() for _ in range(9)]
            NPim = [col() for _ in range(9)]
            V(lambda: nc.vector.memset(Pre[0], 1.0))
            V(lambda: nc.vector.memset(Pim[0], 0.0))
            for tau in range(8):
                cmul(Pre[tau + 1], Pim[tau + 1], Pre[tau], Pim[tau], lbr, lbi)
            for tau in range(9):
                TS(NPim[tau], Pim[tau], -1.0, ALU.mult)
            Gre = [col() for _ in range(8)]
            Gim = [col() for _ in range(8)]
            for tau in range(8):
                cmul(Gre[tau], Gim[tau], Pre[tau], Pim[tau], fre, fim)
            nlev = 0
            while (1 << nlev) < NCH:
                nlev += 1
            Hre = [col() for _ in range(nlev)]
            Him = [col() for _ in range(nlev)]
            NHim = [col() for _ in range(nlev)]
            V(lambda: nc.vector.tensor_copy(out=Hre[0], in_=Pre[8]))
            V(lambda: nc.vector.tensor_copy(out=Him[0], in_=Pim[8]))
            for k in range(1, nlev):
                cmul(Hre[k], Him[k], Hre[k - 1], Him[k - 1], Hre[k - 1], Him[k - 1])
            for k in range(nlev):
                TS(NHim[k], Him[k], -1.0, ALU.mult)
            ev_pr = V(lambda: nc.vector.tensor_copy(out=t1, in_=t2))
            for E in (ACT, POOL, PE):
                E.wait(ev_pr)
            if dbg:
                prdbg = nc.dram_tensor("prdbg", [128, NPR * 32], F32, kind="ExternalOutput").ap()
                dma(PRb.slot, prdbg, PR_t[:], R=[PRb])

            base_slot_ptr = slot_ptr[0]
            for hf in range(2):
              with ExitStack() as ph_ssm:
                uT_t = ph_ssm.enter_context(_sb(f"uT{hf}", [128, 2, LP], BF16))
                uTb = Buf(uT_t)
                with ExitStack() as ph:
                    slot_ptr[0] = base_slot_ptr
                    in_proj(ph, [("u", i, i * 128) for i in range(2)], hf * 256, 256, uTb=uTb)
                    barrier()
                slot_ptr[0] = base_slot_ptr
                ph = ph_ssm
                bzr = Ring([Buf(ph.enter_context(_sb(f"bz{i}", [128, 4, 128], F32)), new_slot()) for i in range(2)])
                Wst_r = Ring([Buf(ph.enter_context(_sb(f"Wst{i}", [128, 16, 128], BF16))) for i in range(2)])
                wtr = Ring([Buf(ph.enter_context(_sb(f"wt{i}", [128, 128], F32))) for i in range(4)])
                tmr = Ring([Buf(ph.enter_context(_sb(f"tm{i}", [128, 128], F32))) for i in range(4)])
                BbZ_t = ph.enter_context(_sb("BbZ", [128, 16, 128], BF16))
                BbZb = Buf(BbZ_t)
                CL_t = ph.enter_context(_sb("CL", [128, 8 * 18, 128], BF16))
                CLb = Buf(CL_t)
                BD_t = ph.enter_context(_sb("BD", [128, 16, 128], BF16))
                BDb = Buf(BD_t)
                XA_t = ph.enter_context(_sb("XA", [128, 2, NCH], F32))
                XB_t = ph.enter_context(_sb("XB", [128, 2, NCH], F32))
                XAb, XBb = Buf(XA_t), Buf(XB_t)
                Xb_t = ph.enter_context(_sb("Xb", [128, 8, 2, NCH], BF16))
                Xbb = Buf(Xb_t)
                zbuf_t = ph.enter_context(_sb("zbuf", [128, LP], BF16))
                zb = Buf(zbuf_t, new_slot())
                ytr = Ring([Buf(ph.enter_context(_sb(f"yt{i}", [128, 512], F32))) for i in range(3)])
                op(POOL, lambda: nc.gpsimd.memset(Xb_t[:].rearrange("p a b c -> p (a b c)"), 0.0), W=[Xbb])
                trf = banks[0]
                sring = Ring(banks[1:3])
                kring = Ring(banks[3:4])
                accr = Ring(banks[4:7])
                cbs = colblocks(NCH)

                for gcl in range(2):
                    gc = hf * 2 + gcl
                    uv = uT_t[:, gcl, :].rearrange("p (c i) -> p c i", i=8)
                    for gpl in range(4):
                        gp = gc * 4 + gpl
                        for d in range(2):
                            cidx = d * 16 + gp
                            slot8 = gpl * 2 + d
                            bzb = bzr.next()
                            dma(bzb.slot, bzb.t[:], bz[cidx].rearrange("p (a c) -> p a c", a=4), W=[bzb])
                            Bre, Bim, Cre, Cim = (bzb.t[:, a, :] for a in range(4))
                            Wst = Wst_r.next()

                            def sc(arr):
                                return arr[:, cidx:cidx + 1]

                            for i in range(8):
                                tau = (7 - i) if d == 0 else i
                                kr, ki = sc(Gre[tau]), sc(Gim[tau])
                                for part in range(2):
                                    tm = tmr.next()
                                    wt = wtr.next()
                                    if part == 0:
                                        op(POOL, lambda: nc.gpsimd.tensor_scalar(out=tm.t[:], in0=Bim, scalar1=ki, scalar2=None, op0=ALU.mult), R=[bzb], W=[tm])
                                        op(DVE, lambda: nc.vector.scalar_tensor_tensor(out=wt.t[:], in0=Bre, scalar=kr, in1=tm.t[:], op0=ALU.mult, op1=ALU.subtract), R=[bzb, tm], W=[wt])
                                    else:
                                        op(POOL, lambda: nc.gpsimd.tensor_scalar(out=tm.t[:], in0=Bre, scalar1=ki, scalar2=None, op0=ALU.mult), R=[bzb], W=[tm])
                                        op(DVE, lambda: nc.vector.scalar_tensor_tensor(out=wt.t[:], in0=Bim, scalar=kr, in1=tm.t[:], op0=ALU.mult, op1=ALU.add), R=[bzb, tm], W=[wt])
                                    op(PE, lambda: nc.tensor.transpose(out=trf.t[:, 0:128], in_=wt.t[:], identity=id_f), R=[wt, cmf], W=[trf])
                                    op(ACT, lambda: nc.scalar.copy(out=Wst.t[:, i * 2 + part, :], in_=trf.t[:, 0:128]), R=[trf], W=[Wst])
                                    if tau == 0:
                                        op(POOL, lambda: nc.gpsimd.tensor_copy(out=BbZ_t[:, slot8 * 2 + part, :], in_=wt.t[:]), R=[wt], W=[BbZb])
                            for tau in range(9):
                                for part in range(2):
                                    tm = tmr.next()
                                    o_ = CL_t[:, slot8 * 18 + tau * 2 + part, :]
                                    if part == 0:
                                        op(POOL, lambda: nc.gpsimd.tensor_scalar(out=tm.t[:], in0=Cim, scalar1=sc(Pim[tau]), scalar2=None, op0=ALU.mult), R=[bzb], W=[tm])
                                        op(DVE, lambda: nc.vector.scalar_tensor_tensor(out=o_, in0=Cre, scalar=sc(Pre[tau]), in1=tm.t[:], op0=ALU.mult, op1=ALU.subtract), R=[bzb, tm], W=[CLb])
                                    else:
                                        op(POOL, lambda: nc.gpsimd.tensor_scalar(out=tm.t[:], in0=Cim, scalar1=sc(Pre[tau]), scalar2=None, op0=ALU.mult), R=[bzb], W=[tm])
                                        op(DVE, lambda: nc.vector.scalar_tensor_tensor(out=o_, in0=Cre, scalar=sc(NPim[tau]), in1=tm.t[:], op0=ALU.mult, op1=ALU.subtract), R=[bzb, tm], W=[CLb])
                            for part in range(2):
                                for (cb0, n) in cbs:
                                    sbk = sring.next()
                                    mm_group(sbk, [(sbk.t[:, :n], Wst.t[:, i * 2 + part, :], uv[:, cb0:cb0 + n, i]) for i in range(8)], R=[Wst, uTb])
                                    op(ACT, lambda: nc.scalar.copy(out=XA_t[:, part, cb0:cb0 + n], in_=sbk.t[:, :n]), R=[sbk], W=[XAb])
                            src, dst = XAb, XBb
                            for k in range(nlev):
                                s_ = 1 << k
                                a_, b_, nb_ = sc(Hre[k]), sc(Him[k]), sc(NHim[k])
                                X, Y = src.t, dst.t
                                if d == 0:
                                    lo, hi = slice(0, NCH - s_), slice(s_, NCH)
                                    keep = slice(0, s_)
                                else:
                                    lo, hi = slice(s_, NCH), slice(0, NCH - s_)
                                    keep = slice(NCH - s_, NCH)
                                op(ACT, lambda: nc.scalar.copy(out=Y[:, :, keep], in_=X[:, :, keep]), R=[src], W=[dst])
                                op(DVE, lambda: nc.vector.scalar_tensor_tensor(out=Y[:, 0, hi], in0=X[:, 0, lo], scalar=a_, in1=X[:, 0, hi], op0=ALU.mult, op1=ALU.add), R=[src], W=[dst])
                                op(DVE, lambda: nc.vector.scalar_tensor_tensor(out=Y[:, 0, hi], in0=X[:, 1, lo], scalar=nb_, in1=Y[:, 0, hi], op0=ALU.mult, op1=ALU.add), R=[src], W=[dst])
                                op(DVE, lambda: nc.vector.scalar_tensor_tensor(out=Y[:, 1, hi], in0=X[:, 0, lo], scalar=b_, in1=X[:, 1, hi], op0=ALU.mult, op1=ALU.add), R=[src], W=[dst])
                                op(DVE, lambda: nc.vector.scalar_tensor_tensor(out=Y[:, 1, hi], in0=X[:, 1, lo], scalar=a_, in1=Y[:, 1, hi], op0=ALU.mult, op1=ALU.add), R=[src], W=[dst])
                                src, dst = dst, src
                            Z = src.t
                            if d == 0:
                                op(ACT, lambda: nc.scalar.copy(out=Xb_t[:, slot8, :, 1:NCH], in_=Z[:, :, 0:NCH - 1]), R=[src], W=[Xbb])
                            else:
                                op(ACT, lambda: nc.scalar.copy(out=Xb_t[:, slot8, :, 0:NCH - 1], in_=Z[:, :, 1:NCH]), R=[src], W=[Xbb])
                    for d in range(2):
                        for tau in range(8):
                            kb = kring.next()
                            mms = []
                            for gpl in range(4):
                                s8 = gpl * 2 + d
                                for part in range(2):
                                    mms.append((kb.t[:, 0:128], BbZ_t[:, s8 * 2 + part, :], CL_t[:, s8 * 18 + tau * 2 + part, :]))
                            mm_group(kb, mms, R=[BbZb, CLb])
                            op(ACT, lambda: nc.scalar.copy(out=BD_t[:, d * 8 + tau, :], in_=kb.t[:, 0:128]), R=[kb], W=[BDb])
                    zv = zbuf_t[:].rearrange("p (c i) -> p c i", i=8)
                    for j in range(8):
                        for (cb0, n) in cbs:
                            ab = accr.next()
                            mms = []
                            for i in range(0, j + 1):
                                mms.append((ab.t[:, :n], BD_t[:, 0 * 8 + (j - i), :], uv[:, cb0:cb0 + n, i]))
                            for i in range(j, 8):
                                mms.append((ab.t[:, :n], BD_t[:, 1 * 8 + (i - j), :], uv[:, cb0:cb0 + n, i]))
                            for gpl in range(4):
                                for d in range(2):
                                    s8 = gpl * 2 + d
                                    tau = (j + 1) if d == 0 else (8 - j)
                                    for part in range(2):
                                        mms.append((ab.t[:, :n], CL_t[:, s8 * 18 + tau * 2 + part, :], Xb_t[:, s8, part, cb0:cb0 + n]))
                            mm_group(ab, mms, R=[BDb, uTb, CLb, Xbb])
                            yt = ytr.next()
                            op(DVE, lambda: nc.vector.scalar_tensor_tensor(out=yt.t[:, :n], in0=uv[:, cb0:cb0 + n, j], scalar=gcol(GV_SSMD + gc), in1=ab.t[:, :n], op0=ALU.mult, op1=ALU.add), R=[ab, uTb], W=[yt])
                            op(ACT, lambda: nc.scalar.activation(out=zv[:, cb0:cb0 + n, j], in_=yt.t[:, :n], func=AF.Gelu), R=[yt], W=[zb])
                    dma(zb.slot, zT_d[gc * 128:(gc + 1) * 128, :], zbuf_t[:], R=[zb])
                barrier()

        with ExitStack() as ph_att:
            kT_t = ph_att.enter_context(_sb("kT", [128, 2, LP], BF16))
            Vx_t = ph_att.enter_context(_sb("Vx", [128, NT, 4, 65], BF16))
            kTb, Vxb = Buf(kT_t), Buf(Vx_t)
            op(POOL, lambda: nc.gpsimd.memset(Vx_t[:].rearrange("p a b c -> p (a b c)"), 1.0), W=[Vxb])
            with ExitStack() as ph:
                slot_ptr[0] = base_slot_ptr
                chunks = [("q", i, i * 128) for i in range(8)] + [("k", i, 1024 + i * 128) for i in range(2)] + [("v", i, 1280 + i * 128) for i in range(2)]
                in_proj(ph, chunks, 512, 1536, kTb=kTb, Vxb=Vxb)
                barrier()
            slot_ptr[0] = base_slot_ptr
            ph = ph_att
            Qr = Ring([Buf(ph.enter_context(_sb(f"Q{i}", [128, LP], BF16)), new_slot()) for i in range(2)])
            PTr = Ring([Buf(ph.enter_context(_sb(f"PT{i}", [128, 2, 512], BF16))) for i in range(3)])
            recr = Ring([Buf(ph.enter_context(_sb(f"rec{i}", [128, 512], F32))) for i in range(2)])
            bcr = Ring([Buf(ph.enter_context(_sb(f"bcs{i}", [64, 512], F32))) for i in range(2)])
            yor = Ring([Buf(ph.enter_context(_sb(f"yo{i}", [64, 512], BF16)), new_slot()) for i in range(3)])
            Sr = Ring([Buf(PSA[:, 0:2, :]), Buf(PSA[:, 2:4, :])])
            Or = Ring(banks[4:6])
            bcb = banks[6]
            groups = [[0]] + [[1 + 2 * i, 2 + 2 * i] for i in range((NT - 1) // 2)]
            if (NT - 1) % 2:
                groups.append([NT - 1])
            for h in range(16):
                g = h // 4
                pb = (g % 2) * 64
                kc = g // 2
                Qb = Qr.next()
                dma(Qb.slot, Qb.t[pb:pb + 64, :], qT_d[h * 64:(h + 1) * 64, :], W=[Qb])
                for (q0, NQ) in colblocks(S):
                    qc0 = 128 + q0
                    Ob = Or.next()
                    pend = None
                    first_pv = True

                    def do_pv(pend, last):
                        nonlocal first_pv
                        PTb, grp = pend
                        mms = []
                        for jj, kt in enumerate(grp):
                            mms.append((Ob.t[0:65, :NQ], Vx_t[:, kt, g, :], PTb.t[:, jj, :NQ], first_pv, last and jj == len(grp) - 1))
                            first_pv = False
                        for b in (PTb, Vxb):
                            PE.wait(b.wr)
                        if mms[0][3]:
                            PE.wait(Ob.wr)
                            PE.wait(list(Ob.rd.values()))
                        inst = None
                        for m in mms:
                            inst = nc.tensor.matmul(m[0], lhsT=m[1], rhs=m[2], start=m[3], stop=m[4])
                        ev = PE.mark(inst)
                        PTb.rd[PE.name] = ev
                        if last:
                            Ob.wr = ev
                            Ob.rd = {}

                    for gi, grp in enumerate(groups):
                        Sb = Sr.next()
                        mms = [(Sb.t[:, jj, :NQ], kT_t[pb:pb + 64, kc, kt * 128:(kt + 1) * 128], Qb.t[pb:pb + 64, qc0:qc0 + NQ], True, True) for jj, kt in enumerate(grp)]
                        mm_group(Sb, mms, R=[kTb, Qb])
                        if pend is not None:
                            do_pv(pend, False)
                        PTb = PTr.next()
                        ng = len(grp)
                        if gi == 0:
                            op(ACT, lambda: nc.scalar.activation(out=PTb.t[:, 0, :NQ], in_=Sb.t[:, 0, :NQ], func=AF.Exp, scale=0.125, bias=gcol(GV_MASK)), R=[Sb], W=[PTb])
                        else:
                            op(ACT, lambda: nc.scalar.activation(out=PTb.t[:, 0:ng, :NQ], in_=Sb.t[:, 0:ng, :NQ], func=AF.Exp, scale=0.125), R=[Sb], W=[PTb])
                        pend = (PTb, grp)
                    do_pv(pend, True)
                    rec = recr.next()
                    op(DVE, lambda: nc.vector.reciprocal(out=rec.t[64:65, :NQ], in_=Ob.t[64:65, :NQ]), R=[Ob], W=[rec])
                    mm_group(bcb, [(bcb.t[0:64, :NQ], ones_f[64:65, 0:64], rec.t[64:65, :NQ])], R=[rec, cmf])
                    bcs = bcr.next()
                    op(ACT, lambda: nc.scalar.copy(out=bcs.t[:, :NQ], in_=bcb.t[0:64, :NQ]), R=[bcb], W=[bcs])
                    yo = yor.next()
                    op(DVE, lambda: nc.vector.tensor_tensor(out=yo.t[:, :NQ], in0=Ob.t[0:64, :NQ], in1=bcs.t[:, :NQ], op=ALU.mult), R=[Ob, bcs], W=[yo])
                    dma(yo.slot, yaT_d[h * 64:(h + 1) * 64, q0:q0 + NQ], yo.t[:, :NQ], R=[yo])
            barrier()

        with ExitStack() as ph:
            slot_ptr[0] = base_slot_ptr

            def wload(name, src, kchunks, cols, c0=0):
                t = ph.enter_context(_sb(name, [128, kchunks, cols], BF16))
                b = Buf(t, new_slot())
                dma(b.slot, t[:], kview(src)[:, :, c0:c0 + cols], W=[b])
                return b
            Wg = wload("Wg", winb, 8, 2048, 2048)
            Wgl = wload("Wgl", wglub, 4, 512)
            Wsp = wload("Wsp", wspb, 4, D)
            Wap = wload("Wap", wapb, 8, D)
            Wo = wload("Wo", woutb, 8, D)
            xring = Ring([Buf(ph.enter_context(_sb(f"xt{i}", [128, 8, 512], F32)), new_slot()) for i in range(2)])
            sqb_ = Buf(ph.enter_context(_sb("sq3", [128, 8, 512], BF16)))
            xbb_ = Buf(ph.enter_context(_sb("xb3", [128, 8, 512], BF16)))
            ztr = Ring([Buf(ph.enter_context(_sb(f"zt{i}", [128, 4, 512], BF16)), new_slot()) for i in range(2)])
            yar = Ring([Buf(ph.enter_context(_sb(f"ya{i}", [128, 8, 512], BF16)), new_slot()) for i in range(2)])
            ysb = Buf(ph.enter_context(_sb("ys", [128, 4, 512], BF16)))
            mgb = Buf(ph.enter_context(_sb("mg", [128, 8, 512], BF16)))
            f32r = Ring([Buf(ph.enter_context(_sb(f"f3r{i}", [128, 512], F32))) for i in range(10)])
            rstdb = Buf(ph.enter_context(_sb("rstd3", [128, 512], F32)))
            h1r = Ring([Buf(ph.enter_context(_sb(f"h1s{i}", [128, 512], F32)), new_slot()) for i in range(3)])
            pr = Ring(banks)
            xTv = kview(xT)
            zTv = kview(zT_d)
            yaTv = kview(yaT_d)
            N = 512
            for (q0, _) in colblocks(S):
                c0 = 128 + q0
                xt = xring.next()
                dma(xt.slot, xt.t[:], xTv[:, :, c0:c0 + N], W=[xt])
                zt = ztr.next()
                dma(zt.slot, zt.t[:], zTv[:, :, c0:c0 + N], W=[zt])
                ya = yar.next()
                dma(ya.slot, ya.t[:], yaTv[:, :, q0:q0 + N], W=[ya])
                op(ACT, lambda: nc.scalar.activation(out=sqb_.t[:], in_=xt.t[:], func=AF.Square), R=[xt], W=[sqb_])
                op(DVE, lambda: nc.vector.tensor_copy(out=xbb_.t[:], in_=xt.t[:]), R=[xt], W=[xbb_])
                ssb = pr.next()
                mm_group(ssb, [(ssb.t[:], ones_b, sqb_.t[:, k, :]) for k in range(8)], R=[sqb_, cmb])
                rt = f32r.next()
                op(ACT, lambda: nc.scalar.activation(out=rt.t[:], in_=ssb.t[:], func=AF.Sqrt, scale=1.0 / D, bias=eps_c), R=[ssb], W=[rt])
                op(DVE, lambda: nc.vector.reciprocal(out=rstdb.t[:], in_=rt.t[:]), R=[rt], W=[rstdb])
                for oc in range(4):
                    ps = pr.next()
                    mm_group(ps, [(ps.t[:], Wgl.t[:, k, oc * 128:(oc + 1) * 128], zt.t[:, k, :]) for k in range(4)], R=[Wgl, zt])
                    sg = f32r.next()
                    op(ACT, lambda: nc.scalar.activation(out=sg.t[:], in_=ps.t[:], func=AF.Sigmoid, bias=gcol(GV_BGLU + oc)), R=[ps], W=[sg])
                    op(DVE, lambda: nc.vector.tensor_tensor(out=ysb.t[:, oc, :], in0=zt.t[:, oc, :], in1=sg.t[:], op=ALU.mult), R=[zt, sg], W=[ysb])
                for oc in range(8):
                    sgs = []
                    for gi in range(2):
                        ps = pr.next()
                        wc = gi * 1024 + oc * 128
                        mm_group(ps, [(ps.t[:], Wg.t[:, k, wc:wc + 128], xbb_.t[:, k, :]) for k in range(8)], R=[Wg, xbb_])
                        tg = f32r.next()
                        op(DVE, lambda: nc.vector.tensor_tensor(out=tg.t[:], in0=ps.t[:], in1=rstdb.t[:], op=ALU.mult), R=[ps, rstdb], W=[tg])
                        sg = f32r.next()
                        op(ACT, lambda: nc.scalar.activation(out=sg.t[:], in_=tg.t[:], func=AF.Sigmoid), R=[tg], W=[sg])
                        sgs.append(sg)
                    ps = pr.next()
                    mm_group(ps, [(ps.t[:], Wsp.t[:, k, oc * 128:(oc + 1) * 128], ysb.t[:, k, :]) for k in range(4)], R=[Wsp, ysb])
                    m1 = f32r.next()
                    op(DVE, lambda: nc.vector.tensor_tensor(out=m1.t[:], in0=ps.t[:], in1=sgs[0].t[:], op=ALU.mult), R=[ps, sgs[0]], W=[m1])
                    ps2 = pr.next()
                    mm_group(ps2, [(ps2.t[:], Wap.t[:, k, oc * 128:(oc + 1) * 128], ya.t[:, k, :]) for k in range(8)], R=[Wap, ya])
                    m2 = f32r.next()
                    op(DVE, lambda: nc.vector.tensor_tensor(out=m2.t[:], in0=ps2.t[:], in1=sgs[1].t[:], op=ALU.mult), R=[ps2, sgs[1]], W=[m2])
                    op(POOL, lambda: nc.gpsimd.tensor_tensor(out=mgb.t[:, oc, :], in0=m1.t[:], in1=m2.t[:], op=ALU.add), R=[m1, m2], W=[mgb])
                for oc in range(8):
                    ps = pr.next()
                    mm_group(ps, [(ps.t[:], Wo.t[:, k, oc * 128:(oc + 1) * 128], mgb.t[:, k, :]) for k in range(8)], R=[Wo, mgb])
                    h1 = h1r.next()
                    op(DVE, lambda: nc.vector.tensor_tensor(out=h1.t[:], in0=ps.t[:], in1=xt.t[:, oc, :], op=ALU.add), R=[ps, xt], W=[h1])
                    dma(h1.slot, h1T_d[oc * 128:(oc + 1) * 128, q0:q0 + N], h1.t[:], R=[h1])
            barrier()

        with ExitStack() as ph:
            slot_ptr[0] = base_slot_ptr
            W1_t = ph.enter_context(_sb("W1", [128, 8, DFF], BF16))
            W2_t = ph.enter_context(_sb("W2", [128, 32, D], BF16))
            W1b_, W2b_ = Buf(W1_t, new_slot()), Buf(W2_t, new_slot())
            dma(W1b_.slot, W1_t[:], kview(w1b), W=[W1b_])
            dma(W2b_.slot, W2_t[:], kview(w2b), W=[W2b_])
            N = 256
            hr = Ring([Buf(ph.enter_context(_sb(f"h{i}", [128, 8, N], F32)), new_slot()) for i in range(2)])
            sqb_ = Buf(ph.enter_context(_sb("sq4", [128, 8, N], BF16)))
            hbb = Buf(ph.enter_context(_sb("hb4", [128, 8, N], BF16)))
            Ab = Buf(ph.enter_context(_sb("A4", [128, 32, N], BF16)))
            h2b = Buf(ph.enter_context(_sb("h2", [128, 8, N], F32)))
            osr = Ring([Buf(ph.enter_context(_sb(f"os{i}", [128, 8, N], F32)), new_slot()) for i in range(1)])
            f32r = Ring([Buf(ph.enter_context(_sb(f"f4r{i}", [128, N], F32))) for i in range(6)])
            rstdb = Buf(ph.enter_context(_sb("rstd4", [128, N], F32)))
            rstd5 = Buf(ph.enter_context(_sb("rstd5", [128, N], F32)))
            pr = Ring(banks)
            h1v = kview(h1T_d)
            outv = kview(outT)
            for (q0, _) in colblocks(S, N):
                hb = hr.next()
                dma(hb.slot, hb.t[:], h1v[:, :, q0:q0 + N], W=[hb])
                op(ACT, lambda: nc.scalar.activation(out=sqb_.t[:], in_=hb.t[:], func=AF.Square), R=[hb], W=[sqb_])
                op(DVE, lambda: nc.vector.tensor_copy(out=hbb.t[:], in_=hb.t[:]), R=[hb], W=[hbb])
                ssb = pr.next()
                mm_group(ssb, [(ssb.t[:, :N], ones_b, sqb_.t[:, k, :]) for k in range(8)], R=[sqb_, cmb])
                rt = f32r.next()
                op(ACT, lambda: nc.scalar.activation(out=rt.t[:], in_=ssb.t[:, :N], func=AF.Sqrt, scale=1.0 / D, bias=eps_c), R=[ssb], W=[rt])
                op(DVE, lambda: nc.vector.reciprocal(out=rstdb.t[:], in_=rt.t[:]), R=[rt], W=[rstdb])
                for f in range(32):
                    ps = pr.next()
                    mm_group(ps, [(ps.t[:, :N], W1_t[:, k, f * 128:(f + 1) * 128], hbb.t[:, k, :]) for k in range(8)], R=[W1b_, hbb])
                    tr_ = f32r.next()
                    op(DVE, lambda: nc.vector.scalar_tensor_tensor(out=tr_.t[:], in0=ps.t[:, :N], scalar=0.0, in1=rstdb.t[:], op0=ALU.max, op1=ALU.mult), R=[ps, rstdb], W=[tr_])
                    op(ACT, lambda: nc.scalar.activation(out=Ab.t[:, f, :], in_=tr_.t[:], func=AF.Square), R=[tr_], W=[Ab])
                for oc in range(8):
                    ps = pr.next()
                    mm_group(ps, [(ps.t[:, :N], W2_t[:, f, oc * 128:(oc + 1) * 128], Ab.t[:, f, :]) for f in range(32)], R=[W2b_, Ab])
                    op(DVE, lambda: nc.vector.tensor_tensor(out=h2b.t[:, oc, :], in0=ps.t[:, :N], in1=hb.t[:, oc, :], op=ALU.add), R=[ps, hb], W=[h2b])
                op(ACT, lambda: nc.scalar.activation(out=sqb_.t[:], in_=h2b.t[:], func=AF.Square), R=[h2b], W=[sqb_])
                ssb = pr.next()
                mm_group(ssb, [(ssb.t[:, :N], ones_b, sqb_.t[:, k, :]) for k in range(8)], R=[sqb_, cmb])
                rt = f32r.next()
                op(ACT, lambda: nc.scalar.activation(out=rt.t[:], in_=ssb.t[:, :N], func=AF.Sqrt, scale=1.0 / D, bias=eps_c), R=[ssb], W=[rt])
                op(DVE, lambda: nc.vector.reciprocal(out=rstd5.t[:], in_=rt.t[:]), R=[rt], W=[rstd5])
                os_ = osr.next()
                for oc in range(8):
                    op(DVE, lambda: nc.vector.scalar_tensor_tensor(out=os_.t[:, oc, :], in0=h2b.t[:, oc, :], scalar=gcol(GV_FIN + oc), in1=rstd5.t[:], op0=ALU.mult, op1=ALU.mult), R=[h2b, rstd5], W=[os_])
                dma(os_.slot, outv[:, :, q0:q0 + N], os_.t[:], R=[os_])
            barrier()
    return nc


def _const_tables(S):
    LP = S + 128
    cm = np.zeros((128, 512), np.float32)
    cm[:, 0:128] = 1.0
    cm[0:64, 128:192] = 1.0
    cm[64:128, 192:256] = 1.0
    for i in range(64):
        cm[2 * i + 1, 256 + 2 * i] = -1.0
        cm[2 * i, 256 + 2 * i + 1] = 1.0
    cm[:, 384:512] = np.eye(128, dtype=np.float32)
    rows = S // 64
    row_id = np.repeat(np.arange(rows, dtype=np.float32), 64)
    col_id = np.tile(np.arange(64, dtype=np.float32), rows)
    inv_freq = (np.float32(10000.0) ** (-np.arange(16, dtype=np.float32) / np.float32(16))).astype(np.float32)
    ang = np.concatenate([row_id[:, None] * inv_freq, col_id[:, None] * inv_freq], axis=-1).astype(np.float32)
    ang = np.concatenate([np.zeros((128, 32), np.float32), ang], axis=0)
    cos = np.cos(ang).astype(np.float32)
    sin = np.sin(ang).astype(np.float32)
    pair = (np.arange(128) % 64) // 2
    C = np.ascontiguousarray(cos[:, pair].T)
    Sn = np.ascontiguousarray(sin[:, pair].T)
    return cm, C, Sn


def _prep_shared(inp, S):
    f = lambda a: np.ascontiguousarray(np.asarray(a, dtype=np.float32))
    cm, C, Sn = _const_tables(S)
    gv = np.zeros((128, NG), np.float32)
    gv[:, GV_MIX:GV_MIX + 8] = f(inp["norm_mix_g"])[0].reshape(8, 128).T
    gv[:, GV_MLP:GV_MLP + 8] = f(inp["norm_mlp_g"])[0].reshape(8, 128).T
    gv[:, GV_FIN:GV_FIN + 8] = f(inp["norm_final_g"]).reshape(8, 128).T
    gv[:, GV_QG] = np.tile(f(inp["q_norm_g"])[0], 2)
    gv[:, GV_KG] = np.tile(f(inp["k_norm_g"])[0], 2)
    gv[:, GV_BGLU:GV_BGLU + 4] = f(inp["b_glu"])[0].reshape(4, 128).T
    gv[:, GV_SSMD:GV_SSMD + 4] = f(inp["ssm_d"])[0].reshape(4, 128).T
    gv[0:112, GV_MASK] = -30000.0
    a_re, a_im, ldt = f(inp["ssm_a_re"])[0], f(inp["ssm_a_im"])[0], f(inp["ssm_log_dt"])[0]
    b_re, b_im = f(inp["ssm_b_re"])[0], f(inp["ssm_b_im"])[0]
    c_re, c_im = f(inp["ssm_c_re"])[0], f(inp["ssm_c_im"])[0]
    ssmA = np.zeros((128, 96), np.float32)
    bz = np.zeros((32, 128, 4, 128), np.float32)
    for d in range(2):
        for gp in range(16):
            ci = d * 16 + gp
            for g2 in range(2):
                g = 2 * gp + g2
                gl = g % 8
                ps = slice(g2 * 64, g2 * 64 + 64)
                ssmA[ps, ci] = a_re[d, g]
                ssmA[ps, 32 + ci] = a_im[d, g]
                ssmA[ps, 64 + ci] = ldt[d, g]
                bz[ci, ps, 0, gl * 16:(gl + 1) * 16] = b_re[d, g]
                bz[ci, ps, 1, gl * 16:(gl + 1) * 16] = b_im[d, g]
                bz[ci, ps, 2, gl * 16:(gl + 1) * 16] = c_re[d, g].T
                bz[ci, ps, 3, gl * 16:(gl + 1) * 16] = c_im[d, g].T
    shared = {
        "w_in": f(inp["w_in"])[0], "w_glu": f(inp["w_glu"])[0], "w_sp": f(inp["w_ssm_proj"])[0],
        "w_ap": f(inp["w_attn_proj"])[0], "w_out": f(inp["w_out"])[0], "w1": f(inp["w_mlp_in"])[0],
        "w2": f(inp["w_mlp_out"])[0], "gv": gv, "cmat": cm, "ropeC": C, "ropeS": Sn, "ssmA": ssmA,
        "bz": bz.reshape(32, 128, 512),
    }
    return shared


def _make_xT(xb, meta):
    S = xb.shape[0]
    full = np.concatenate([np.zeros((112, D), np.float32), np.asarray(meta, np.float32), np.asarray(xb, np.float32)], axis=0)
    return np.ascontiguousarray(full.T)


_NC_CACHE = {}


def kernel(**inputs):
    x = np.asarray(inputs["x"], dtype=np.float32)
    B, S, _ = x.shape
    shared = _prep_shared(inputs, S)
    if S not in _NC_CACHE:
        _NC_CACHE[S] = build(S)
    nc = _NC_CACHE[S]
    in_maps = []
    for b in range(B):
        m = dict(shared)
        m["xT"] = _make_xT(x[b], inputs["meta_tokens"])
        in_maps.append(m)
    res = run_bass_kernel_spmd(nc, in_maps, core_ids=list(range(B)))
    out = np.stack([np.ascontiguousarray(r["outT"].T) for r in res.results], axis=0)
    return out.astype(np.float32)
```

```python
import numpy as np
from contextlib import ExitStack
import concourse.bass as bass
import concourse.mybir as mybir
from concourse.bass_utils import run_bass_kernel_spmd

F32 = mybir.dt.float32
BF16 = mybir.dt.bfloat16
AF = mybir.ActivationFunctionType
ALU = mybir.AluOpType

D = 1024
DFF = 4096
EPS = 1e-6
GV_MIX, GV_MLP, GV_FIN, GV_QG, GV_KG, GV_BGLU, GV_SSMD, GV_MASK, NG = 0, 8, 16, 24, 25, 26, 30, 34, 35


class Ev:
    __slots__ = ("sem", "val", "key")

    def __init__(s, sem, val, key):
        s.sem, s.val, s.key = sem, val, key


class Eng:
    def __init__(s, name, h, sem):
        s.name, s.h, s.sem = name, h, sem
        s.cnt = 0
        s.seen = {}
        s.last = None
        s.selfsync = True

    def wait(s, evs):
        if evs is None:
            return
        if isinstance(evs, Ev):
            evs = [evs]
        for ev in evs:
            if ev is None:
                continue
            if isinstance(ev, (list, tuple)):
                s.wait(ev)
                continue
            if ev.key == s.name and (s.name == "pe" or not s.selfsync):
                continue
            if s.seen.get(ev.key, 0) >= ev.val:
                continue
            s.h.wait_ge(ev.sem, ev.val)
            s.seen[ev.key] = ev.val

    def mark(s, inst):
        s.cnt += 1
        inst.then_inc(s.sem, 1)
        s.last = Ev(s.sem, s.cnt, s.name)
        return s.last


class Slot:
    def __init__(s, sem, key):
        s.sem, s.key = sem, key
        s.cnt = 0
        s.last = None


class Buf:
    def __init__(s, t, slot=None):
        s.t = t
        s.slot = slot
        s.wr = None
        s.rd = {}


class Ring:
    def __init__(s, bufs):
        s.bufs = bufs
        s.i = 0

    def next(s):
        b = s.bufs[s.i % len(s.bufs)]
        s.i += 1
        return b


def build(S, dbg=False):
    LP = S + 128
    NT = LP // 128
    NCH = LP // 8
    nc = bass.Bass("TRN2", target_bir_lowering=False)

    def din(name, shape, dt=F32):
        return nc.dram_tensor(name, shape, dt, kind="ExternalInput").ap()

    def dscr(name, shape, dt):
        return nc.dram_tensor(name, shape, dt, kind=("ExternalOutput" if dbg else "Internal")).ap()

    xT = din("xT", [D, LP])
    w_in = din("w_in", [D, 4096])
    w_glu = din("w_glu", [512, 512])
    w_sp = din("w_sp", [512, D])
    w_ap = din("w_ap", [D, D])
    w_out = din("w_out", [D, D])
    w1 = din("w1", [D, DFF])
    w2 = din("w2", [DFF, D])
    gv_d = din("gv", [128, NG])
    cmat_d = din("cmat", [128, 4 * 128])
    ropeC = din("ropeC", [128, LP])
    ropeS = din("ropeS", [128, LP])
    ssmA = din("ssmA", [128, 96])
    bz = din("bz", [32, 128, 512])
    outT = nc.dram_tensor("outT", [D, S], F32, kind="ExternalOutput").ap()

    winb = dscr("winb", [D, 4096], BF16)
    wglub = dscr("wglub", [512, 512], BF16)
    wspb = dscr("wspb", [512, D], BF16)
    wapb = dscr("wapb", [D, D], BF16)
    woutb = dscr("woutb", [D, D], BF16)
    w1b = dscr("w1b", [D, DFF], BF16)
    w2b = dscr("w2b", [DFF, D], BF16)
    qT_d = dscr("qT_d", [D, LP], BF16)
    zT_d = dscr("zT_d", [512, LP], BF16)
    yaT_d = dscr("yaT_d", [D, S], BF16)
    h1T_d = dscr("h1T_d", [D, S], F32)

    _uid = [0]

    def _sb(name, shape, dt):
        _uid[0] += 1
        return nc.sbuf_tensor(f"{name}_{_uid[0]}", shape, dt)

    def kview(ap):
        return ap.rearrange("(k p) l -> p k l", p=128)

    top = ExitStack()
    with top:
        sems = [top.enter_context(nc.semaphore(f"sem{i}")) for i in range(64)]
        PE = Eng("pe", nc.tensor, sems[0])
        ACT = Eng("act", nc.scalar, sems[1])
        DVE = Eng("dve", nc.vector, sems[2])
        POOL = Eng("pool", nc.gpsimd, sems[3])
        SP = Eng("sp", nc.sync, None)
        engines = [PE, ACT, DVE, POOL]
        slots = [Slot(sems[4 + i], f"dma{i}") for i in range(60)]
        slot_ptr = [0]

        def new_slot():
            s = slots[slot_ptr[0]]
            slot_ptr[0] += 1
            return s

        def op(E, fn, R=(), W=(), mark=True):
            for b in R:
                E.wait(b.wr)
            for b in W:
                E.wait(b.wr)
                E.wait([e for k_, e in b.rd.items() if k_ != E.name])
            inst = fn()
            if mark:
                ev = E.mark(inst)
                for b in R:
                    b.rd[E.name] = ev
                for b in W:
                    b.wr = ev
                    b.rd = {}
                return ev
            return None

        def dma(slot, out, in_, R=(), W=(), upd=()):
            for b in R:
                SP.wait(b.wr)
            for b in W:
                SP.wait(b.wr)
                SP.wait(list(b.rd.values()))
            inst = nc.sync.dma_start(out=out, in_=in_)
            slot.cnt += 16
            inst.then_inc(slot.sem, 16)
            ev = Ev(slot.sem, slot.cnt, slot.key)
            slot.last = ev
            for b in R:
                b.rd[slot.key] = ev
            for b in list(W) + list(upd):
                b.wr = ev
                b.rd = {}
            return ev

        def mm_group(bank, mms, R=()):
            for b in R:
                PE.wait(b.wr)
            PE.wait(bank.wr)
            PE.wait(list(bank.rd.values()))
            n = len(mms)
            inst = None
            for i, m in enumerate(mms):
                if len(m) == 3:
                    st, sp_ = (i == 0), (i == n - 1)
                else:
                    st, sp_ = m[3], m[4]
                inst = nc.tensor.matmul(m[0], lhsT=m[1], rhs=m[2], start=st, stop=sp_)
            ev = PE.mark(inst)
            for b in R:
                b.rd[PE.name] = ev
            bank.wr = ev
            bank.rd = {}
            return ev

        def barrier():
            evs = [E.last for E in engines if E.last is not None]
            evs += [s.last for s in slots if s.last is not None]
            for E in engines + [SP]:
                E.wait(evs)

        def colblocks(n, w=512):
            return [(c, min(w, n - c)) for c in range(0, n, w)]

        gv_t = top.enter_context(_sb("gv", [128, NG], F32))
        cm_f = top.enter_context(_sb("cm_f", [128, 512], F32))
        cm_b = top.enter_context(_sb("cm_b", [128, 512], BF16))
        PSA = top.enter_context(nc.psum_tensor("psa", [128, 7, 512], F32))
        PSB = top.enter_context(nc.psum_tensor("psb", [128, 1024], BF16))
        gv = Buf(gv_t, new_slot())
        cmf = Buf(cm_f, new_slot())
        cmb = Buf(cm_b)
        dma(gv.slot, gv_t[:], gv_d, W=[gv])
        dma(cmf.slot, cm_f[:], cmat_d, W=[cmf])
        op(DVE, lambda: nc.vector.tensor_copy(out=cm_b[:], in_=cm_f[:]), R=[cmf], W=[cmb])
        ones_b = cm_b[:, 0:128]
        blk_b = cm_b[:, 128:256]
        rot_b = cm_b[:, 256:384]
        id_b = cm_b[:, 384:512]
        id_f = cm_f[:, 384:512]
        ones_f = cm_f[:, 0:128]
        DVE.wait(gv.wr)
        ACT.wait(gv.wr)
        POOL.wait(gv.wr)
        banks = [Buf(PSA[:, i, :]) for i in range(7)]
        bankB = Buf(PSB)
        base_slot_ptr = slot_ptr[0]

        def gcol(c):
            return gv_t[:, c:c + 1]

        with ExitStack() as ph:
            slot_ptr[0] = base_slot_ptr
            stg = Ring([Buf(ph.enter_context(_sb(f"wst{i}", [128, 2048], F32)), new_slot()) for i in range(3)])
            ost = Ring([Buf(ph.enter_context(_sb(f"wso{i}", [128, 2048], BF16)), new_slot()) for i in range(3)])
            rr = [0]

            def convert(src, dst, rows, cols, gain0=None):
                for k in range(rows // 128):
                    for c0 in range(0, cols, 2048):
                        n = min(2048, cols - c0)
                        st = stg.next()
                        dma(st.slot, st.t[:, :n], src[k * 128:(k + 1) * 128, c0:c0 + n], W=[st])
                        ob = ost.next()
                        which = rr[0] % 3
                        rr[0] += 1
                        if gain0 is None:
                            if which == 0:
                                op(DVE, lambda: nc.vector.tensor_copy(out=ob.t[:, :n], in_=st.t[:, :n]), R=[st], W=[ob])
                            elif which == 1:
                                op(POOL, lambda: nc.gpsimd.tensor_copy(out=ob.t[:, :n], in_=st.t[:, :n]), R=[st], W=[ob])
                            else:
                                op(ACT, lambda: nc.scalar.copy(out=ob.t[:, :n], in_=st.t[:, :n]), R=[st], W=[ob])
                        else:
                            g = gcol(gain0 + k)
                            if which == 0:
                                op(DVE, lambda: nc.vector.tensor_scalar(out=ob.t[:, :n], in0=st.t[:, :n], scalar1=g, scalar2=None, op0=ALU.mult), R=[st], W=[ob])
                            elif which == 1:
                                op(POOL, lambda: nc.gpsimd.tensor_scalar(out=ob.t[:, :n], in0=st.t[:, :n], scalar1=g, scalar2=None, op0=ALU.mult), R=[st], W=[ob])
                            else:
                                op(ACT, lambda: nc.scalar.activation(out=ob.t[:, :n], in_=st.t[:, :n], func=AF.Copy, scale=g), R=[st], W=[ob])
                        dma(ob.slot, dst[k * 128:(k + 1) * 128, c0:c0 + n], ob.t[:, :n], R=[ob])

            convert(w_in, winb, D, 4096, GV_MIX)
            convert(w_glu, wglub, 512, 512)
            convert(w_sp, wspb, 512, D)
            convert(w_ap, wapb, D, D)
            convert(w_out, woutb, D, D)
            convert(w1, w1b, D, DFF, GV_MLP)
            convert(w2, w2b, DFF, D)
            barrier()

        def in_proj(ph, chunks, wcol0, wcols, uTb=None, kTb=None, Vxb=None):
            W_t = ph.enter_context(_sb("Wp", [128, 8, wcols], BF16))
            Wb = Buf(W_t, new_slot())
            dma(Wb.slot, W_t[:], kview(winb)[:, :, wcol0:wcol0 + wcols], W=[Wb])
            xring = Ring([Buf(ph.enter_context(_sb(f"xt{i}", [128, 8, 512], F32)), new_slot()) for i in range(2)])
            sqr = Ring([Buf(ph.enter_context(_sb(f"sq{i}", [128, 8, 512], BF16))) for i in range(1)])
            xbr = Ring([Buf(ph.enter_context(_sb(f"xb{i}", [128, 8, 512], BF16))) for i in range(2)])
            need_rope = any(c[0] in "qk" for c in chunks)
            if need_rope:
                csr = Ring([Buf(ph.enter_context(_sb(f"cs{i}", [128, 2, 512], F32)), new_slot()) for i in range(2)])
                qor = Ring([Buf(ph.enter_context(_sb(f"qo{i}", [128, 512], BF16)), new_slot()) for i in range(3)])
            f32r = Ring([Buf(ph.enter_context(_sb(f"f32r{i}", [128, 512], F32))) for i in range(8)])
            bfr = Ring([Buf(ph.enter_context(_sb(f"bfr{i}", [128, 512], BF16))) for i in range(4)])
            rstdr = Ring([Buf(ph.enter_context(_sb(f"rstd{i}", [128, 512], F32))) for i in range(2)])
            mainr = Ring(banks[0:3])
            ssb = banks[3]
            hsr = Ring(banks[4:5])
            rotr = Ring(banks[5:7])
            trb = bankB
            xTv = kview(xT)
            for (c0, N) in colblocks(LP):
                xt = xring.next()
                dma(xt.slot, xt.t[:, :, :N], xTv[:, :, c0:c0 + N], W=[xt])
                if need_rope:
                    cs = csr.next()
                    dma(cs.slot, cs.t[:, 0, :N], ropeC[:, c0:c0 + N], W=[cs])
                    dma(cs.slot, cs.t[:, 1, :N], ropeS[:, c0:c0 + N], upd=[cs])
                sq = sqr.next()
                op(ACT, lambda: nc.scalar.activation(out=sq.t[:, :, :N], in_=xt.t[:, :, :N], func=AF.Square), R=[xt], W=[sq])
                xb = xbr.next()
                op(DVE, lambda: nc.vector.tensor_copy(out=xb.t[:, :, :N], in_=xt.t[:, :, :N]), R=[xt], W=[xb])
                mm_group(ssb, [(ssb.t[:, :N], ones_b, sq.t[:, k, :N]) for k in range(8)], R=[sq, cmb])
                rt = f32r.next()
                op(ACT, lambda: nc.scalar.activation(out=rt.t[:, :N], in_=ssb.t[:, :N], func=AF.Ln, scale=1.0 / D, bias=eps_c), R=[ssb], W=[rt])
                rstd = rstdr.next()
                op(ACT, lambda: nc.scalar.activation(out=rstd.t[:, :N], in_=rt.t[:, :N], func=AF.Exp, scale=-0.5), R=[rt], W=[rstd])
                for (kind, idx, wc) in chunks:
                    ps = mainr.next()
                    mm_group(ps, [(ps.t[:, :N], W_t[:, k, wc:wc + 128], xb.t[:, k, :N]) for k in range(8)], R=[Wb, xb])
                    if kind == "u":
                        op(DVE, lambda: nc.vector.tensor_tensor(out=uTb.t[:, idx, :, c0 // 8:(c0 + N) // 8].rearrange("p i c -> p c i"), in0=ps.t[:, :N].rearrange("p (c i) -> p c i", i=8), in1=rstd.t[:, :N].rearrange("p (c i) -> p c i", i=8), op=ALU.mult), R=[ps, rstd], W=[uTb])
                    elif kind == "v":
                        vt = bfr.next()
                        op(DVE, lambda: nc.vector.tensor_tensor(out=vt.t[:, :N], in0=ps.t[:, :N], in1=rstd.t[:, :N], op=ALU.mult), R=[ps, rstd], W=[vt])
                        for s_ in range(N // 128):
                            tt = c0 // 128 + s_
                            op(PE, lambda: nc.tensor.transpose(out=trb.t[:, s_ * 128:(s_ + 1) * 128], in_=vt.t[:, s_ * 128:(s_ + 1) * 128], identity=id_b), R=[vt, cmb], W=[trb])
                            op(ACT, lambda: nc.scalar.copy(out=Vxb.t[:, tt, 2 * idx:2 * idx + 2, 0:64],
                                                           in_=trb.t[:, s_ * 128:(s_ + 1) * 128].rearrange("p (g d) -> p g d", g=2)), R=[trb], W=[Vxb])
                    else:
                        tq = f32r.next()
                        op(DVE, lambda: nc.vector.tensor_tensor(out=tq.t[:, :N], in0=ps.t[:, :N], in1=rstd.t[:, :N], op=ALU.mult), R=[ps, rstd], W=[tq])
                        sq2 = bfr.next()
                        op(ACT, lambda: nc.scalar.activation(out=sq2.t[:, :N], in_=tq.t[:, :N], func=AF.Square), R=[tq], W=[sq2])
                        hs = hsr.next()
                        mm_group(hs, [(hs.t[:, :N], blk_b, sq2.t[:, :N])], R=[sq2, cmb])
                        rt2 = f32r.next()
                        op(ACT, lambda: nc.scalar.activation(out=rt2.t[:, :N], in_=hs.t[:, :N], func=AF.Ln, scale=1.0 / 64, bias=eps_c), R=[hs], W=[rt2])
                        rq = f32r.next()
                        op(ACT, lambda: nc.scalar.activation(out=rq.t[:, :N], in_=rt2.t[:, :N], func=AF.Exp, scale=-0.5), R=[rt2], W=[rq])
                        tq2 = bfr.next()
                        gc_ = gcol(GV_QG if kind == "q" else GV_KG)
                        op(DVE, lambda: nc.vector.scalar_tensor_tensor(out=tq2.t[:, :N], in0=tq.t[:, :N], scalar=gc_, in1=rq.t[:, :N], op0=ALU.mult, op1=ALU.mult), R=[tq, rq], W=[tq2])
                        rot = rotr.next()
                        mm_group(rot, [(rot.t[:, :N], rot_b, tq2.t[:, :N])], R=[tq2, cmb])
                        t1 = f32r.next()
                        op(DVE, lambda: nc.vector.tensor_tensor(out=t1.t[:, :N], in0=tq2.t[:, :N], in1=cs.t[:, 0, :N], op=ALU.mult), R=[tq2, cs], W=[t1])
                        t2 = f32r.next()
                        op(DVE, lambda: nc.vector.tensor_tensor(out=t2.t[:, :N], in0=rot.t[:, :N], in1=cs.t[:, 1, :N], op=ALU.mult), R=[rot, cs], W=[t2])
                        if kind == "k":
                            op(DVE, lambda: nc.vector.tensor_tensor(out=kTb.t[:, idx, c0:c0 + N], in0=t1.t[:, :N], in1=t2.t[:, :N], op=ALU.add), R=[t1, t2], W=[kTb])
                        else:
                            qo = qor.next()
                            op(DVE, lambda: nc.vector.tensor_tensor(out=qo.t[:, :N], in0=t1.t[:, :N], in1=t2.t[:, :N], op=ALU.add), R=[t1, t2], W=[qo])
                            dma(qo.slot, qT_d[idx * 128:(idx + 1) * 128, c0:c0 + N], qo.t[:, :N], R=[qo])

        eps_t = top.enter_context(_sb("eps", [128, 1], F32))
        epsb = Buf(eps_t)
        op(DVE, lambda: nc.vector.memset(eps_t[:], EPS), W=[epsb])
        ACT.wait(epsb.wr)
        eps_c = eps_t[:, 0:1]

        with ExitStack() as ph_prm:
            ph = ph_prm
            slot_ptr[0] = base_slot_ptr
            NPR = 120
            PR_t = ph.enter_context(_sb("PR", [128, NPR * 32], F32))
            PRb = Buf(PR_t, new_slot())
            pr_i = [0]

            def col():
                i = pr_i[0]
                pr_i[0] += 1
                assert i < NPR
                return PR_t[:, i * 32:(i + 1) * 32]

            A_re, A_im, LDT = col(), col(), col()
            dma(PRb.slot, PR_t[:, 0:96], ssmA, W=[PRb])

            def V(fn):
                return op(DVE, fn, R=[PRb], W=[PRb])

            def A(fn):
                return op(ACT, fn, R=[PRb], W=[PRb])

            def TT(o, a, b, o_):
                V(lambda: nc.vector.tensor_tensor(out=o, in0=a, in1=b, op=o_))

            def TS(o, a, s1, o1, s2=None, o2=None):
                if o2 is None:
                    V(lambda: nc.vector.tensor_scalar(out=o, in0=a, scalar1=s1, scalar2=None, op0=o1))
                else:
                    V(lambda: nc.vector.tensor_scalar(out=o, in0=a, scalar1=s1, scalar2=s2, op0=o1, op1=o2))

            t1, t2 = col(), col()

            def cmul(orr, oi, ar, ai, br, bi):
                TT(t1, ar, br, ALU.mult)
                TT(t2, ai, bi, ALU.mult)
                TT(orr, t1, t2, ALU.subtract)
                TT(t1, ar, bi, ALU.mult)
                TT(t2, ai, br, ALU.mult)
                TT(oi, t1, t2, ALU.add)

            dt_, lre, mag, ang, cs_, sn_, c2, s2 = col(), col(), col(), col(), col(), col(), col(), col()
            pio2 = col()
            V(lambda: nc.vector.memset(pio2, float(np.pi / 2)))
            A(lambda: nc.scalar.activation(out=dt_, in_=LDT, func=AF.Exp))
            TS(lre, A_re, -1e-4, ALU.min)
            TT(t1, lre, dt_, ALU.mult)
            A(lambda: nc.scalar.activation(out=mag, in_=t1, func=AF.Exp))
            TT(ang, A_im, dt_, ALU.mult)
            ki_t = ph.enter_context(_sb("ki", [128, 32], mybir.dt.int32))
            kf, rr_, mm_ = col(), col(), col()
            TS(kf, ang, float(1.0 / (2 * np.pi)), ALU.mult)
            V(lambda: nc.vector.tensor_copy(out=ki_t[:], in_=kf))
            V(lambda: nc.vector.tensor_copy(out=kf, in_=ki_t[:]))
            V(lambda: nc.vector.scalar_tensor_tensor(out=rr_, in0=kf, scalar=float(-2 * np.pi), in1=ang, op0=ALU.mult, op1=ALU.add))
            pi_c, npi_c = col(), col()
            V(lambda: nc.vector.memset(pi_c, float(np.pi)))
            V(lambda: nc.vector.memset(npi_c, float(-np.pi)))
            TT(mm_, rr_, pi_c, ALU.is_gt)
            V(lambda: nc.vector.scalar_tensor_tensor(out=rr_, in0=mm_, scalar=float(-2 * np.pi), in1=rr_, op0=ALU.mult, op1=ALU.add))
            TT(mm_, rr_, npi_c, ALU.is_lt)
            V(lambda: nc.vector.scalar_tensor_tensor(out=rr_, in0=mm_, scalar=float(2 * np.pi), in1=rr_, op0=ALU.mult, op1=ALU.add))
            TS(s2, rr_, -1.0, ALU.mult)
            TT(c2, rr_, s2, ALU.max)
            A(lambda: nc.scalar.activation(out=sn_, in_=rr_, func=AF.Sin))
            A(lambda: nc.scalar.activation(out=cs_, in_=c2, func=AF.Sin, scale=-1.0, bias=pio2[:, 0:1]))
            lbr, lbi = col(), col()
            TT(lbr, mag, cs_, ALU.mult)
            TT(lbi, mag, sn_, ALU.mult)
            numr, den, rden, fre, fim = col(), col(), col(), col(), col()
            TS(numr, lbr, -1.0, ALU.add)
            TT(t1, lre, lre, ALU.mult)
            TT(t2, A_im, A_im, ALU.mult)
            TT(den, t1, t2, ALU.add)
            V(lambda: nc.vector.reciprocal(out=rden, in_=den))
            TT(t1, numr, lre, ALU.mult)
            TT(t2, lbi, A_im, ALU.mult)
            TT(fre, t1, t2, ALU.add)
            TT(fre, fre, rden, ALU.mult)
            TT(t1, lbi, lre, ALU.mult)
            TT(t2, numr, A_im, ALU.mult)
            TT(fim, t1, t2, ALU.subtract)
            TT(fim, fim, rden, ALU.mult)
            Pre = [col() for _ in range(9)]
            Pim = [col() for _ in range(9)]
            NPim = [col() for _ in range(9)]
            V(lambda: nc.vector.memset(Pre[0], 1.0))
            V(lambda: nc.vector.memset(Pim[0], 0.0))
            for tau in range(8):
                cmul(Pre[tau + 1], Pim[tau + 1], Pre[tau], Pim[tau], lbr, lbi)
            for tau in range(9):
                TS(NPim[tau], Pim[tau], -1.0, ALU.mult)
            Gre = [col() for _ in range(8)]
            Gim = [col() for _ in range(8)]
            for tau in range(8):
                cmul(Gre[tau], Gim[tau], Pre[tau], Pim[tau], fre, fim)
            nlev = 0
            while (1 << nlev) < NCH:
                nlev += 1
            Hre = [col() for _ in range(nlev)]
            Him = [col() for _ in range(nlev)]
            NHim = [col() for _ in range(nlev)]
            V(lambda: nc.vector.tensor_copy(out=Hre[0], in_=Pre[8]))
            V(lambda: nc.vector.tensor_copy(out=Him[0], in_=Pim[8]))
            for k in range(1, nlev):
                cmul(Hre[k], Him[k], Hre[k - 1], Him[k - 1], Hre[k - 1], Him[k - 1])
            for k in range(nlev):
                TS(NHim[k], Him[k], -1.0, ALU.mult)
            ev_pr = V(lambda: nc.vector.tensor_copy(out=t1, in_=t2))
            for E in (ACT, POOL, PE):
                E.wait(ev_pr)
            if dbg:
                prdbg = nc.dram_tensor("prdbg", [128, NPR * 32], F32, kind="ExternalOutput").ap()
                dma(PRb.slot, prdbg, PR_t[:], R=[PRb])

            base_slot_ptr = slot_ptr[0]
            for hf in range(2):
              with ExitStack() as ph_ssm:
                uT_t = ph_ssm.enter_context(_sb(f"uT{hf}", [128, 2, 8, NCH], BF16))
                uTb = Buf(uT_t)
                with ExitStack() as ph:
                    slot_ptr[0] = base_slot_ptr
                    in_proj(ph, [("u", i, i * 128) for i in range(2)], hf * 256, 256, uTb=uTb)
                    barrier()
                slot_ptr[0] = base_slot_ptr
                ph = ph_ssm
                bzr = Ring([Buf(ph.enter_context(_sb(f"bz{i}", [128, 4, 128], F32)), new_slot()) for i in range(2)])
                Wst_r = Ring([Buf(ph.enter_context(_sb(f"Wst{i}", [128, 16, 128], BF16))) for i in range(2)])
                wtr = Ring([Buf(ph.enter_context(_sb(f"wt{i}", [128, 128], F32))) for i in range(4)])
                tmr = Ring([Buf(ph.enter_context(_sb(f"tm{i}", [128, 128], F32))) for i in range(4)])
                BbZ_t = ph.enter_context(_sb("BbZ", [128, 16, 128], BF16))
                BbZb = Buf(BbZ_t)
                CL_t = ph.enter_context(_sb("CL", [128, 8 * 18, 128], BF16))
                CLb = Buf(CL_t)
                BD_t = ph.enter_context(_sb("BD", [128, 16, 128], BF16))
                BDb = Buf(BD_t)
                XA_t = ph.enter_context(_sb("XA", [128, 2, NCH], F32))
                XB_t = ph.enter_context(_sb("XB", [128, 2, NCH], F32))
                XAb, XBb = Buf(XA_t), Buf(XB_t)
                Xb_t = ph.enter_context(_sb("Xb", [128, 8, 2, NCH], BF16))
                Xbb = Buf(Xb_t)
                zbuf_t = ph.enter_context(_sb("zbuf", [128, LP], BF16))
                zb = Buf(zbuf_t, new_slot())
                ytr = Ring([Buf(ph.enter_context(_sb(f"yt{i}", [128, 512], F32))) for i in range(3)])
                op(POOL, lambda: nc.gpsimd.memset(Xb_t[:].rearrange("p a b c -> p (a b c)"), 0.0), W=[Xbb])
                trf = banks[0]
                sring = Ring(banks[1:3])
                kring = Ring(banks[3:4])
                accr = Ring(banks[4:7])
                cbs = colblocks(NCH)

                for gcl in range(2):
                    gc = hf * 2 + gcl
                    uv = uT_t[:, gcl, :, :].rearrange("p i c -> p c i")
                    for gpl in range(4):
                        gp = gc * 4 + gpl
                        for d in range(2):
                            cidx = d * 16 + gp
                            slot8 = gpl * 2 + d
                            bzb = bzr.next()
                            dma(bzb.slot, bzb.t[:], bz[cidx].rearrange("p (a c) -> p a c", a=4), W=[bzb])
                            Bre, Bim, Cre, Cim = (bzb.t[:, a, :] for a in range(4))
                            Wst = Wst_r.next()

                            def sc(arr):
                                return arr[:, cidx:cidx + 1]

                            for i in range(8):
                                tau = (7 - i) if d == 0 else i
                                kr, ki = sc(Gre[tau]), sc(Gim[tau])
                                for part in range(2):
                                    tm = tmr.next()
                                    wt = wtr.next()
                                    if part == 0:
                                        op(ACT, lambda: nc.scalar.activation(out=tm.t[:], in_=Bim, func=AF.Copy, scale=ki), R=[bzb], W=[tm])
                                        op(DVE, lambda: nc.vector.scalar_tensor_tensor(out=wt.t[:], in0=Bre, scalar=kr, in1=tm.t[:], op0=ALU.mult, op1=ALU.subtract), R=[bzb, tm], W=[wt])
                                    else:
                                        op(ACT, lambda: nc.scalar.activation(out=tm.t[:], in_=Bre, func=AF.Copy, scale=ki), R=[bzb], W=[tm])
                                        op(DVE, lambda: nc.vector.scalar_tensor_tensor(out=wt.t[:], in0=Bim, scalar=kr, in1=tm.t[:], op0=ALU.mult, op1=ALU.add), R=[bzb, tm], W=[wt])
                                    op(PE, lambda: nc.tensor.transpose(out=trf.t[:, 0:128], in_=wt.t[:], identity=id_f), R=[wt, cmf], W=[trf])
                                    op(ACT, lambda: nc.scalar.copy(out=Wst.t[:, i * 2 + part, :], in_=trf.t[:, 0:128]), R=[trf], W=[Wst])
                                    if tau == 0:
                                        op(DVE, lambda: nc.vector.tensor_copy(out=BbZ_t[:, slot8 * 2 + part, :], in_=wt.t[:]), R=[wt], W=[BbZb])
                            for tau in range(9):
                                for part in range(2):
                                    tm = tmr.next()
                                    o_ = CL_t[:, slot8 * 18 + tau * 2 + part, :]
                                    if part == 0:
                                        op(ACT, lambda: nc.scalar.activation(out=tm.t[:], in_=Cim, func=AF.Copy, scale=sc(Pim[tau])), R=[bzb], W=[tm])
                                        op(DVE, lambda: nc.vector.scalar_tensor_tensor(out=o_, in0=Cre, scalar=sc(Pre[tau]), in1=tm.t[:], op0=ALU.mult, op1=ALU.subtract), R=[bzb, tm], W=[CLb])
                                    else:
                                        op(ACT, lambda: nc.scalar.activation(out=tm.t[:], in_=Cim, func=AF.Copy, scale=sc(Pre[tau])), R=[bzb], W=[tm])
                                        op(DVE, lambda: nc.vector.scalar_tensor_tensor(out=o_, in0=Cre, scalar=sc(NPim[tau]), in1=tm.t[:], op0=ALU.mult, op1=ALU.subtract), R=[bzb, tm], W=[CLb])
                            for part in range(2):
                                for (cb0, n) in cbs:
                                    sbk = sring.next()
                                    mm_group(sbk, [(sbk.t[:, :n], Wst.t[:, i * 2 + part, :], uv[:, cb0:cb0 + n, i]) for i in range(8)], R=[Wst, uTb])
                                    op(ACT, lambda: nc.scalar.copy(out=XA_t[:, part, cb0:cb0 + n], in_=sbk.t[:, :n]), R=[sbk], W=[XAb])
                            src, dst = XAb, XBb
                            for k in range(nlev):
                                s_ = 1 << k
                                a_, b_, nb_ = sc(Hre[k]), sc(Him[k]), sc(NHim[k])
                                X, Y = src.t, dst.t
                                if d == 0:
                                    lo, hi = slice(0, NCH - s_), slice(s_, NCH)
                                    keep = slice(0, s_)
                                else:
                                    lo, hi = slice(s_, NCH), slice(0, NCH - s_)
                                    keep = slice(NCH - s_, NCH)
                                op(ACT, lambda: nc.scalar.copy(out=Y[:, :, keep], in_=X[:, :, keep]), R=[src], W=[dst])
                                op(DVE, lambda: nc.vector.scalar_tensor_tensor(out=Y[:, 0, hi], in0=X[:, 0, lo], scalar=a_, in1=X[:, 0, hi], op0=ALU.mult, op1=ALU.add), R=[src], W=[dst])
                                op(DVE, lambda: nc.vector.scalar_tensor_tensor(out=Y[:, 0, hi], in0=X[:, 1, lo], scalar=nb_, in1=Y[:, 0, hi], op0=ALU.mult, op1=ALU.add), R=[src], W=[dst])
                                op(DVE, lambda: nc.vector.scalar_tensor_tensor(out=Y[:, 1, hi], in0=X[:, 0, lo], scalar=b_, in1=X[:, 1, hi], op0=ALU.mult, op1=ALU.add), R=[src], W=[dst])
                                op(DVE, lambda: nc.vector.scalar_tensor_tensor(out=Y[:, 1, hi], in0=X[:, 1, lo], scalar=a_, in1=Y[:, 1, hi], op0=ALU.mult, op1=ALU.add), R=[src], W=[dst])
                                src, dst = dst, src
                            Z = src.t
                            if d == 0:
                                op(ACT, lambda: nc.scalar.copy(out=Xb_t[:, slot8, :, 1:NCH], in_=Z[:, :, 0:NCH - 1]), R=[src], W=[Xbb])
                            else:
                                op(ACT, lambda: nc.scalar.copy(out=Xb_t[:, slot8, :, 0:NCH - 1], in_=Z[:, :, 1:NCH]), R=[src], W=[Xbb])
                    for d in range(2):
                        for tau in range(8):
                            kb = kring.next()
                            mms = []
                            for gpl in range(4):
                                s8 = gpl * 2 + d
                                for part in range(2):
                                    mms.append((kb.t[:, 0:128], BbZ_t[:, s8 * 2 + part, :], CL_t[:, s8 * 18 + tau * 2 + part, :]))
                            mm_group(kb, mms, R=[BbZb, CLb])
                            op(ACT, lambda: nc.scalar.copy(out=BD_t[:, d * 8 + tau, :], in_=kb.t[:, 0:128]), R=[kb], W=[BDb])
                    zv = zbuf_t[:].rearrange("p (c i) -> p c i", i=8)
                    for j in range(8):
                        for (cb0, n) in cbs:
                            ab = accr.next()
                            mms = []
                            for i in range(0, j + 1):
                                mms.append((ab.t[:, :n], BD_t[:, 0 * 8 + (j - i), :], uv[:, cb0:cb0 + n, i]))
                            for i in range(j, 8):
                                mms.append((ab.t[:, :n], BD_t[:, 1 * 8 + (i - j), :], uv[:, cb0:cb0 + n, i]))
                            for gpl in range(4):
                                for d in range(2):
                                    s8 = gpl * 2 + d
                                    tau = (j + 1) if d == 0 else (8 - j)
                                    for part in range(2):
                                        mms.append((ab.t[:, :n], CL_t[:, s8 * 18 + tau * 2 + part, :], Xb_t[:, s8, part, cb0:cb0 + n]))
                            mm_group(ab, mms, R=[BDb, uTb, CLb, Xbb])
                            yt = ytr.next()
                            op(DVE, lambda: nc.vector.scalar_tensor_tensor(out=yt.t[:, :n], in0=uv[:, cb0:cb0 + n, j], scalar=gcol(GV_SSMD + gc), in1=ab.t[:, :n], op0=ALU.mult, op1=ALU.add), R=[ab, uTb], W=[yt])
                            op(ACT, lambda: nc.scalar.activation(out=zv[:, cb0:cb0 + n, j], in_=yt.t[:, :n], func=AF.Gelu), R=[yt], W=[zb])
                    dma(zb.slot, zT_d[gc * 128:(gc + 1) * 128, :], zbuf_t[:], R=[zb])
                barrier()

        with ExitStack() as ph_att:
            kT_t = ph_att.enter_context(_sb("kT", [128, 2, LP], BF16))
            Vx_t = ph_att.enter_context(_sb("Vx", [128, NT, 4, 65], BF16))
            kTb, Vxb = Buf(kT_t), Buf(Vx_t)
            op(POOL, lambda: nc.gpsimd.memset(Vx_t[:].rearrange("p a b c -> p (a b c)"), 1.0), W=[Vxb])
            with ExitStack() as ph:
                slot_ptr[0] = base_slot_ptr
                chunks = [("q", i, i * 128) for i in range(8)] + [("k", i, 1024 + i * 128) for i in range(2)] + [("v", i, 1280 + i * 128) for i in range(2)]
                in_proj(ph, chunks, 512, 1536, kTb=kTb, Vxb=Vxb)
                barrier()
            slot_ptr[0] = base_slot_ptr
            ph = ph_att
            Qr = Ring([Buf(ph.enter_context(_sb(f"Q{i}", [128, LP], BF16)), new_slot()) for i in range(2)])
            PTr = Ring([Buf(ph.enter_context(_sb(f"PT{i}", [128, 2, 512], BF16))) for i in range(3)])
            recb = Buf(ph.enter_context(_sb("rec", [128, 512], F32)))
            yor = Ring([Buf(ph.enter_context(_sb(f"yo{i}", [128, 512], BF16)), new_slot()) for i in range(2)])
            Sr = Ring([Buf(PSA[:, 0:2, :]), Buf(PSA[:, 2:4, :])])
            Or = Ring([banks[4], banks[6]])
            DENb = banks[5]
            pairs = [(h, h + 4) for h in (0, 1, 2, 3)] + [(h, h + 4) for h in (8, 9, 10, 11)]
            for (ha, hb) in pairs:
                ga, gb = ha // 4, hb // 4
                kc = ga // 2
                Qb = Qr.next()
                dma(Qb.slot, Qb.t[0:64, :], qT_d[ha * 64:(ha + 1) * 64, :], W=[Qb])
                dma(Qb.slot, Qb.t[64:128, :], qT_d[hb * 64:(hb + 1) * 64, :], upd=[Qb])
                for (q0, NQ) in colblocks(S):
                    qc0 = 128 + q0
                    Ob = Or.next()
                    pend = None
                    first_pv = True

                    def do_pv(pend, last):
                        nonlocal first_pv
                        PTb, kt = pend
                        f_ = first_pv
                        first_pv = False
                        for b in (PTb, Vxb, cmb):
                            PE.wait(b.wr)
                        if f_:
                            for bk in (Ob, DENb):
                                PE.wait(bk.wr)
                                PE.wait(list(bk.rd.values()))
                        nc.tensor.matmul(Ob.t[0:64, :NQ], lhsT=Vx_t[:, kt, ga, 0:64], rhs=PTb.t[:, 0, :NQ], start=f_, stop=last)
                        nc.tensor.matmul(Ob.t[64:128, :NQ], lhsT=Vx_t[:, kt, gb, 0:64], rhs=PTb.t[:, 1, :NQ], start=f_, stop=last)
                        nc.tensor.matmul(DENb.t[0:64, :NQ], lhsT=ones_b[:, 0:64], rhs=PTb.t[:, 0, :NQ], start=f_, stop=last)
                        inst = nc.tensor.matmul(DENb.t[64:128, :NQ], lhsT=ones_b[:, 0:64], rhs=PTb.t[:, 1, :NQ], start=f_, stop=last)
                        ev = PE.mark(inst)
                        PTb.rd[PE.name] = ev
                        if last:
                            for bk in (Ob, DENb):
                                bk.wr = ev
                                bk.rd = {}

                    for kt in range(NT):
                        Sb = Sr.next()
                        mms = [(Sb.t[:, 0, :NQ], kT_t[0:64, kc, kt * 128:(kt + 1) * 128], Qb.t[0:64, qc0:qc0 + NQ], True, True),
                               (Sb.t[:, 1, :NQ], kT_t[64:128, kc, kt * 128:(kt + 1) * 128], Qb.t[64:128, qc0:qc0 + NQ], True, True)]
                        mm_group(Sb, mms, R=[kTb, Qb])
                        if pend is not None:
                            do_pv(pend, False)
                        PTb = PTr.next()
                        if kt == 0:
                            op(ACT, lambda: nc.scalar.activation(out=PTb.t[:, :, :NQ], in_=Sb.t[:, :, :NQ], func=AF.Exp, scale=0.125, bias=gcol(GV_MASK)), R=[Sb], W=[PTb])
                        else:
                            op(ACT, lambda: nc.scalar.activation(out=PTb.t[:, :, :NQ], in_=Sb.t[:, :, :NQ], func=AF.Exp, scale=0.125), R=[Sb], W=[PTb])
                        pend = (PTb, kt)
                    do_pv(pend, True)
                    op(DVE, lambda: nc.vector.reciprocal(out=recb.t[:, :NQ], in_=DENb.t[:, :NQ]), R=[DENb], W=[recb])
                    yo = yor.next()
                    op(DVE, lambda: nc.vector.tensor_tensor(out=yo.t[:, :NQ], in0=Ob.t[:, :NQ], in1=recb.t[:, :NQ], op=ALU.mult), R=[Ob, recb], W=[yo])
                    dma(yo.slot, yaT_d[ha * 64:(ha + 1) * 64, q0:q0 + NQ], yo.t[0:64, :NQ], R=[yo])
                    dma(yo.slot, yaT_d[hb * 64:(hb + 1) * 64, q0:q0 + NQ], yo.t[64:128, :NQ], R=[yo])
            barrier()

        with ExitStack() as ph:
            slot_ptr[0] = base_slot_ptr

            def wload(name, src, kchunks, cols, c0=0):
                t = ph.enter_context(_sb(name, [128, kchunks, cols], BF16))
                b = Buf(t, new_slot())
                dma(b.slot, t[:], kview(src)[:, :, c0:c0 + cols], W=[b])
                return b
            Wg = wload("Wg", winb, 8, 2048, 2048)
            Wgl = wload("Wgl", wglub, 4, 512)
            Wsp = wload("Wsp", wspb, 4, D)
            Wap = wload("Wap", wapb, 8, D)
            Wo = wload("Wo", woutb, 8, D)
            xring = Ring([Buf(ph.enter_context(_sb(f"xt{i}", [128, 8, 512], F32)), new_slot()) for i in range(2)])
            sqb_ = Buf(ph.enter_context(_sb("sq3", [128, 8, 512], BF16)))
            xbb_ = Buf(ph.enter_context(_sb("xb3", [128, 8, 512], BF16)))
            ztr = Ring([Buf(ph.enter_context(_sb(f"zt{i}", [128, 4, 512], BF16)), new_slot()) for i in range(2)])
            yar = Ring([Buf(ph.enter_context(_sb(f"ya{i}", [128, 8, 512], BF16)), new_slot()) for i in range(2)])
            ysb = Buf(ph.enter_context(_sb("ys", [128, 4, 512], BF16)))
            mgb = Buf(ph.enter_context(_sb("mg", [128, 8, 512], BF16)))
            f32r = Ring([Buf(ph.enter_context(_sb(f"f3r{i}", [128, 512], F32))) for i in range(10)])
            rstdb = Buf(ph.enter_context(_sb("rstd3", [128, 512], F32)))
            h1r = Ring([Buf(ph.enter_context(_sb(f"h1s{i}", [128, 512], F32)), new_slot()) for i in range(3)])
            pr = Ring(banks)
            xTv = kview(xT)
            zTv = kview(zT_d)
            yaTv = kview(yaT_d)
            N = 512
            for (q0, _) in colblocks(S):
                c0 = 128 + q0
                xt = xring.next()
                dma(xt.slot, xt.t[:], xTv[:, :, c0:c0 + N], W=[xt])
                zt = ztr.next()
                dma(zt.slot, zt.t[:], zTv[:, :, c0:c0 + N], W=[zt])
                ya = yar.next()
                dma(ya.slot, ya.t[:], yaTv[:, :, q0:q0 + N], W=[ya])
                op(ACT, lambda: nc.scalar.activation(out=sqb_.t[:], in_=xt.t[:], func=AF.Square), R=[xt], W=[sqb_])
                op(DVE, lambda: nc.vector.tensor_copy(out=xbb_.t[:], in_=xt.t[:]), R=[xt], W=[xbb_])
                ssb = pr.next()
                mm_group(ssb, [(ssb.t[:], ones_b, sqb_.t[:, k, :]) for k in range(8)], R=[sqb_, cmb])
                rt = f32r.next()
                op(ACT, lambda: nc.scalar.activation(out=rt.t[:], in_=ssb.t[:], func=AF.Ln, scale=1.0 / D, bias=eps_c), R=[ssb], W=[rt])
                op(ACT, lambda: nc.scalar.activation(out=rstdb.t[:], in_=rt.t[:], func=AF.Exp, scale=-0.5), R=[rt], W=[rstdb])
                for oc in range(4):
                    ps = pr.next()
                    mm_group(ps, [(ps.t[:], Wgl.t[:, k, oc * 128:(oc + 1) * 128], zt.t[:, k, :]) for k in range(4)], R=[Wgl, zt])
                    sg = f32r.next()
                    op(ACT, lambda: nc.scalar.activation(out=sg.t[:], in_=ps.t[:], func=AF.Sigmoid, bias=gcol(GV_BGLU + oc)), R=[ps], W=[sg])
                    op(DVE, lambda: nc.vector.tensor_tensor(out=ysb.t[:, oc, :], in0=zt.t[:, oc, :], in1=sg.t[:], op=ALU.mult), R=[zt, sg], W=[ysb])
                for oc in range(8):
                    sgs = []
                    for gi in range(2):
                        ps = pr.next()
                        wc = gi * 1024 + oc * 128
                        mm_group(ps, [(ps.t[:], Wg.t[:, k, wc:wc + 128], xbb_.t[:, k, :]) for k in range(8)], R=[Wg, xbb_])
                        tg = f32r.next()
                        op(DVE, lambda: nc.vector.tensor_tensor(out=tg.t[:], in0=ps.t[:], in1=rstdb.t[:], op=ALU.mult), R=[ps, rstdb], W=[tg])
                        sg = f32r.next()
                        op(ACT, lambda: nc.scalar.activation(out=sg.t[:], in_=tg.t[:], func=AF.Sigmoid), R=[tg], W=[sg])
                        sgs.append(sg)
                    ps = pr.next()
                    mm_group(ps, [(ps.t[:], Wsp.t[:, k, oc * 128:(oc + 1) * 128], ysb.t[:, k, :]) for k in range(4)], R=[Wsp, ysb])
                    m1 = f32r.next()
                    op(DVE, lambda: nc.vector.tensor_tensor(out=m1.t[:], in0=ps.t[:], in1=sgs[0].t[:], op=ALU.mult), R=[ps, sgs[0]], W=[m1])
                    ps2 = pr.next()
                    mm_group(ps2, [(ps2.t[:], Wap.t[:, k, oc * 128:(oc + 1) * 128], ya.t[:, k, :]) for k in range(8)], R=[Wap, ya])
                    m2 = f32r.next()
                    op(DVE, lambda: nc.vector.tensor_tensor(out=m2.t[:], in0=ps2.t[:], in1=sgs[1].t[:], op=ALU.mult), R=[ps2, sgs[1]], W=[m2])
                    op(DVE, lambda: nc.vector.tensor_tensor(out=mgb.t[:, oc, :], in0=m1.t[:], in1=m2.t[:], op=ALU.add), R=[m1, m2], W=[mgb])
                for oc in range(8):
                    ps = pr.next()
                    mm_group(ps, [(ps.t[:], Wo.t[:, k, oc * 128:(oc + 1) * 128], mgb.t[:, k, :]) for k in range(8)], R=[Wo, mgb])
                    h1 = h1r.next()
                    op(DVE, lambda: nc.vector.tensor_tensor(out=h1.t[:], in0=ps.t[:], in1=xt.t[:, oc, :], op=ALU.add), R=[ps, xt], W=[h1])
                    dma(h1.slot, h1T_d[oc * 128:(oc + 1) * 128, q0:q0 + N], h1.t[:], R=[h1])
            barrier()

        with ExitStack() as ph:
            slot_ptr[0] = base_slot_ptr
            W1_t = ph.enter_context(_sb("W1", [128, 8, DFF], BF16))
            W2_t = ph.enter_context(_sb("W2", [128, 32, D], BF16))
            W1b_, W2b_ = Buf(W1_t, new_slot()), Buf(W2_t, new_slot())
            dma(W1b_.slot, W1_t[:], kview(w1b), W=[W1b_])
            dma(W2b_.slot, W2_t[:], kview(w2b), W=[W2b_])
            N = 256
            hr = Ring([Buf(ph.enter_context(_sb(f"h{i}", [128, 8, N], F32)), new_slot()) for i in range(2)])
            sqb_ = Buf(ph.enter_context(_sb("sq4", [128, 8, N], BF16)))
            hbb = Buf(ph.enter_context(_sb("hb4", [128, 8, N], BF16)))
            Ab = Buf(ph.enter_context(_sb("A4", [128, 32, N], BF16)))
            h2b = Buf(ph.enter_context(_sb("h2", [128, 8, N], F32)))
            osr = Ring([Buf(ph.enter_context(_sb(f"os{i}", [128, 8, N], F32)), new_slot()) for i in range(1)])
            f32r = Ring([Buf(ph.enter_context(_sb(f"f4r{i}", [128, N], F32))) for i in range(6)])
            rstdb = Buf(ph.enter_context(_sb("rstd4", [128, N], F32)))
            rstd5 = Buf(ph.enter_context(_sb("rstd5", [128, N], F32)))
            pr = Ring(banks)
            h1v = kview(h1T_d)
            outv = kview(outT)
            for (q0, _) in colblocks(S, N):
                hb = hr.next()
                dma(hb.slot, hb.t[:], h1v[:, :, q0:q0 + N], W=[hb])
                op(ACT, lambda: nc.scalar.activation(out=sqb_.t[:], in_=hb.t[:], func=AF.Square), R=[hb], W=[sqb_])
                op(DVE, lambda: nc.vector.tensor_copy(out=hbb.t[:], in_=hb.t[:]), R=[hb], W=[hbb])
                ssb = pr.next()
                mm_group(ssb, [(ssb.t[:, :N], ones_b, sqb_.t[:, k, :]) for k in range(8)], R=[sqb_, cmb])
                rt = f32r.next()
                op(ACT, lambda: nc.scalar.activation(out=rt.t[:], in_=ssb.t[:, :N], func=AF.Ln, scale=1.0 / D, bias=eps_c), R=[ssb], W=[rt])
                op(ACT, lambda: nc.scalar.activation(out=rstdb.t[:], in_=rt.t[:], func=AF.Exp, scale=-0.5), R=[rt], W=[rstdb])
                for f in range(32):
                    ps = pr.next()
                    mm_group(ps, [(ps.t[:, :N], W1_t[:, k, f * 128:(f + 1) * 128], hbb.t[:, k, :]) for k in range(8)], R=[W1b_, hbb])
                    tr_ = f32r.next()
                    op(DVE, lambda: nc.vector.scalar_tensor_tensor(out=tr_.t[:], in0=ps.t[:, :N], scalar=0.0, in1=rstdb.t[:], op0=ALU.max, op1=ALU.mult), R=[ps, rstdb], W=[tr_])
                    op(ACT, lambda: nc.scalar.activation(out=Ab.t[:, f, :], in_=tr_.t[:], func=AF.Square), R=[tr_], W=[Ab])
                for oc in range(8):
                    ps = pr.next()
                    mm_group(ps, [(ps.t[:, :N], W2_t[:, f, oc * 128:(oc + 1) * 128], Ab.t[:, f, :]) for f in range(32)], R=[W2b_, Ab])
                    op(DVE, lambda: nc.vector.tensor_tensor(out=h2b.t[:, oc, :], in0=ps.t[:, :N], in1=hb.t[:, oc, :], op=ALU.add), R=[ps, hb], W=[h2b])
                op(ACT, lambda: nc.scalar.activation(out=sqb_.t[:], in_=h2b.t[:], func=AF.Square), R=[h2b], W=[sqb_])
                ssb = pr.next()
                mm_group(ssb, [(ssb.t[:, :N], ones_b, sqb_.t[:, k, :]) for k in range(8)], R=[sqb_, cmb])
                rt = f32r.next()
                op(ACT, lambda: nc.scalar.activation(out=rt.t[:], in_=ssb.t[:, :N], func=AF.Ln, scale=1.0 / D, bias=eps_c), R=[ssb], W=[rt])
                op(ACT, lambda: nc.scalar.activation(out=rstd5.t[:], in_=rt.t[:], func=AF.Exp, scale=-0.5), R=[rt], W=[rstd5])
                os_ = osr.next()
                for oc in range(8):
                    op(DVE, lambda: nc.vector.scalar_tensor_tensor(out=os_.t[:, oc, :], in0=h2b.t[:, oc, :], scalar=gcol(GV_FIN + oc), in1=rstd5.t[:], op0=ALU.mult, op1=ALU.mult), R=[h2b, rstd5], W=[os_])
                dma(os_.slot, outv[:, :, q0:q0 + N], os_.t[:], R=[os_])
            barrier()
    return nc


def _const_tables(S):
    LP = S + 128
    cm = np.zeros((128, 512), np.float32)
    cm[:, 0:128] = 1.0
    cm[0:64, 128:192] = 1.0
    cm[64:128, 192:256] = 1.0
    for i in range(64):
        cm[2 * i + 1, 256 + 2 * i] = -1.0
        cm[2 * i, 256 + 2 * i + 1] = 1.0
    cm[:, 384:512] = np.eye(128, dtype=np.float32)
    rows = S // 64
    row_id = np.repeat(np.arange(rows, dtype=np.float32), 64)
    col_id = np.tile(np.arange(64, dtype=np.float32), rows)
    inv_freq = (np.float32(10000.0) ** (-np.arange(16, dtype=np.float32) / np.float32(16))).astype(np.float32)
    ang = np.concatenate([row_id[:, None] * inv_freq, col_id[:, None] * inv_freq], axis=-1).astype(np.float32)
    ang = np.concatenate([np.zeros((128, 32), np.float32), ang], axis=0)
    cos = np.cos(ang).astype(np.float32)
    sin = np.sin(ang).astype(np.float32)
    pair = (np.arange(128) % 64) // 2
    C = np.ascontiguousarray(cos[:, pair].T)
    Sn = np.ascontiguousarray(sin[:, pair].T)
    return cm, C, Sn


def _prep_shared(inp, S):
    f = lambda a: np.ascontiguousarray(np.asarray(a, dtype=np.float32))
    cm, C, Sn = _const_tables(S)
    gv = np.zeros((128, NG), np.float32)
    gv[:, GV_MIX:GV_MIX + 8] = f(inp["norm_mix_g"])[0].reshape(8, 128).T
    gv[:, GV_MLP:GV_MLP + 8] = f(inp["norm_mlp_g"])[0].reshape(8, 128).T
    gv[:, GV_FIN:GV_FIN + 8] = f(inp["norm_final_g"]).reshape(8, 128).T
    gv[:, GV_QG] = np.tile(f(inp["q_norm_g"])[0], 2)
    gv[:, GV_KG] = np.tile(f(inp["k_norm_g"])[0], 2)
    gv[:, GV_BGLU:GV_BGLU + 4] = f(inp["b_glu"])[0].reshape(4, 128).T
    gv[:, GV_SSMD:GV_SSMD + 4] = f(inp["ssm_d"])[0].reshape(4, 128).T
    gv[0:112, GV_MASK] = -30000.0
    a_re, a_im, ldt = f(inp["ssm_a_re"])[0], f(inp["ssm_a_im"])[0], f(inp["ssm_log_dt"])[0]
    b_re, b_im = f(inp["ssm_b_re"])[0], f(inp["ssm_b_im"])[0]
    c_re, c_im = f(inp["ssm_c_re"])[0], f(inp["ssm_c_im"])[0]
    ssmA = np.zeros((128, 96), np.float32)
    bz = np.zeros((32, 128, 4, 128), np.float32)
    for d in range(2):
        for gp in range(16):
            ci = d * 16 + gp
            for g2 in range(2):
                g = 2 * gp + g2
                gl = g % 8
                ps = slice(g2 * 64, g2 * 64 + 64)
                ssmA[ps, ci] = a_re[d, g]
                ssmA[ps, 32 + ci] = a_im[d, g]
                ssmA[ps, 64 + ci] = ldt[d, g]
                bz[ci, ps, 0, gl * 16:(gl + 1) * 16] = b_re[d, g]
                bz[ci, ps, 1, gl * 16:(gl + 1) * 16] = b_im[d, g]
                bz[ci, ps, 2, gl * 16:(gl + 1) * 16] = c_re[d, g].T
                bz[ci, ps, 3, gl * 16:(gl + 1) * 16] = c_im[d, g].T
    shared = {
        "w_in": f(inp["w_in"])[0], "w_glu": f(inp["w_glu"])[0], "w_sp": f(inp["w_ssm_proj"])[0],
        "w_ap": f(inp["w_attn_proj"])[0], "w_out": f(inp["w_out"])[0], "w1": f(inp["w_mlp_in"])[0],
        "w2": f(inp["w_mlp_out"])[0], "gv": gv, "cmat": cm, "ropeC": C, "ropeS": Sn, "ssmA": ssmA,
        "bz": bz.reshape(32, 128, 512),
    }
    return shared


def _make_xT(xb, meta):
    S = xb.shape[0]
    full = np.concatenate([np.zeros((112, D), np.float32), np.asarray(meta, np.float32), np.asarray(xb, np.float32)], axis=0)
    return np.ascontiguousarray(full.T)


_NC_CACHE = {}


def kernel(**inputs):
    x = np.asarray(inputs["x"], dtype=np.float32)
    B, S, _ = x.shape
    shared = _prep_shared(inputs, S)
    if S not in _NC_CACHE:
        _NC_CACHE[S] = build(S)
    nc = _NC_CACHE[S]
    in_maps = []
    for b in range(B):
        m = dict(shared)
        m["xT"] = _make_xT(x[b], inputs["meta_tokens"])
        in_maps.append(m)
    res = run_bass_kernel_spmd(nc, in_maps, core_ids=list(range(B)))
    out = np.stack([np.ascontiguousarray(r["outT"].T) for r in res.results], axis=0)
    return out.astype(np.float32)
```

```python
import numpy as np
from contextlib import ExitStack
import concourse.bass as bass
import concourse.mybir as mybir
from concourse.bass_utils import run_bass_kernel_spmd

F32 = mybir.dt.float32
BF16 = mybir.dt.bfloat16
AF = mybir.ActivationFunctionType
ALU = mybir.AluOpType

D = 1024
DFF = 4096
EPS = 1e-6
GV_MIX, GV_MLP, GV_FIN, GV_QG, GV_KG, GV_BGLU, GV_SSMD, GV_MASK, NG = 0, 8, 16, 24, 25, 26, 30, 34, 35


class Ev:
    __slots__ = ("sem", "val", "key")

    def __init__(s, sem, val, key):
        s.sem, s.val, s.key = sem, val, key


class Eng:
    def __init__(s, name, h, sem):
        s.name, s.h, s.sem = name, h, sem
        s.cnt = 0
        s.seen = {}
        s.last = None
        s.selfsync = True

    def wait(s, evs):
        if evs is None:
            return
        if isinstance(evs, Ev):
            evs = [evs]
        for ev in evs:
            if ev is None:
                continue
            if isinstance(ev, (list, tuple)):
                s.wait(ev)
                continue
            if ev.key == s.name and (s.name == "pe" or not s.selfsync):
                continue
            if s.seen.get(ev.key, 0) >= ev.val:
                continue
            s.h.wait_ge(ev.sem, ev.val)
            s.seen[ev.key] = ev.val

    def mark(s, inst):
        s.cnt += 1
        inst.then_inc(s.sem, 1)
        s.last = Ev(s.sem, s.cnt, s.name)
        return s.last


class Slot:
    def __init__(s, sem, key):
        s.sem, s.key = sem, key
        s.cnt = 0
        s.last = None


class Buf:
    def __init__(s, t, slot=None):
        s.t = t
        s.slot = slot
        s.wr = None
        s.rd = {}


class Ring:
    def __init__(s, bufs):
        s.bufs = bufs
        s.i = 0

    def next(s):
        b = s.bufs[s.i % len(s.bufs)]
        s.i += 1
        return b


def build(S, dbg=False):
    LP = S + 128
    NT = LP // 128
    NCH = LP // 8
    nc = bass.Bass("TRN2", target_bir_lowering=False)

    def din(name, shape, dt=F32):
        return nc.dram_tensor(name, shape, dt, kind="ExternalInput").ap()

    def dscr(name, shape, dt):
        return nc.dram_tensor(name, shape, dt, kind=("ExternalOutput" if dbg else "Internal")).ap()

    xT = din("xT", [D, LP])
    w_in = din("w_in", [D, 4096])
    w_glu = din("w_glu", [512, 512])
    w_sp = din("w_sp", [512, D])
    w_ap = din("w_ap", [D, D])
    w_out = din("w_out", [D, D])
    w1 = din("w1", [D, DFF])
    w2 = din("w2", [DFF, D])
    gv_d = din("gv", [128, NG])
    cmat_d = din("cmat", [128, 4 * 128])
    ropeC = din("ropeC", [128, LP])
    ropeS = din("ropeS", [128, LP])
    ssmA = din("ssmA", [128, 96])
    bz = din("bz", [32, 128, 512])
    outT = nc.dram_tensor("outT", [D, S], F32, kind="ExternalOutput").ap()

    winb = dscr("winb", [D, 4096], BF16)
    wglub = dscr("wglub", [512, 512], BF16)
    wspb = dscr("wspb", [512, D], BF16)
    wapb = dscr("wapb", [D, D], BF16)
    woutb = dscr("woutb", [D, D], BF16)
    w1b = dscr("w1b", [D, DFF], BF16)
    w2b = dscr("w2b", [DFF, D], BF16)
    qT_d = dscr("qT_d", [D, LP], BF16)
    zT_d = dscr("zT_d", [512, LP], BF16)
    yaT_d = dscr("yaT_d", [D, S], BF16)
    h1T_d = dscr("h1T_d", [D, S], F32)

    _uid = [0]

    def _sb(name, shape, dt):
        _uid[0] += 1
        return nc.sbuf_tensor(f"{name}_{_uid[0]}", shape, dt)

    def kview(ap):
        return ap.rearrange("(k p) l -> p k l", p=128)

    top = ExitStack()
    with top:
        sems = [top.enter_context(nc.semaphore(f"sem{i}")) for i in range(64)]
        PE = Eng("pe", nc.tensor, sems[0])
        ACT = Eng("act", nc.scalar, sems[1])
        DVE = Eng("dve", nc.vector, sems[2])
        POOL = Eng("pool", nc.gpsimd, sems[3])
        SP = Eng("sp", nc.sync, None)
        engines = [PE, ACT, DVE, POOL]
        slots = [Slot(sems[4 + i], f"dma{i}") for i in range(60)]
        slot_ptr = [0]

        def new_slot():
            s = slots[slot_ptr[0]]
            slot_ptr[0] += 1
            return s

        def op(E, fn, R=(), W=(), mark=True):
            for b in R:
                E.wait(b.wr)
            for b in W:
                E.wait(b.wr)
                E.wait([e for k_, e in b.rd.items() if k_ != E.name])
            inst = fn()
            if mark:
                ev = E.mark(inst)
                for b in R:
                    b.rd[E.name] = ev
                for b in W:
                    b.wr = ev
                    b.rd = {}
                return ev
            return None

        def dma(slot, out, in_, R=(), W=(), upd=()):
            for b in R:
                SP.wait(b.wr)
            for b in W:
                SP.wait(b.wr)
                SP.wait(list(b.rd.values()))
            inst = nc.sync.dma_start(out=out, in_=in_)
            slot.cnt += 16
            inst.then_inc(slot.sem, 16)
            ev = Ev(slot.sem, slot.cnt, slot.key)
            slot.last = ev
            for b in R:
                b.rd[slot.key] = ev
            for b in list(W) + list(upd):
                b.wr = ev
                b.rd = {}
            return ev

        def mm_group(bank, mms, R=()):
            for b in R:
                PE.wait(b.wr)
            PE.wait(bank.wr)
            PE.wait(list(bank.rd.values()))
            n = len(mms)
            inst = None
            for i, m in enumerate(mms):
                if len(m) == 3:
                    st, sp_ = (i == 0), (i == n - 1)
                else:
                    st, sp_ = m[3], m[4]
                inst = nc.tensor.matmul(m[0], lhsT=m[1], rhs=m[2], start=st, stop=sp_)
            ev = PE.mark(inst)
            for b in R:
                b.rd[PE.name] = ev
            bank.wr = ev
            bank.rd = {}
            return ev

        def barrier():
            evs = [E.last for E in engines if E.last is not None]
            evs += [s.last for s in slots if s.last is not None]
            for E in engines + [SP]:
                E.wait(evs)

        def colblocks(n, w=512):
            return [(c, min(w, n - c)) for c in range(0, n, w)]

        gv_t = top.enter_context(_sb("gv", [128, NG], F32))
        cm_f = top.enter_context(_sb("cm_f", [128, 512], F32))
        cm_b = top.enter_context(_sb("cm_b", [128, 512], BF16))
        PSA = top.enter_context(nc.psum_tensor("psa", [128, 7, 512], F32))
        PSB = top.enter_context(nc.psum_tensor("psb", [128, 1024], BF16))
        gv = Buf(gv_t, new_slot())
        cmf = Buf(cm_f, new_slot())
        cmb = Buf(cm_b)
        dma(gv.slot, gv_t[:], gv_d, W=[gv])
        dma(cmf.slot, cm_f[:], cmat_d, W=[cmf])
        op(DVE, lambda: nc.vector.tensor_copy(out=cm_b[:], in_=cm_f[:]), R=[cmf], W=[cmb])
        ones_b = cm_b[:, 0:128]
        blk_b = cm_b[:, 128:256]
        rot_b = cm_b[:, 256:384]
        id_b = cm_b[:, 384:512]
        id_f = cm_f[:, 384:512]
        ones_f = cm_f[:, 0:128]
        DVE.wait(gv.wr)
        ACT.wait(gv.wr)
        POOL.wait(gv.wr)
        banks = [Buf(PSA[:, i, :]) for i in range(7)]
        bankB = Buf(PSB)
        base_slot_ptr = slot_ptr[0]

        def gcol(c):
            return gv_t[:, c:c + 1]

        with ExitStack() as ph:
            slot_ptr[0] = base_slot_ptr
            stg = Ring([Buf(ph.enter_context(_sb(f"wst{i}", [128, 2048], F32)), new_slot()) for i in range(3)])
            ost = Ring([Buf(ph.enter_context(_sb(f"wso{i}", [128, 2048], BF16)), new_slot()) for i in range(3)])
            rr = [0]

            def convert(src, dst, rows, cols, gain0=None):
                for k in range(rows // 128):
                    for c0 in range(0, cols, 2048):
                        n = min(2048, cols - c0)
                        st = stg.next()
                        dma(st.slot, st.t[:, :n], src[k * 128:(k + 1) * 128, c0:c0 + n], W=[st])
                        ob = ost.next()
                        which = rr[0] % 3
                        rr[0] += 1
                        if gain0 is None:
                            if which == 0:
                                op(DVE, lambda: nc.vector.tensor_copy(out=ob.t[:, :n], in_=st.t[:, :n]), R=[st], W=[ob])
                            elif which == 1:
                                op(POOL, lambda: nc.gpsimd.tensor_copy(out=ob.t[:, :n], in_=st.t[:, :n]), R=[st], W=[ob])
                            else:
                                op(ACT, lambda: nc.scalar.copy(out=ob.t[:, :n], in_=st.t[:, :n]), R=[st], W=[ob])
                        else:
                            g = gcol(gain0 + k)
                            if which == 0:
                                op(DVE, lambda: nc.vector.tensor_scalar(out=ob.t[:, :n], in0=st.t[:, :n], scalar1=g, scalar2=None, op0=ALU.mult), R=[st], W=[ob])
                            elif which == 1:
                                op(POOL, lambda: nc.gpsimd.tensor_scalar(out=ob.t[:, :n], in0=st.t[:, :n], scalar1=g, scalar2=None, op0=ALU.mult), R=[st], W=[ob])
                            else:
                                op(ACT, lambda: nc.scalar.activation(out=ob.t[:, :n], in_=st.t[:, :n], func=AF.Copy, scale=g), R=[st], W=[ob])
                        dma(ob.slot, dst[k * 128:(k + 1) * 128, c0:c0 + n], ob.t[:, :n], R=[ob])

            convert(w_in, winb, D, 4096, GV_MIX)
            barrier()

        def in_proj(ph, chunks, wcol0, wcols, uTb=None, kTb=None, Vxb=None):
            W_t = ph.enter_context(_sb("Wp", [128, 8, wcols], BF16))
            Wb = Buf(W_t, new_slot())
            dma(Wb.slot, W_t[:], kview(winb)[:, :, wcol0:wcol0 + wcols], W=[Wb])
            xring = Ring([Buf(ph.enter_context(_sb(f"xt{i}", [128, 8, 512], F32)), new_slot()) for i in range(2)])
            sqr = Ring([Buf(ph.enter_context(_sb(f"sq{i}", [128, 8, 512], BF16))) for i in range(1)])
            xbr = Ring([Buf(ph.enter_context(_sb(f"xb{i}", [128, 8, 512], BF16))) for i in range(2)])
            need_rope = any(c[0] in "qk" for c in chunks)
            if need_rope:
                csr = Ring([Buf(ph.enter_context(_sb(f"cs{i}", [128, 2, 512], F32)), new_slot()) for i in range(2)])
                qor = Ring([Buf(ph.enter_context(_sb(f"qo{i}", [128, 512], BF16)), new_slot()) for i in range(3)])
            f32r = Ring([Buf(ph.enter_context(_sb(f"f32r{i}", [128, 512], F32))) for i in range(8)])
            bfr = Ring([Buf(ph.enter_context(_sb(f"bfr{i}", [128, 512], BF16))) for i in range(4)])
            rstdr = Ring([Buf(ph.enter_context(_sb(f"rstd{i}", [128, 512], F32))) for i in range(2)])
            mainr = Ring(banks[0:3])
            ssb = banks[3]
            hsr = Ring(banks[4:5])
            rotr = Ring(banks[5:7])
            trb = bankB
            xTv = kview(xT)
            for (c0, N) in colblocks(LP):
                xt = xring.next()
                dma(xt.slot, xt.t[:, :, :N], xTv[:, :, c0:c0 + N], W=[xt])
                if need_rope:
                    cs = csr.next()
                    dma(cs.slot, cs.t[:, 0, :N], ropeC[:, c0:c0 + N], W=[cs])
                    dma(cs.slot, cs.t[:, 1, :N], ropeS[:, c0:c0 + N], upd=[cs])
                sq = sqr.next()
                op(ACT, lambda: nc.scalar.activation(out=sq.t[:, :, :N], in_=xt.t[:, :, :N], func=AF.Square), R=[xt], W=[sq])
                xb = xbr.next()
                op(DVE, lambda: nc.vector.tensor_copy(out=xb.t[:, :, :N], in_=xt.t[:, :, :N]), R=[xt], W=[xb])
                mm_group(ssb, [(ssb.t[:, :N], ones_b, sq.t[:, k, :N]) for k in range(8)], R=[sq, cmb])
                rt = f32r.next()
                op(ACT, lambda: nc.scalar.activation(out=rt.t[:, :N], in_=ssb.t[:, :N], func=AF.Ln, scale=1.0 / D, bias=eps_c), R=[ssb], W=[rt])
                rstd = rstdr.next()
                op(ACT, lambda: nc.scalar.activation(out=rstd.t[:, :N], in_=rt.t[:, :N], func=AF.Exp, scale=-0.5), R=[rt], W=[rstd])
                for (kind, idx, wc) in chunks:
                    ps = mainr.next()
                    mm_group(ps, [(ps.t[:, :N], W_t[:, k, wc:wc + 128], xb.t[:, k, :N]) for k in range(8)], R=[Wb, xb])
                    if kind == "u":
                        op(DVE, lambda: nc.vector.tensor_tensor(out=uTb.t[:, idx, :, c0 // 8:(c0 + N) // 8].rearrange("p i c -> p c i"), in0=ps.t[:, :N].rearrange("p (c i) -> p c i", i=8), in1=rstd.t[:, :N].rearrange("p (c i) -> p c i", i=8), op=ALU.mult), R=[ps, rstd], W=[uTb])
                    elif kind == "v":
                        vt = bfr.next()
                        op(DVE, lambda: nc.vector.tensor_tensor(out=vt.t[:, :N], in0=ps.t[:, :N], in1=rstd.t[:, :N], op=ALU.mult), R=[ps, rstd], W=[vt])
                        for s_ in range(N // 128):
                            tt = c0 // 128 + s_
                            op(PE, lambda: nc.tensor.transpose(out=trb.t[:, s_ * 128:(s_ + 1) * 128], in_=vt.t[:, s_ * 128:(s_ + 1) * 128], identity=id_b), R=[vt, cmb], W=[trb])
                            op(ACT, lambda: nc.scalar.copy(out=Vxb.t[:, tt, 2 * idx:2 * idx + 2, 0:64],
                                                           in_=trb.t[:, s_ * 128:(s_ + 1) * 128].rearrange("p (g d) -> p g d", g=2)), R=[trb], W=[Vxb])
                    else:
                        tq = f32r.next()
                        op(DVE, lambda: nc.vector.tensor_tensor(out=tq.t[:, :N], in0=ps.t[:, :N], in1=rstd.t[:, :N], op=ALU.mult), R=[ps, rstd], W=[tq])
                        sq2 = bfr.next()
                        op(ACT, lambda: nc.scalar.activation(out=sq2.t[:, :N], in_=tq.t[:, :N], func=AF.Square), R=[tq], W=[sq2])
                        hs = hsr.next()
                        mm_group(hs, [(hs.t[:, :N], blk_b, sq2.t[:, :N])], R=[sq2, cmb])
                        rt2 = f32r.next()
                        op(ACT, lambda: nc.scalar.activation(out=rt2.t[:, :N], in_=hs.t[:, :N], func=AF.Ln, scale=1.0 / 64, bias=eps_c), R=[hs], W=[rt2])
                        rq = f32r.next()
                        op(ACT, lambda: nc.scalar.activation(out=rq.t[:, :N], in_=rt2.t[:, :N], func=AF.Exp, scale=-0.5), R=[rt2], W=[rq])
                        tq2 = bfr.next()
                        gc_ = gcol(GV_QG if kind == "q" else GV_KG)
                        op(DVE, lambda: nc.vector.scalar_tensor_tensor(out=tq2.t[:, :N], in0=tq.t[:, :N], scalar=gc_, in1=rq.t[:, :N], op0=ALU.mult, op1=ALU.mult), R=[tq, rq], W=[tq2])
                        rot = rotr.next()
                        mm_group(rot, [(rot.t[:, :N], rot_b, tq2.t[:, :N])], R=[tq2, cmb])
                        t1 = f32r.next()
                        op(DVE, lambda: nc.vector.tensor_tensor(out=t1.t[:, :N], in0=tq2.t[:, :N], in1=cs.t[:, 0, :N], op=ALU.mult), R=[tq2, cs], W=[t1])
                        t2 = f32r.next()
                        op(DVE, lambda: nc.vector.tensor_tensor(out=t2.t[:, :N], in0=rot.t[:, :N], in1=cs.t[:, 1, :N], op=ALU.mult), R=[rot, cs], W=[t2])
                        if kind == "k":
                            op(DVE, lambda: nc.vector.tensor_tensor(out=kTb.t[:, idx, c0:c0 + N], in0=t1.t[:, :N], in1=t2.t[:, :N], op=ALU.add), R=[t1, t2], W=[kTb])
                        else:
                            qo = qor.next()
                            op(DVE, lambda: nc.vector.tensor_tensor(out=qo.t[:, :N], in0=t1.t[:, :N], in1=t2.t[:, :N], op=ALU.add), R=[t1, t2], W=[qo])
                            dma(qo.slot, qT_d[idx * 128:(idx + 1) * 128, c0:c0 + N], qo.t[:, :N], R=[qo])

        eps_t = top.enter_context(_sb("eps", [128, 1], F32))
        epsb = Buf(eps_t)
        op(DVE, lambda: nc.vector.memset(eps_t[:], EPS), W=[epsb])
        ACT.wait(epsb.wr)
        eps_c = eps_t[:, 0:1]

        with ExitStack() as ph_prm:
            ph = ph_prm
            slot_ptr[0] = base_slot_ptr
            NPR = 104
            PR_t = ph.enter_context(_sb("PR", [128, NPR * 32], F32))
            PRb = Buf(PR_t, new_slot())
            pr_i = [0]

            def col():
                i = pr_i[0]
                pr_i[0] += 1
                assert i < NPR
                return PR_t[:, i * 32:(i + 1) * 32]

            A_re, A_im, LDT = col(), col(), col()
            dma(PRb.slot, PR_t[:, 0:96], ssmA, W=[PRb])

            def V(fn):
                return op(DVE, fn, R=[PRb], W=[PRb])

            def A(fn):
                return op(ACT, fn, R=[PRb], W=[PRb])

            def TT(o, a, b, o_):
                V(lambda: nc.vector.tensor_tensor(out=o, in0=a, in1=b, op=o_))

            def TS(o, a, s1, o1, s2=None, o2=None):
                if o2 is None:
                    V(lambda: nc.vector.tensor_scalar(out=o, in0=a, scalar1=s1, scalar2=None, op0=o1))
                else:
                    V(lambda: nc.vector.tensor_scalar(out=o, in0=a, scalar1=s1, scalar2=s2, op0=o1, op1=o2))

            t1, t2 = col(), col()

            def cmul(orr, oi, ar, ai, br, bi):
                TT(t1, ar, br, ALU.mult)
                TT(t2, ai, bi, ALU.mult)
                TT(orr, t1, t2, ALU.subtract)
                TT(t1, ar, bi, ALU.mult)
                TT(t2, ai, br, ALU.mult)
                TT(oi, t1, t2, ALU.add)

            dt_, lre, mag, ang, cs_, sn_, c2, s2 = col(), col(), col(), col(), col(), col(), col(), col()
            pio2 = col()
            V(lambda: nc.vector.memset(pio2, float(np.pi / 2)))
            A(lambda: nc.scalar.activation(out=dt_, in_=LDT, func=AF.Exp))
            TS(lre, A_re, -1e-4, ALU.min)
            TT(t1, lre, dt_, ALU.mult)
            A(lambda: nc.scalar.activation(out=mag, in_=t1, func=AF.Exp))
            TT(ang, A_im, dt_, ALU.mult)
            ki_t = ph.enter_context(_sb("ki", [128, 32], mybir.dt.int32))
            kf, rr_, mm_ = col(), col(), col()
            TS(kf, ang, float(1.0 / (2 * np.pi)), ALU.mult)
            V(lambda: nc.vector.tensor_copy(out=ki_t[:], in_=kf))
            V(lambda: nc.vector.tensor_copy(out=kf, in_=ki_t[:]))
            V(lambda: nc.vector.scalar_tensor_tensor(out=rr_, in0=kf, scalar=float(-2 * np.pi), in1=ang, op0=ALU.mult, op1=ALU.add))
            pi_c, npi_c = col(), col()
            V(lambda: nc.vector.memset(pi_c, float(np.pi)))
            V(lambda: nc.vector.memset(npi_c, float(-np.pi)))
            TT(mm_, rr_, pi_c, ALU.is_gt)
            V(lambda: nc.vector.scalar_tensor_tensor(out=rr_, in0=mm_, scalar=float(-2 * np.pi), in1=rr_, op0=ALU.mult, op1=ALU.add))
            TT(mm_, rr_, npi_c, ALU.is_lt)
            V(lambda: nc.vector.scalar_tensor_tensor(out=rr_, in0=mm_, scalar=float(2 * np.pi), in1=rr_, op0=ALU.mult, op1=ALU.add))
            TS(s2, rr_, -1.0, ALU.mult)
            TT(c2, rr_, s2, ALU.max)
            A(lambda: nc.scalar.activation(out=sn_, in_=rr_, func=AF.Sin))
            A(lambda: nc.scalar.activation(out=cs_, in_=c2, func=AF.Sin, scale=-1.0, bias=pio2[:, 0:1]))
            lbr, lbi = col(), col()
            TT(lbr, mag, cs_, ALU.mult)
            TT(lbi, mag, sn_, ALU.mult)
            numr, den, rden, fre, fim = col(), col(), col(), col(), col()
            TS(numr, lbr, -1.0, ALU.add)
            TT(t1, lre, lre, ALU.mult)
            TT(t2, A_im, A_im, ALU.mult)
            TT(den, t1, t2, ALU.add)
            V(lambda: nc.vector.reciprocal(out=rden, in_=den))
            TT(t1, numr, lre, ALU.mult)
            TT(t2, lbi, A_im, ALU.mult)
            TT(fre, t1, t2, ALU.add)
            TT(fre, fre, rden, ALU.mult)
            TT(t1, lbi, lre, ALU.mult)
            TT(t2, numr, A_im, ALU.mult)
            TT(fim, t1, t2, ALU.subtract)
            TT(fim, fim, rden, ALU.mult)
            Pre = [col() for _ in range(9)]
            Pim = [col() for _ in range(9)]
            NPim = [col() for _ in range(9)]
            V(lambda: nc.vector.memset(Pre[0], 1.0))
            V(lambda: nc.vector.memset(Pim[0], 0.0))
            for tau in range(8):
                cmul(Pre[tau + 1], Pim[tau + 1], Pre[tau], Pim[tau], lbr, lbi)
            for tau in range(9):
                TS(NPim[tau], Pim[tau], -1.0, ALU.mult)
            Gre = [col() for _ in range(8)]
            Gim = [col() for _ in range(8)]
            for tau in range(8):
                cmul(Gre[tau], Gim[tau], Pre[tau], Pim[tau], fre, fim)
            nlev = 0
            while (1 << nlev) < NCH:
                nlev += 1
            Hre = [col() for _ in range(nlev)]
            Him = [col() for _ in range(nlev)]
            NHim = [col() for _ in range(nlev)]
            V(lambda: nc.vector.tensor_copy(out=Hre[0], in_=Pre[8]))
            V(lambda: nc.vector.tensor_copy(out=Him[0], in_=Pim[8]))
            for k in range(1, nlev):
                cmul(Hre[k], Him[k], Hre[k - 1], Him[k - 1], Hre[k - 1], Him[k - 1])
            for k in range(nlev):
                TS(NHim[k], Him[k], -1.0, ALU.mult)
            KT_t = ph.enter_context(_sb("KT", [128, 2, 32, 8], F32))
            PT_t = ph.enter_context(_sb("PTt", [128, 3, 32, 9], F32))
            for i_ in range(8):
                for (ri, Garr) in ((0, Gre), (1, Gim)):
                    V(lambda: nc.vector.tensor_copy(out=KT_t[:, ri, 0:16, i_], in_=Garr[7 - i_][:, 0:16]))
                    V(lambda: nc.vector.tensor_copy(out=KT_t[:, ri, 16:32, i_], in_=Garr[i_][:, 16:32]))
            for tau_ in range(9):
                for (ri, Parr) in ((0, Pre), (1, Pim), (2, NPim)):
                    V(lambda: nc.vector.tensor_copy(out=PT_t[:, ri, :, tau_], in_=Parr[tau_]))
            ev_pr = V(lambda: nc.vector.tensor_copy(out=t1, in_=t2))
            for E in (ACT, POOL, PE):
                E.wait(ev_pr)
            if dbg:
                prdbg = nc.dram_tensor("prdbg", [128, NPR * 32], F32, kind="ExternalOutput").ap()
                dma(PRb.slot, prdbg, PR_t[:], R=[PRb])

            base_slot_ptr = slot_ptr[0]
            for hf in range(2):
              with ExitStack() as ph_ssm:
                uT_t = ph_ssm.enter_context(_sb(f"uT{hf}", [128, 2, 8, NCH], BF16))
                uTb = Buf(uT_t)
                with ExitStack() as ph:
                    slot_ptr[0] = base_slot_ptr
                    in_proj(ph, [("u", i, i * 128) for i in range(2)], hf * 256, 256, uTb=uTb)
                    barrier()
                slot_ptr[0] = base_slot_ptr
                ph = ph_ssm
                bzr = Ring([Buf(ph.enter_context(_sb(f"bz{i}", [128, 4, 128], F32)), new_slot()) for i in range(2)])
                Wst_r = Ring([Buf(ph.enter_context(_sb(f"Wst{i}", [128, 16, 128], BF16))) for i in range(1)])
                WAb = Buf(ph.enter_context(_sb("WA", [128, 16, 128], F32)))
                T1b = Buf(ph.enter_context(_sb("T1", [128, 16, 128], F32)))
                T2b = Buf(ph.enter_context(_sb("T2", [128, 16, 128], F32)))
                BbZ_t = ph.enter_context(_sb("BbZ", [128, 16, 128], BF16))
                BbZb = Buf(BbZ_t)
                CL_t = ph.enter_context(_sb("CL", [128, 8 * 18, 128], BF16))
                CLb = Buf(CL_t)
                BD_t = ph.enter_context(_sb("BD", [128, 16, 128], BF16))
                BDb = Buf(BD_t)
                XA_t = ph.enter_context(_sb("XA", [128, 2, NCH], F32))
                XB_t = ph.enter_context(_sb("XB", [128, 2, NCH], F32))
                XAb, XBb = Buf(XA_t), Buf(XB_t)
                Xb_t = ph.enter_context(_sb("Xb", [128, 8, 2, NCH], BF16))
                Xbb = Buf(Xb_t)
                zbuf_t = ph.enter_context(_sb("zbuf", [128, LP], BF16))
                zb = Buf(zbuf_t, new_slot())
                ytr = Ring([Buf(ph.enter_context(_sb(f"yt{i}", [128, 512], F32))) for i in range(2)])
                op(POOL, lambda: nc.gpsimd.memset(Xb_t[:].rearrange("p a b c -> p (a b c)"), 0.0), W=[Xbb])
                trr = Ring([banks[0], banks[3]])
                sring = Ring(banks[1:3])
                kring = Ring(banks[3:4])
                accr = Ring(banks[4:7])
                cbs = colblocks(NCH)

                for gcl in range(2):
                    gc = hf * 2 + gcl
                    uv = uT_t[:, gcl, :, :].rearrange("p i c -> p c i")
                    for gpl in range(4):
                        gp = gc * 4 + gpl
                        for d in range(2):
                            cidx = d * 16 + gp
                            slot8 = gpl * 2 + d
                            bzb = bzr.next()
                            dma(bzb.slot, bzb.t[:], bz[cidx].rearrange("p (a c) -> p a c", a=4), W=[bzb])
                            Bre, Bim, Cre, Cim = (bzb.t[:, a, :] for a in range(4))
                            Wst = Wst_r.next()

                            def sc(arr):
                                return arr[:, cidx:cidx + 1]

                            wa = WAb
                            kre = KT_t[:, 0, cidx, :].unsqueeze(2).unsqueeze(3).broadcast_to([128, 8, 2, 128])
                            kim2 = KT_t[:, 1, cidx, :].unsqueeze(2).broadcast_to([128, 8, 128])
                            bpair = bzb.t[:, 0:2, :].unsqueeze(1).broadcast_to([128, 8, 2, 128])
                            bre8 = Bre.unsqueeze(1).broadcast_to([128, 8, 128])
                            bim8 = Bim.unsqueeze(1).broadcast_to([128, 8, 128])
                            T1v = T1b.t[:].rearrange("p (i a) c -> p i a c", a=2)
                            T2v = T2b.t[:].rearrange("p (i a) c -> p i a c", a=2)
                            WAv = wa.t[:].rearrange("p (i a) c -> p i a c", a=2)
                            op(DVE, lambda: nc.vector.tensor_tensor(out=T1v, in0=bpair, in1=kre, op=ALU.mult), R=[bzb], W=[T1b])
                            op(DVE, lambda: nc.vector.tensor_tensor(out=T2v[:, :, 0, :], in0=bim8, in1=kim2, op=ALU.mult), R=[bzb], W=[T2b])
                            op(DVE, lambda: nc.vector.tensor_tensor(out=T2v[:, :, 1, :], in0=bre8, in1=kim2, op=ALU.mult), R=[bzb], W=[T2b])
                            op(DVE, lambda: nc.vector.tensor_tensor(out=WAv[:, :, 0, :], in0=T1v[:, :, 0, :], in1=T2v[:, :, 0, :], op=ALU.subtract), R=[T1b, T2b], W=[wa])
                            op(DVE, lambda: nc.vector.tensor_tensor(out=WAv[:, :, 1, :], in0=T1v[:, :, 1, :], in1=T2v[:, :, 1, :], op=ALU.add), R=[T1b, T2b], W=[wa])
                            i0 = 7 if d == 0 else 0
                            op(DVE, lambda: nc.vector.tensor_copy(out=BbZ_t[:, slot8 * 2:slot8 * 2 + 2, :], in_=wa.t[:, i0 * 2:i0 * 2 + 2, :]), R=[wa], W=[BbZb])
                            for q4 in range(4):
                                tb = trr.next()
                                for m4 in range(4):
                                    idx = q4 * 4 + m4
                                    op(PE, lambda: nc.tensor.transpose(out=tb.t[:, m4 * 128:(m4 + 1) * 128], in_=wa.t[:, idx, :], identity=id_f), R=[wa, cmf], W=[tb], mark=(m4 == 3))
                                op(ACT, lambda: nc.scalar.copy(out=Wst.t[:, q4 * 4:(q4 + 1) * 4, :], in_=tb.t[:].rearrange("p (m c) -> p m c", m=4)), R=[tb], W=[Wst])
                            pre9 = PT_t[:, 0, cidx, :].unsqueeze(2).broadcast_to([128, 9, 128])
                            pim9 = PT_t[:, 1, cidx, :].unsqueeze(2).broadcast_to([128, 9, 128])
                            npim9 = PT_t[:, 2, cidx, :].unsqueeze(2).broadcast_to([128, 9, 128])
                            cre9 = Cre.unsqueeze(1).broadcast_to([128, 9, 128])
                            cim9 = Cim.unsqueeze(1).broadcast_to([128, 9, 128])
                            A1 = T1b.t[:, 0:9, :]
                            A2 = T2b.t[:, 0:9, :]
                            CLv = CL_t[:, slot8 * 18:(slot8 + 1) * 18, :].rearrange("p (t a) c -> p t a c", a=2)
                            op(DVE, lambda: nc.vector.tensor_tensor(out=A1, in0=cre9, in1=pre9, op=ALU.mult), R=[bzb], W=[T1b])
                            op(DVE, lambda: nc.vector.tensor_tensor(out=A2, in0=cim9, in1=pim9, op=ALU.mult), R=[bzb], W=[T2b])
                            op(DVE, lambda: nc.vector.tensor_tensor(out=CLv[:, :, 0, :], in0=A1, in1=A2, op=ALU.subtract), R=[T1b, T2b], W=[CLb])
                            op(DVE, lambda: nc.vector.tensor_tensor(out=A1, in0=cre9, in1=npim9, op=ALU.mult), R=[bzb], W=[T1b])
                            op(DVE, lambda: nc.vector.tensor_tensor(out=A2, in0=cim9, in1=pre9, op=ALU.mult), R=[bzb], W=[T2b])
                            op(DVE, lambda: nc.vector.tensor_tensor(out=CLv[:, :, 1, :], in0=A1, in1=A2, op=ALU.subtract), R=[T1b, T2b], W=[CLb])
                            for part in range(2):
                                for (cb0, n) in cbs:
                                    sbk = sring.next()
                                    mm_group(sbk, [(sbk.t[:, :n], Wst.t[:, i * 2 + part, :], uv[:, cb0:cb0 + n, i]) for i in range(8)], R=[Wst, uTb])
                                    op(ACT, lambda: nc.scalar.copy(out=XA_t[:, part, cb0:cb0 + n], in_=sbk.t[:, :n]), R=[sbk], W=[XAb])
                            src, dst = XAb, XBb
                            for k in range(nlev):
                                s_ = 1 << k
                                a_, b_, nb_ = sc(Hre[k]), sc(Him[k]), sc(NHim[k])
                                X, Y = src.t, dst.t
                                if d == 0:
                                    lo, hi = slice(0, NCH - s_), slice(s_, NCH)
                                    keep = slice(0, s_)
                                else:
                                    lo, hi = slice(s_, NCH), slice(0, NCH - s_)
                                    keep = slice(NCH - s_, NCH)
                                op(ACT, lambda: nc.scalar.copy(out=Y[:, :, keep], in_=X[:, :, keep]), R=[src], W=[dst])
                                op(DVE, lambda: nc.vector.scalar_tensor_tensor(out=Y[:, 0, hi], in0=X[:, 0, lo], scalar=a_, in1=X[:, 0, hi], op0=ALU.mult, op1=ALU.add), R=[src], W=[dst])
                                op(DVE, lambda: nc.vector.scalar_tensor_tensor(out=Y[:, 0, hi], in0=X[:, 1, lo], scalar=nb_, in1=Y[:, 0, hi], op0=ALU.mult, op1=ALU.add), R=[src], W=[dst])
                                op(DVE, lambda: nc.vector.scalar_tensor_tensor(out=Y[:, 1, hi], in0=X[:, 0, lo], scalar=b_, in1=X[:, 1, hi], op0=ALU.mult, op1=ALU.add), R=[src], W=[dst])
                                op(DVE, lambda: nc.vector.scalar_tensor_tensor(out=Y[:, 1, hi], in0=X[:, 1, lo], scalar=a_, in1=Y[:, 1, hi], op0=ALU.mult, op1=ALU.add), R=[src], W=[dst])
                                src, dst = dst, src
                            Z = src.t
                            if d == 0:
                                op(ACT, lambda: nc.scalar.copy(out=Xb_t[:, slot8, :, 1:NCH], in_=Z[:, :, 0:NCH - 1]), R=[src], W=[Xbb])
                            else:
                                op(ACT, lambda: nc.scalar.copy(out=Xb_t[:, slot8, :, 0:NCH - 1], in_=Z[:, :, 1:NCH]), R=[src], W=[Xbb])
                    for d in range(2):
                        for tau in range(8):
                            kb = kring.next()
                            mms = []
                            for gpl in range(4):
                                s8 = gpl * 2 + d
                                for part in range(2):
                                    mms.append((kb.t[:, 0:128], BbZ_t[:, s8 * 2 + part, :], CL_t[:, s8 * 18 + tau * 2 + part, :]))
                            mm_group(kb, mms, R=[BbZb, CLb])
                            op(ACT, lambda: nc.scalar.copy(out=BD_t[:, d * 8 + tau, :], in_=kb.t[:, 0:128]), R=[kb], W=[BDb])
                    zv = zbuf_t[:].rearrange("p (c i) -> p c i", i=8)
                    for j in range(8):
                        for (cb0, n) in cbs:
                            ab = accr.next()
                            mms = []
                            for i in range(0, j + 1):
                                mms.append((ab.t[:, :n], BD_t[:, 0 * 8 + (j - i), :], uv[:, cb0:cb0 + n, i]))
                            for i in range(j, 8):
                                mms.append((ab.t[:, :n], BD_t[:, 1 * 8 + (i - j), :], uv[:, cb0:cb0 + n, i]))
                            for gpl in range(4):
                                for d in range(2):
                                    s8 = gpl * 2 + d
                                    tau = (j + 1) if d == 0 else (8 - j)
                                    for part in range(2):
                                        mms.append((ab.t[:, :n], CL_t[:, s8 * 18 + tau * 2 + part, :], Xb_t[:, s8, part, cb0:cb0 + n]))
                            mm_group(ab, mms, R=[BDb, uTb, CLb, Xbb])
                            yt = ytr.next()
                            op(DVE, lambda: nc.vector.scalar_tensor_tensor(out=yt.t[:, :n], in0=uv[:, cb0:cb0 + n, j], scalar=gcol(GV_SSMD + gc), in1=ab.t[:, :n], op0=ALU.mult, op1=ALU.add), R=[ab, uTb], W=[yt])
                            op(ACT, lambda: nc.scalar.activation(out=zv[:, cb0:cb0 + n, j], in_=yt.t[:, :n], func=AF.Gelu), R=[yt], W=[zb])
                    dma(zb.slot, zT_d[gc * 128:(gc + 1) * 128, :], zbuf_t[:], R=[zb])
                barrier()

        conv_jobs = []
        for (src_, dst_, rows_, cols_, g0_) in [(w_glu, wglub, 512, 512, None), (w_sp, wspb, 512, D, None), (w_ap, wapb, D, D, None),
                                                  (w_out, woutb, D, D, None), (w1, w1b, D, DFF, GV_MLP), (w2, w2b, DFF, D, None)]:
            for k_ in range(rows_ // 128):
                for c0_ in range(0, cols_, 2048):
                    conv_jobs.append((src_, dst_, k_, c0_, min(2048, cols_ - c0_), g0_))
        conv_state = {"i": 0, "stg": None, "ost": None}

        def conv_step(nj):
            for _ in range(nj):
                if conv_state["i"] >= len(conv_jobs):
                    return
                (src_, dst_, k_, c0_, n_, g0_) = conv_jobs[conv_state["i"]]
                conv_state["i"] += 1
                st = conv_state["stg"].next()
                dma(st.slot, st.t[:, :n_], src_[k_ * 128:(k_ + 1) * 128, c0_:c0_ + n_], W=[st])
                ob = conv_state["ost"].next()
                if g0_ is None:
                    op(DVE, lambda: nc.vector.tensor_copy(out=ob.t[:, :n_], in_=st.t[:, :n_]), R=[st], W=[ob])
                else:
                    op(DVE, lambda: nc.vector.tensor_scalar(out=ob.t[:, :n_], in0=st.t[:, :n_], scalar1=gcol(g0_ + k_), scalar2=None, op0=ALU.mult), R=[st], W=[ob])
                dma(ob.slot, dst_[k_ * 128:(k_ + 1) * 128, c0_:c0_ + n_], ob.t[:, :n_], R=[ob])

        with ExitStack() as ph_att:
            kT_t = ph_att.enter_context(_sb("kT", [128, 2, LP], BF16))
            Vx_t = ph_att.enter_context(_sb("Vx", [128, NT, 4, 65], BF16))
            kTb, Vxb = Buf(kT_t), Buf(Vx_t)
            op(POOL, lambda: nc.gpsimd.memset(Vx_t[:].rearrange("p a b c -> p (a b c)"), 1.0), W=[Vxb])
            with ExitStack() as ph:
                slot_ptr[0] = base_slot_ptr
                chunks = [("q", i, i * 128) for i in range(8)] + [("k", i, 1024 + i * 128) for i in range(2)] + [("v", i, 1280 + i * 128) for i in range(2)]
                in_proj(ph, chunks, 512, 1536, kTb=kTb, Vxb=Vxb)
                barrier()
            slot_ptr[0] = base_slot_ptr
            ph = ph_att
            conv_state["stg"] = Ring([Buf(ph.enter_context(_sb(f"cst{i}", [128, 2048], F32)), new_slot()) for i in range(3)])
            conv_state["ost"] = Ring([Buf(ph.enter_context(_sb(f"cso{i}", [128, 2048], BF16)), new_slot()) for i in range(3)])
            Qr = Ring([Buf(ph.enter_context(_sb(f"Q{i}", [128, LP], BF16)), new_slot()) for i in range(2)])
            PTr = Ring([Buf(ph.enter_context(_sb(f"PT{i}", [128, 2, 512], BF16))) for i in range(3)])
            recb = Buf(ph.enter_context(_sb("rec", [128, 512], F32)))
            yor = Ring([Buf(ph.enter_context(_sb(f"yo{i}", [128, 512], BF16)), new_slot()) for i in range(2)])
            Sr = Ring([Buf(PSA[:, 0:2, :]), Buf(PSA[:, 2:4, :])])
            Or = Ring([banks[4], banks[6]])
            DENb = banks[5]
            pairs = [(h, h + 4) for h in (0, 1, 2, 3)] + [(h, h + 4) for h in (8, 9, 10, 11)]
            for (ha, hb) in pairs:
                ga, gb = ha // 4, hb // 4
                kc = ga // 2
                Qb = Qr.next()
                dma(Qb.slot, Qb.t[0:64, :], qT_d[ha * 64:(ha + 1) * 64, :], W=[Qb])
                dma(Qb.slot, Qb.t[64:128, :], qT_d[hb * 64:(hb + 1) * 64, :], upd=[Qb])
                for (q0, NQ) in colblocks(S):
                    qc0 = 128 + q0
                    Ob = Or.next()
                    pend = None
                    first_pv = True

                    def do_pv(pend, last):
                        nonlocal first_pv
                        PTb, kt = pend
                        f_ = first_pv
                        first_pv = False
                        for b in (PTb, Vxb, cmb):
                            PE.wait(b.wr)
                        if f_:
                            for bk in (Ob, DENb):
                                PE.wait(bk.wr)
                                PE.wait(list(bk.rd.values()))
                        nc.tensor.matmul(Ob.t[0:64, :NQ], lhsT=Vx_t[:, kt, ga, 0:64], rhs=PTb.t[:, 0, :NQ], start=f_, stop=last)
                        nc.tensor.matmul(Ob.t[64:128, :NQ], lhsT=Vx_t[:, kt, gb, 0:64], rhs=PTb.t[:, 1, :NQ], start=f_, stop=last)
                        nc.tensor.matmul(DENb.t[0:64, :NQ], lhsT=ones_b[:, 0:64], rhs=PTb.t[:, 0, :NQ], start=f_, stop=last)
                        inst = nc.tensor.matmul(DENb.t[64:128, :NQ], lhsT=ones_b[:, 0:64], rhs=PTb.t[:, 1, :NQ], start=f_, stop=last)
                        ev = PE.mark(inst)
                        PTb.rd[PE.name] = ev
                        if last:
                            for bk in (Ob, DENb):
                                bk.wr = ev
                                bk.rd = {}

                    for kt in range(NT):
                        Sb = Sr.next()
                        mms = [(Sb.t[:, 0, :NQ], kT_t[0:64, kc, kt * 128:(kt + 1) * 128], Qb.t[0:64, qc0:qc0 + NQ], True, True),
                               (Sb.t[:, 1, :NQ], kT_t[64:128, kc, kt * 128:(kt + 1) * 128], Qb.t[64:128, qc0:qc0 + NQ], True, True)]
                        mm_group(Sb, mms, R=[kTb, Qb])
                        if pend is not None:
                            do_pv(pend, False)
                        PTb = PTr.next()
                        if kt == 0:
                            op(ACT, lambda: nc.scalar.activation(out=PTb.t[:, :, :NQ], in_=Sb.t[:, :, :NQ], func=AF.Exp, scale=0.125, bias=gcol(GV_MASK)), R=[Sb], W=[PTb])
                        else:
                            op(ACT, lambda: nc.scalar.activation(out=PTb.t[:, :, :NQ], in_=Sb.t[:, :, :NQ], func=AF.Exp, scale=0.125), R=[Sb], W=[PTb])
                        pend = (PTb, kt)
                    do_pv(pend, True)
                    op(DVE, lambda: nc.vector.reciprocal(out=recb.t[:, :NQ], in_=DENb.t[:, :NQ]), R=[DENb], W=[recb])
                    yo = yor.next()
                    op(DVE, lambda: nc.vector.tensor_tensor(out=yo.t[:, :NQ], in0=Ob.t[:, :NQ], in1=recb.t[:, :NQ], op=ALU.mult), R=[Ob, recb], W=[yo])
                    dma(yo.slot, yaT_d[ha * 64:(ha + 1) * 64, q0:q0 + NQ], yo.t[0:64, :NQ], R=[yo])
                    dma(yo.slot, yaT_d[hb * 64:(hb + 1) * 64, q0:q0 + NQ], yo.t[64:128, :NQ], R=[yo])
                    conv_step(1)
            conv_step(len(conv_jobs))
            barrier()

        with ExitStack() as ph:
            slot_ptr[0] = base_slot_ptr

            def wload(name, src, kchunks, cols, c0=0):
                t = ph.enter_context(_sb(name, [128, kchunks, cols], BF16))
                b = Buf(t, new_slot())
                dma(b.slot, t[:], kview(src)[:, :, c0:c0 + cols], W=[b])
                return b
            Wg = wload("Wg", winb, 8, 2048, 2048)
            Wgl = wload("Wgl", wglub, 4, 512)
            Wsp = wload("Wsp", wspb, 4, D)
            Wap = wload("Wap", wapb, 8, D)
            Wo = wload("Wo", woutb, 8, D)
            xring = Ring([Buf(ph.enter_context(_sb(f"xt{i}", [128, 8, 512], F32)), new_slot()) for i in range(2)])
            sqb_ = Buf(ph.enter_context(_sb("sq3", [128, 8, 512], BF16)))
            xbb_ = Buf(ph.enter_context(_sb("xb3", [128, 8, 512], BF16)))
            ztr = Ring([Buf(ph.enter_context(_sb(f"zt{i}", [128, 4, 512], BF16)), new_slot()) for i in range(2)])
            yar = Ring([Buf(ph.enter_context(_sb(f"ya{i}", [128, 8, 512], BF16)), new_slot()) for i in range(2)])
            ysb = Buf(ph.enter_context(_sb("ys", [128, 4, 512], BF16)))
            mgb = Buf(ph.enter_context(_sb("mg", [128, 8, 512], BF16)))
            f32r = Ring([Buf(ph.enter_context(_sb(f"f3r{i}", [128, 512], F32))) for i in range(10)])
            rstdb = Buf(ph.enter_context(_sb("rstd3", [128, 512], F32)))
            h1r = Ring([Buf(ph.enter_context(_sb(f"h1s{i}", [128, 512], F32)), new_slot()) for i in range(3)])
            pr = Ring(banks)
            xTv = kview(xT)
            zTv = kview(zT_d)
            yaTv = kview(yaT_d)
            N = 512
            for (q0, _) in colblocks(S):
                c0 = 128 + q0
                xt = xring.next()
                dma(xt.slot, xt.t[:], xTv[:, :, c0:c0 + N], W=[xt])
                zt = ztr.next()
                dma(zt.slot, zt.t[:], zTv[:, :, c0:c0 + N], W=[zt])
                ya = yar.next()
                dma(ya.slot, ya.t[:], yaTv[:, :, q0:q0 + N], W=[ya])
                op(ACT, lambda: nc.scalar.activation(out=sqb_.t[:], in_=xt.t[:], func=AF.Square), R=[xt], W=[sqb_])
                op(DVE, lambda: nc.vector.tensor_copy(out=xbb_.t[:], in_=xt.t[:]), R=[xt], W=[xbb_])
                ssb = pr.next()
                mm_group(ssb, [(ssb.t[:], ones_b, sqb_.t[:, k, :]) for k in range(8)], R=[sqb_, cmb])
                rt = f32r.next()
                op(ACT, lambda: nc.scalar.activation(out=rt.t[:], in_=ssb.t[:], func=AF.Ln, scale=1.0 / D, bias=eps_c), R=[ssb], W=[rt])
                op(ACT, lambda: nc.scalar.activation(out=rstdb.t[:], in_=rt.t[:], func=AF.Exp, scale=-0.5), R=[rt], W=[rstdb])
                for oc in range(4):
                    ps = pr.next()
                    mm_group(ps, [(ps.t[:], Wgl.t[:, k, oc * 128:(oc + 1) * 128], zt.t[:, k, :]) for k in range(4)], R=[Wgl, zt])
                    sg = f32r.next()
                    op(ACT, lambda: nc.scalar.activation(out=sg.t[:], in_=ps.t[:], func=AF.Sigmoid, bias=gcol(GV_BGLU + oc)), R=[ps], W=[sg])
                    op(DVE, lambda: nc.vector.tensor_tensor(out=ysb.t[:, oc, :], in0=zt.t[:, oc, :], in1=sg.t[:], op=ALU.mult), R=[zt, sg], W=[ysb])
                for oc in range(8):
                    sgs = []
                    for gi in range(2):
                        ps = pr.next()
                        wc = gi * 1024 + oc * 128
                        mm_group(ps, [(ps.t[:], Wg.t[:, k, wc:wc + 128], xbb_.t[:, k, :]) for k in range(8)], R=[Wg, xbb_])
                        tg = f32r.next()
                        op(DVE, lambda: nc.vector.tensor_tensor(out=tg.t[:], in0=ps.t[:], in1=rstdb.t[:], op=ALU.mult), R=[ps, rstdb], W=[tg])
                        sg = f32r.next()
                        op(ACT, lambda: nc.scalar.activation(out=sg.t[:], in_=tg.t[:], func=AF.Sigmoid), R=[tg], W=[sg])
                        sgs.append(sg)
                    ps = pr.next()
                    mm_group(ps, [(ps.t[:], Wsp.t[:, k, oc * 128:(oc + 1) * 128], ysb.t[:, k, :]) for k in range(4)], R=[Wsp, ysb])
                    m1 = f32r.next()
                    op(DVE, lambda: nc.vector.tensor_tensor(out=m1.t[:], in0=ps.t[:], in1=sgs[0].t[:], op=ALU.mult), R=[ps, sgs[0]], W=[m1])
                    ps2 = pr.next()
                    mm_group(ps2, [(ps2.t[:], Wap.t[:, k, oc * 128:(oc + 1) * 128], ya.t[:, k, :]) for k in range(8)], R=[Wap, ya])
                    m2 = f32r.next()
                    op(DVE, lambda: nc.vector.tensor_tensor(out=m2.t[:], in0=ps2.t[:], in1=sgs[1].t[:], op=ALU.mult), R=[ps2, sgs[1]], W=[m2])
                    op(DVE, lambda: nc.vector.tensor_tensor(out=mgb.t[:, oc, :], in0=m1.t[:], in1=m2.t[:], op=ALU.add), R=[m1, m2], W=[mgb])
                for oc in range(8):
                    ps = pr.next()
                    mm_group(ps, [(ps.t[:], Wo.t[:, k, oc * 128:(oc + 1) * 128], mgb.t[:, k, :]) for k in range(8)], R=[Wo, mgb])
                    h1 = h1r.next()
                    op(DVE, lambda: nc.vector.tensor_tensor(out=h1.t[:], in0=ps.t[:], in1=xt.t[:, oc, :], op=ALU.add), R=[ps, xt], W=[h1])
                    dma(h1.slot, h1T_d[oc * 128:(oc + 1) * 128, q0:q0 + N], h1.t[:], R=[h1])
            barrier()

        with ExitStack() as ph:
            slot_ptr[0] = base_slot_ptr
            W1_t = ph.enter_context(_sb("W1", [128, 8, DFF], BF16))
            W2_t = ph.enter_context(_sb("W2", [128, 32, D], BF16))
            W1b_, W2b_ = Buf(W1_t, new_slot()), Buf(W2_t, new_slot())
            dma(W1b_.slot, W1_t[:], kview(w1b), W=[W1b_])
            dma(W2b_.slot, W2_t[:], kview(w2b), W=[W2b_])
            N = 256
            hr = Ring([Buf(ph.enter_context(_sb(f"h{i}", [128, 8, N], F32)), new_slot()) for i in range(2)])
            sqb_ = Buf(ph.enter_context(_sb("sq4", [128, 8, N], BF16)))
            hbb = Buf(ph.enter_context(_sb("hb4", [128, 8, N], BF16)))
            Ab = Buf(ph.enter_context(_sb("A4", [128, 32, N], BF16)))
            h2b = Buf(ph.enter_context(_sb("h2", [128, 8, N], F32)))
            osr = Ring([Buf(ph.enter_context(_sb(f"os{i}", [128, 8, N], F32)), new_slot()) for i in range(1)])
            f32r = Ring([Buf(ph.enter_context(_sb(f"f4r{i}", [128, N], F32))) for i in range(6)])
            rstdb = Buf(ph.enter_context(_sb("rstd4", [128, N], F32)))
            rstd5 = Buf(ph.enter_context(_sb("rstd5", [128, N], F32)))
            pr = Ring(banks)
            h1v = kview(h1T_d)
            outv = kview(outT)
            for (q0, _) in colblocks(S, N):
                hb = hr.next()
                dma(hb.slot, hb.t[:], h1v[:, :, q0:q0 + N], W=[hb])
                op(ACT, lambda: nc.scalar.activation(out=sqb_.t[:], in_=hb.t[:], func=AF.Square), R=[hb], W=[sqb_])
                op(DVE, lambda: nc.vector.tensor_copy(out=hbb.t[:], in_=hb.t[:]), R=[hb], W=[hbb])
                ssb = pr.next()
                mm_group(ssb, [(ssb.t[:, :N], ones_b, sqb_.t[:, k, :]) for k in range(8)], R=[sqb_, cmb])
                rt = f32r.next()
                op(ACT, lambda: nc.scalar.activation(out=rt.t[:], in_=ssb.t[:, :N], func=AF.Ln, scale=1.0 / D, bias=eps_c), R=[ssb], W=[rt])
                op(ACT, lambda: nc.scalar.activation(out=rstdb.t[:], in_=rt.t[:], func=AF.Exp, scale=-0.5), R=[rt], W=[rstdb])
                for f in range(32):
                    ps = pr.next()
                    mm_group(ps, [(ps.t[:, :N], W1_t[:, k, f * 128:(f + 1) * 128], hbb.t[:, k, :]) for k in range(8)], R=[W1b_, hbb])
                    tr_ = f32r.next()
                    op(DVE, lambda: nc.vector.scalar_tensor_tensor(out=tr_.t[:], in0=ps.t[:, :N], scalar=0.0, in1=rstdb.t[:], op0=ALU.max, op1=ALU.mult), R=[ps, rstdb], W=[tr_])
                    op(ACT, lambda: nc.scalar.activation(out=Ab.t[:, f, :], in_=tr_.t[:], func=AF.Square), R=[tr_], W=[Ab])
                for oc in range(8):
                    ps = pr.next()
                    mm_group(ps, [(ps.t[:, :N], W2_t[:, f, oc * 128:(oc + 1) * 128], Ab.t[:, f, :]) for f in range(32)], R=[W2b_, Ab])
                    op(DVE, lambda: nc.vector.tensor_tensor(out=h2b.t[:, oc, :], in0=ps.t[:, :N], in1=hb.t[:, oc, :], op=ALU.add), R=[ps, hb], W=[h2b])
                op(ACT, lambda: nc.scalar.activation(out=sqb_.t[:], in_=h2b.t[:], func=AF.Square), R=[h2b], W=[sqb_])
                ssb = pr.next()
                mm_group(ssb, [(ssb.t[:, :N], ones_b, sqb_.t[:, k, :]) for k in range(8)], R=[sqb_, cmb])
                rt = f32r.next()
                op(ACT, lambda: nc.scalar.activation(out=rt.t[:], in_=ssb.t[:, :N], func=AF.Ln, scale=1.0 / D, bias=eps_c), R=[ssb], W=[rt])
                op(ACT, lambda: nc.scalar.activation(out=rstd5.t[:], in_=rt.t[:], func=AF.Exp, scale=-0.5), R=[rt], W=[rstd5])
                os_ = osr.next()
                for oc in range(8):
                    op(DVE, lambda: nc.vector.scalar_tensor_tensor(out=os_.t[:, oc, :], in0=h2b.t[:, oc, :], scalar=gcol(GV_FIN + oc), in1=rstd5.t[:], op0=ALU.mult, op1=ALU.mult), R=[h2b, rstd5], W=[os_])
                dma(os_.slot, outv[:, :, q0:q0 + N], os_.t[:], R=[os_])
            barrier()
    return nc


def _const_tables(S):
    LP = S + 128
    cm = np.zeros((128, 512), np.float32)
    cm[:, 0:128] = 1.0
    cm[0:64, 128:192] = 1.0
    cm[64:128, 192:256] = 1.0
    for i in range(64):
        cm[2 * i + 1, 256 + 2 * i] = -1.0
        cm[2 * i, 256 + 2 * i + 1] = 1.0
    cm[:, 384:512] = np.eye(128, dtype=np.float32)
    rows = S // 64
    row_id = np.repeat(np.arange(rows, dtype=np.float32), 64)
    col_id = np.tile(np.arange(64, dtype=np.float32), rows)
    inv_freq = (np.float32(10000.0) ** (-np.arange(16, dtype=np.float32) / np.float32(16))).astype(np.float32)
    ang = np.concatenate([row_id[:, None] * inv_freq, col_id[:, None] * inv_freq], axis=-1).astype(np.float32)
    ang = np.concatenate([np.zeros((128, 32), np.float32), ang], axis=0)
    cos = np.cos(ang).astype(np.float32)
    sin = np.sin(ang).astype(np.float32)
    pair = (np.arange(128) % 64) // 2
    C = np.ascontiguousarray(cos[:, pair].T)
    Sn = np.ascontiguousarray(sin[:, pair].T)
    return cm, C, Sn


def _prep_shared(inp, S):
    f = lambda a: np.ascontiguousarray(np.asarray(a, dtype=np.float32))
    cm, C, Sn = _const_tables(S)
    gv = np.zeros((128, NG), np.float32)
    gv[:, GV_MIX:GV_MIX + 8] = f(inp["norm_mix_g"])[0].reshape(8, 128).T
    gv[:, GV_MLP:GV_MLP + 8] = f(inp["norm_mlp_g"])[0].reshape(8, 128).T
    gv[:, GV_FIN:GV_FIN + 8] = f(inp["norm_final_g"]).reshape(8, 128).T
    gv[:, GV_QG] = np.tile(f(inp["q_norm_g"])[0], 2)
    gv[:, GV_KG] = np.tile(f(inp["k_norm_g"])[0], 2)
    gv[:, GV_BGLU:GV_BGLU + 4] = f(inp["b_glu"])[0].reshape(4, 128).T
    gv[:, GV_SSMD:GV_SSMD + 4] = f(inp["ssm_d"])[0].reshape(4, 128).T
    gv[0:112, GV_MASK] = -30000.0
    a_re, a_im, ldt = f(inp["ssm_a_re"])[0], f(inp["ssm_a_im"])[0], f(inp["ssm_log_dt"])[0]
    b_re, b_im = f(inp["ssm_b_re"])[0], f(inp["ssm_b_im"])[0]
    c_re, c_im = f(inp["ssm_c_re"])[0], f(inp["ssm_c_im"])[0]
    ssmA = np.zeros((128, 96), np.float32)
    bz = np.zeros((32, 128, 4, 128), np.float32)
    for d in range(2):
        for gp in range(16):
            ci = d * 16 + gp
            for g2 in range(2):
                g = 2 * gp + g2
                gl = g % 8
                ps = slice(g2 * 64, g2 * 64 + 64)
                ssmA[ps, ci] = a_re[d, g]
                ssmA[ps, 32 + ci] = a_im[d, g]
                ssmA[ps, 64 + ci] = ldt[d, g]
                bz[ci, ps, 0, gl * 16:(gl + 1) * 16] = b_re[d, g]
                bz[ci, ps, 1, gl * 16:(gl + 1) * 16] = b_im[d, g]
                bz[ci, ps, 2, gl * 16:(gl + 1) * 16] = c_re[d, g].T
                bz[ci, ps, 3, gl * 16:(gl + 1) * 16] = c_im[d, g].T
    shared = {
        "w_in": f(inp["w_in"])[0], "w_glu": f(inp["w_glu"])[0], "w_sp": f(inp["w_ssm_proj"])[0],
        "w_ap": f(inp["w_attn_proj"])[0], "w_out": f(inp["w_out"])[0], "w1": f(inp["w_mlp_in"])[0],
        "w2": f(inp["w_mlp_out"])[0], "gv": gv, "cmat": cm, "ropeC": C, "ropeS": Sn, "ssmA": ssmA,
        "bz": bz.reshape(32, 128, 512),
    }
    return shared


def _make_xT(xb, meta):
    S = xb.shape[0]
    full = np.concatenate([np.zeros((112, D), np.float32), np.asarray(meta, np.float32), np.asarray(xb, np.float32)], axis=0)
    return np.ascontiguousarray(full.T)


_NC_CACHE = {}


def kernel(**inputs):
    x = np.asarray(inputs["x"], dtype=np.float32)
    B, S, _ = x.shape
    shared = _prep_shared(inputs, S)
    if S not in _NC_CACHE:
        _NC_CACHE[S] = build(S)
    nc = _NC_CACHE[S]
    in_maps = []
    for b in range(B):
        m = dict(shared)
        m["xT"] = _make_xT(x[b], inputs["meta_tokens"])
        in_maps.append(m)
    res = run_bass_kernel_spmd(nc, in_maps, core_ids=list(range(B)))
    out = np.stack([np.ascontiguousarray(r["outT"].T) for r in res.results], axis=0)
    return out.astype(np.float32)
```

```python
import numpy as np
from contextlib import ExitStack
import concourse.bass as bass
import concourse.mybir as mybir
from concourse.bass_utils import run_bass_kernel_spmd

F32 = mybir.dt.float32
BF16 = mybir.dt.bfloat16
AF = mybir.ActivationFunctionType
ALU = mybir.AluOpType

D = 1024
DFF = 4096
EPS = 1e-6
GV_MIX, GV_MLP, GV_FIN, GV_QG, GV_KG, GV_BGLU, GV_SSMD, GV_MASK, NG = 0, 8, 16, 24, 25, 26, 30, 34, 35


class Ev:
    __slots__ = ("sem", "val", "key")

    def __init__(s, sem, val, key):
        s.sem, s.val, s.key = sem, val, key


class Eng:
    def __init__(s, name, h, sem):
        s.name, s.h, s.sem = name, h, sem
        s.cnt = 0
        s.seen = {}
        s.last = None
        s.selfsync = True

    def wait(s, evs):
        if evs is None:
            return
        if isinstance(evs, Ev):
            evs = [evs]
        for ev in evs:
            if ev is None:
                continue
            if isinstance(ev, (list, tuple)):
                s.wait(ev)
                continue
            if ev.key == s.name and (s.name == "pe" or not s.selfsync):
                continue
            if s.seen.get(ev.key, 0) >= ev.val:
                continue
            s.h.wait_ge(ev.sem, ev.val)
            s.seen[ev.key] = ev.val

    def mark(s, inst):
        s.cnt += 1
        inst.then_inc(s.sem, 1)
        s.last = Ev(s.sem, s.cnt, s.name)
        return s.last


class Slot:
    def __init__(s, sem, key):
        s.sem, s.key = sem, key
        s.cnt = 0
        s.last = None


class Buf:
    def __init__(s, t, slot=None):
        s.t = t
        s.slot = slot
        s.wr = None
        s.rd = {}


class Ring:
    def __init__(s, bufs):
        s.bufs = bufs
        s.i = 0

    def next(s):
        b = s.bufs[s.i % len(s.bufs)]
        s.i += 1
        return b


def build(S, dbg=False):
    LP = S + 128
    NT = LP // 128
    NCH = LP // 8
    nc = bass.Bass("TRN2", target_bir_lowering=False)

    def din(name, shape, dt=F32):
        return nc.dram_tensor(name, shape, dt, kind="ExternalInput").ap()

    def dscr(name, shape, dt):
        return nc.dram_tensor(name, shape, dt, kind=("ExternalOutput" if dbg else "Internal")).ap()

    xT = din("xT", [D, LP])
    w_in = din("w_in", [D, 4096])
    w_glu = din("w_glu", [512, 512])
    w_sp = din("w_sp", [512, D])
    w_ap = din("w_ap", [D, D])
    w_out = din("w_out", [D, D])
    w1 = din("w1", [D, DFF])
    w2 = din("w2", [DFF, D])
    gv_d = din("gv", [128, NG])
    cmat_d = din("cmat", [128, 4 * 128])
    ropeC = din("ropeC", [128, LP])
    ropeS = din("ropeS", [128, LP])
    ssmA = din("ssmA", [128, 96])
    bz = din("bz", [32, 128, 512])
    outT = nc.dram_tensor("outT", [D, S], F32, kind="ExternalOutput").ap()

    winb = dscr("winb", [D, 4096], BF16)
    wglub = dscr("wglub", [512, 512], BF16)
    wspb = dscr("wspb", [512, D], BF16)
    wapb = dscr("wapb", [D, D], BF16)
    woutb = dscr("woutb", [D, D], BF16)
    w1b = dscr("w1b", [D, DFF], BF16)
    w2b = dscr("w2b", [DFF, D], BF16)
    qT_d = dscr("qT_d", [D, LP], BF16)
    zT_d = dscr("zT_d", [512, LP], BF16)
    yaT_d = dscr("yaT_d", [D, S], BF16)
    h1T_d = dscr("h1T_d", [D, S], F32)

    _uid = [0]

    def _sb(name, shape, dt):
        _uid[0] += 1
        return nc.sbuf_tensor(f"{name}_{_uid[0]}", shape, dt)

    def kview(ap):
        return ap.rearrange("(k p) l -> p k l", p=128)

    top = ExitStack()
    with top:
        sems = [top.enter_context(nc.semaphore(f"sem{i}")) for i in range(64)]
        PE = Eng("pe", nc.tensor, sems[0])
        ACT = Eng("act", nc.scalar, sems[1])
        DVE = Eng("dve", nc.vector, sems[2])
        POOL = Eng("pool", nc.gpsimd, sems[3])
        SP = Eng("sp", nc.sync, None)
        engines = [PE, ACT, DVE, POOL]
        slots = [Slot(sems[4 + i], f"dma{i}") for i in range(60)]
        slot_ptr = [0]

        def new_slot():
            s = slots[slot_ptr[0]]
            slot_ptr[0] += 1
            return s

        def op(E, fn, R=(), W=(), mark=True):
            for b in R:
                E.wait(b.wr)
            for b in W:
                E.wait(b.wr)
                E.wait([e for k_, e in b.rd.items() if k_ != E.name])
            inst = fn()
            if mark:
                ev = E.mark(inst)
                for b in R:
                    b.rd[E.name] = ev
                for b in W:
                    b.wr = ev
                    b.rd = {}
                return ev
            return None

        def dma(slot, out, in_, R=(), W=(), upd=()):
            for b in R:
                SP.wait(b.wr)
            for b in W:
                SP.wait(b.wr)
                SP.wait(list(b.rd.values()))
            inst = nc.sync.dma_start(out=out, in_=in_)
            slot.cnt += 16
            inst.then_inc(slot.sem, 16)
            ev = Ev(slot.sem, slot.cnt, slot.key)
            slot.last = ev
            for b in R:
                b.rd[slot.key] = ev
            for b in list(W) + list(upd):
                b.wr = ev
                b.rd = {}
            return ev

        def mm_group(bank, mms, R=()):
            for b in R:
                PE.wait(b.wr)
            PE.wait(bank.wr)
            PE.wait(list(bank.rd.values()))
            n = len(mms)
            inst = None
            for i, m in enumerate(mms):
                if len(m) == 3:
                    st, sp_ = (i == 0), (i == n - 1)
                else:
                    st, sp_ = m[3], m[4]
                inst = nc.tensor.matmul(m[0], lhsT=m[1], rhs=m[2], start=st, stop=sp_)
            ev = PE.mark(inst)
            for b in R:
                b.rd[PE.name] = ev
            bank.wr = ev
            bank.rd = {}
            return ev

        def barrier():
            evs = [E.last for E in engines if E.last is not None]
            evs += [s.last for s in slots if s.last is not None]
            for E in engines + [SP]:
                E.wait(evs)

        def colblocks(n, w=512):
            return [(c, min(w, n - c)) for c in range(0, n, w)]

        gv_t = top.enter_context(_sb("gv", [128, NG], F32))
        cm_f = top.enter_context(_sb("cm_f", [128, 512], F32))
        cm_b = top.enter_context(_sb("cm_b", [128, 512], BF16))
        PSA = top.enter_context(nc.psum_tensor("psa", [128, 7, 512], F32))
        PSB = top.enter_context(nc.psum_tensor("psb", [128, 1024], BF16))
        gv = Buf(gv_t, new_slot())
        cmf = Buf(cm_f, new_slot())
        cmb = Buf(cm_b)
        dma(gv.slot, gv_t[:], gv_d, W=[gv])
        dma(cmf.slot, cm_f[:], cmat_d, W=[cmf])
        op(DVE, lambda: nc.vector.tensor_copy(out=cm_b[:], in_=cm_f[:]), R=[cmf], W=[cmb])
        ones_b = cm_b[:, 0:128]
        blk_b = cm_b[:, 128:256]
        rot_b = cm_b[:, 256:384]
        id_b = cm_b[:, 384:512]
        id_f = cm_f[:, 384:512]
        ones_f = cm_f[:, 0:128]
        DVE.wait(gv.wr)
        ACT.wait(gv.wr)
        POOL.wait(gv.wr)
        banks = [Buf(PSA[:, i, :]) for i in range(7)]
        bankB = Buf(PSB)
        base_slot_ptr = slot_ptr[0]

        def gcol(c):
            return gv_t[:, c:c + 1]

        with ExitStack() as ph:
            slot_ptr[0] = base_slot_ptr
            stg = Ring([Buf(ph.enter_context(_sb(f"wst{i}", [128, 2048], F32)), new_slot()) for i in range(3)])
            ost = Ring([Buf(ph.enter_context(_sb(f"wso{i}", [128, 2048], BF16)), new_slot()) for i in range(3)])
            rr = [0]

            def convert(src, dst, rows, cols, gain0=None):
                for k in range(rows // 128):
                    for c0 in range(0, cols, 2048):
                        n = min(2048, cols - c0)
                        st = stg.next()
                        dma(st.slot, st.t[:, :n], src[k * 128:(k + 1) * 128, c0:c0 + n], W=[st])
                        ob = ost.next()
                        which = rr[0] % 3
                        rr[0] += 1
                        if gain0 is None:
                            if which == 0:
                                op(DVE, lambda: nc.vector.tensor_copy(out=ob.t[:, :n], in_=st.t[:, :n]), R=[st], W=[ob])
                            elif which == 1:
                                op(POOL, lambda: nc.gpsimd.tensor_copy(out=ob.t[:, :n], in_=st.t[:, :n]), R=[st], W=[ob])
                            else:
                                op(ACT, lambda: nc.scalar.copy(out=ob.t[:, :n], in_=st.t[:, :n]), R=[st], W=[ob])
                        else:
                            g = gcol(gain0 + k)
                            if which == 0:
                                op(DVE, lambda: nc.vector.tensor_scalar(out=ob.t[:, :n], in0=st.t[:, :n], scalar1=g, scalar2=None, op0=ALU.mult), R=[st], W=[ob])
                            elif which == 1:
                                op(POOL, lambda: nc.gpsimd.tensor_scalar(out=ob.t[:, :n], in0=st.t[:, :n], scalar1=g, scalar2=None, op0=ALU.mult), R=[st], W=[ob])
                            else:
                                op(ACT, lambda: nc.scalar.activation(out=ob.t[:, :n], in_=st.t[:, :n], func=AF.Copy, scale=g), R=[st], W=[ob])
                        dma(ob.slot, dst[k * 128:(k + 1) * 128, c0:c0 + n], ob.t[:, :n], R=[ob])

            convert(w_in, winb, D, 4096, GV_MIX)
            barrier()

        def in_proj(ph, chunks, wcol0, wcols, uTb=None, kTb=None, Vxb=None):
            W_t = ph.enter_context(_sb("Wp", [128, 8, wcols], BF16))
            Wb = Buf(W_t, new_slot())
            dma(Wb.slot, W_t[:], kview(winb)[:, :, wcol0:wcol0 + wcols], W=[Wb])
            xring = Ring([Buf(ph.enter_context(_sb(f"xt{i}", [128, 8, 512], F32)), new_slot()) for i in range(2)])
            sqr = Ring([Buf(ph.enter_context(_sb(f"sq{i}", [128, 8, 512], BF16))) for i in range(1)])
            xbr = Ring([Buf(ph.enter_context(_sb(f"xb{i}", [128, 8, 512], BF16))) for i in range(2)])
            need_rope = any(c[0] in "qk" for c in chunks)
            if need_rope:
                csr = Ring([Buf(ph.enter_context(_sb(f"cs{i}", [128, 2, 512], F32)), new_slot()) for i in range(2)])
                qor = Ring([Buf(ph.enter_context(_sb(f"qo{i}", [128, 512], BF16)), new_slot()) for i in range(3)])
            f32r = Ring([Buf(ph.enter_context(_sb(f"f32r{i}", [128, 512], F32))) for i in range(12)])
            bfr = Ring([Buf(ph.enter_context(_sb(f"bfr{i}", [128, 512], BF16))) for i in range(6)])
            rstdr = Ring([Buf(ph.enter_context(_sb(f"rstd{i}", [128, 512], F32))) for i in range(2)])
            mainr = Ring(banks[0:3])
            ssb = banks[3]
            hsr = Ring(banks[4:5])
            rotr = Ring(banks[5:7])
            trb = bankB
            xTv = kview(xT)
            tiles_ = colblocks(LP)

            def prologue(ti):
                (c0, N) = tiles_[ti]
                xt = xring.next()
                dma(xt.slot, xt.t[:, :, :N], xTv[:, :, c0:c0 + N], W=[xt])
                cs = None
                if need_rope:
                    cs = csr.next()
                    dma(cs.slot, cs.t[:, 0, :N], ropeC[:, c0:c0 + N], W=[cs])
                    dma(cs.slot, cs.t[:, 1, :N], ropeS[:, c0:c0 + N], upd=[cs])
                sq = sqr.next()
                op(ACT, lambda: nc.scalar.activation(out=sq.t[:, :, :N], in_=xt.t[:, :, :N], func=AF.Square), R=[xt], W=[sq])
                xb = xbr.next()
                op(DVE, lambda: nc.vector.tensor_copy(out=xb.t[:, :, :N], in_=xt.t[:, :, :N]), R=[xt], W=[xb])
                mm_group(ssb, [(ssb.t[:, :N], ones_b, sq.t[:, k, :N]) for k in range(8)], R=[sq, cmb])
                rt = f32r.next()
                op(ACT, lambda: nc.scalar.activation(out=rt.t[:, :N], in_=ssb.t[:, :N], func=AF.Ln, scale=1.0 / D, bias=eps_c), R=[ssb], W=[rt])
                rstd = rstdr.next()
                op(ACT, lambda: nc.scalar.activation(out=rstd.t[:, :N], in_=rt.t[:, :N], func=AF.Exp, scale=-0.5), R=[rt], W=[rstd])
                return dict(c0=c0, N=N, cs=cs, xb=xb, rstd=rstd)

            def stageA(P, ch, stt):
                (kind, idx, wc) = ch
                c0, N, xb, rstd = P["c0"], P["N"], P["xb"], P["rstd"]
                ps = mainr.next()
                mm_group(ps, [(ps.t[:, :N], W_t[:, k, wc:wc + 128], xb.t[:, k, :N]) for k in range(8)], R=[Wb, xb])
                if kind == "u":
                    op(DVE, lambda: nc.vector.tensor_tensor(out=uTb.t[:, idx, :, c0 // 8:(c0 + N) // 8].rearrange("p i c -> p c i"), in0=ps.t[:, :N].rearrange("p (c i) -> p c i", i=8), in1=rstd.t[:, :N].rearrange("p (c i) -> p c i", i=8), op=ALU.mult), R=[ps, rstd], W=[uTb])
                elif kind == "v":
                    vt = bfr.next()
                    op(DVE, lambda: nc.vector.tensor_tensor(out=vt.t[:, :N], in0=ps.t[:, :N], in1=rstd.t[:, :N], op=ALU.mult), R=[ps, rstd], W=[vt])
                    stt["vt"] = vt
                else:
                    tq = f32r.next()
                    op(DVE, lambda: nc.vector.tensor_tensor(out=tq.t[:, :N], in0=ps.t[:, :N], in1=rstd.t[:, :N], op=ALU.mult), R=[ps, rstd], W=[tq])
                    stt["tq"] = tq

            def stageB(P, ch, stt):
                (kind, idx, wc) = ch
                c0, N = P["c0"], P["N"]
                if kind == "u":
                    return
                if kind == "v":
                    vt = stt["vt"]
                    for s_ in range(N // 128):
                        tt = c0 // 128 + s_
                        op(PE, lambda: nc.tensor.transpose(out=trb.t[:, s_ * 128:(s_ + 1) * 128], in_=vt.t[:, s_ * 128:(s_ + 1) * 128], identity=id_b), R=[vt, cmb], W=[trb])
                        op(ACT, lambda: nc.scalar.copy(out=Vxb.t[:, tt, 2 * idx:2 * idx + 2, 0:64],
                                                       in_=trb.t[:, s_ * 128:(s_ + 1) * 128].rearrange("p (g d) -> p g d", g=2)), R=[trb], W=[Vxb])
                    return
                tq = stt["tq"]
                sq2 = bfr.next()
                op(ACT, lambda: nc.scalar.activation(out=sq2.t[:, :N], in_=tq.t[:, :N], func=AF.Square), R=[tq], W=[sq2])
                hs = hsr.next()
                mm_group(hs, [(hs.t[:, :N], blk_b, sq2.t[:, :N])], R=[sq2, cmb])
                rt2 = f32r.next()
                op(ACT, lambda: nc.scalar.activation(out=rt2.t[:, :N], in_=hs.t[:, :N], func=AF.Ln, scale=1.0 / 64, bias=eps_c), R=[hs], W=[rt2])
                rq = f32r.next()
                op(ACT, lambda: nc.scalar.activation(out=rq.t[:, :N], in_=rt2.t[:, :N], func=AF.Exp, scale=-0.5), R=[rt2], W=[rq])
                tq2 = bfr.next()
                gc_ = gcol(GV_QG if kind == "q" else GV_KG)
                op(DVE, lambda: nc.vector.scalar_tensor_tensor(out=tq2.t[:, :N], in0=tq.t[:, :N], scalar=gc_, in1=rq.t[:, :N], op0=ALU.mult, op1=ALU.mult), R=[tq, rq], W=[tq2])
                stt["tq2"] = tq2

            def stageC(P, ch, stt):
                (kind, idx, wc) = ch
                c0, N, cs = P["c0"], P["N"], P["cs"]
                if kind in "uv":
                    return
                tq2 = stt["tq2"]
                rot = rotr.next()
                mm_group(rot, [(rot.t[:, :N], rot_b, tq2.t[:, :N])], R=[tq2, cmb])
                t1 = f32r.next()
                op(DVE, lambda: nc.vector.tensor_tensor(out=t1.t[:, :N], in0=tq2.t[:, :N], in1=cs.t[:, 0, :N], op=ALU.mult), R=[tq2, cs], W=[t1])
                t2 = f32r.next()
                op(DVE, lambda: nc.vector.tensor_tensor(out=t2.t[:, :N], in0=rot.t[:, :N], in1=cs.t[:, 1, :N], op=ALU.mult), R=[rot, cs], W=[t2])
                if kind == "k":
                    op(DVE, lambda: nc.vector.tensor_tensor(out=kTb.t[:, idx, c0:c0 + N], in0=t1.t[:, :N], in1=t2.t[:, :N], op=ALU.add), R=[t1, t2], W=[kTb])
                else:
                    qo = qor.next()
                    op(DVE, lambda: nc.vector.tensor_tensor(out=qo.t[:, :N], in0=t1.t[:, :N], in1=t2.t[:, :N], op=ALU.add), R=[t1, t2], W=[qo])
                    dma(qo.slot, qT_d[idx * 128:(idx + 1) * 128, c0:c0 + N], qo.t[:, :N], R=[qo])

            nch = len(chunks)
            Pn = prologue(0)
            for ti in range(len(tiles_)):
                P = Pn
                stts = [dict() for _ in range(nch)]
                for s_i in range(nch + 2):
                    if s_i < nch:
                        stageA(P, chunks[s_i], stts[s_i])
                        if s_i == nch - 1 and ti + 1 < len(tiles_):
                            Pn = prologue(ti + 1)
                    if 0 <= s_i - 1 < nch:
                        stageB(P, chunks[s_i - 1], stts[s_i - 1])
                    if 0 <= s_i - 2 < nch:
                        stageC(P, chunks[s_i - 2], stts[s_i - 2])

        eps_t = top.enter_context(_sb("eps", [128, 1], F32))
        epsb = Buf(eps_t)
        op(DVE, lambda: nc.vector.memset(eps_t[:], EPS), W=[epsb])
        ACT.wait(epsb.wr)
        eps_c = eps_t[:, 0:1]

        with ExitStack() as ph_prm:
            ph = ph_prm
            slot_ptr[0] = base_slot_ptr
            NPR = 104
            PR_t = ph.enter_context(_sb("PR", [128, NPR * 32], F32))
            PRb = Buf(PR_t, new_slot())
            pr_i = [0]

            def col():
                i = pr_i[0]
                pr_i[0] += 1
                assert i < NPR
                return PR_t[:, i * 32:(i + 1) * 32]

            A_re, A_im, LDT = col(), col(), col()
            dma(PRb.slot, PR_t[:, 0:96], ssmA, W=[PRb])

            def V(fn):
                return op(DVE, fn, R=[PRb], W=[PRb])

            def A(fn):
                return op(ACT, fn, R=[PRb], W=[PRb])

            def TT(o, a, b, o_):
                V(lambda: nc.vector.tensor_tensor(out=o, in0=a, in1=b, op=o_))

            def TS(o, a, s1, o1, s2=None, o2=None):
                if o2 is None:
                    V(lambda: nc.vector.tensor_scalar(out=o, in0=a, scalar1=s1, scalar2=None, op0=o1))
                else:
                    V(lambda: nc.vector.tensor_scalar(out=o, in0=a, scalar1=s1, scalar2=s2, op0=o1, op1=o2))

            t1, t2 = col(), col()

            def cmul(orr, oi, ar, ai, br, bi):
                TT(t1, ar, br, ALU.mult)
                TT(t2, ai, bi, ALU.mult)
                TT(orr, t1, t2, ALU.subtract)
                TT(t1, ar, bi, ALU.mult)
                TT(t2, ai, br, ALU.mult)
                TT(oi, t1, t2, ALU.add)

            dt_, lre, mag, ang, cs_, sn_, c2, s2 = col(), col(), col(), col(), col(), col(), col(), col()
            pio2 = col()
            V(lambda: nc.vector.memset(pio2, float(np.pi / 2)))
            A(lambda: nc.scalar.activation(out=dt_, in_=LDT, func=AF.Exp))
            TS(lre, A_re, -1e-4, ALU.min)
            TT(t1, lre, dt_, ALU.mult)
            A(lambda: nc.scalar.activation(out=mag, in_=t1, func=AF.Exp))
            TT(ang, A_im, dt_, ALU.mult)
            ki_t = ph.enter_context(_sb("ki", [128, 32], mybir.dt.int32))
            kf, rr_, mm_ = col(), col(), col()
            TS(kf, ang, float(1.0 / (2 * np.pi)), ALU.mult)
            V(lambda: nc.vector.tensor_copy(out=ki_t[:], in_=kf))
            V(lambda: nc.vector.tensor_copy(out=kf, in_=ki_t[:]))
            V(lambda: nc.vector.scalar_tensor_tensor(out=rr_, in0=kf, scalar=float(-2 * np.pi), in1=ang, op0=ALU.mult, op1=ALU.add))
            pi_c, npi_c = col(), col()
            V(lambda: nc.vector.memset(pi_c, float(np.pi)))
            V(lambda: nc.vector.memset(npi_c, float(-np.pi)))
            TT(mm_, rr_, pi_c, ALU.is_gt)
            V(lambda: nc.vector.scalar_tensor_tensor(out=rr_, in0=mm_, scalar=float(-2 * np.pi), in1=rr_, op0=ALU.mult, op1=ALU.add))
            TT(mm_, rr_, npi_c, ALU.is_lt)
            V(lambda: nc.vector.scalar_tensor_tensor(out=rr_, in0=mm_, scalar=float(2 * np.pi), in1=rr_, op0=ALU.mult, op1=ALU.add))
            TS(s2, rr_, -1.0, ALU.mult)
            TT(c2, rr_, s2, ALU.max)
            A(lambda: nc.scalar.activation(out=sn_, in_=rr_, func=AF.Sin))
            A(lambda: nc.scalar.activation(out=cs_, in_=c2, func=AF.Sin, scale=-1.0, bias=pio2[:, 0:1]))
            lbr, lbi = col(), col()
            TT(lbr, mag, cs_, ALU.mult)
            TT(lbi, mag, sn_, ALU.mult)
            numr, den, rden, fre, fim = col(), col(), col(), col(), col()
            TS(numr, lbr, -1.0, ALU.add)
            TT(t1, lre, lre, ALU.mult)
            TT(t2, A_im, A_im, ALU.mult)
            TT(den, t1, t2, ALU.add)
            V(lambda: nc.vector.reciprocal(out=rden, in_=den))
            TT(t1, numr, lre, ALU.mult)
            TT(t2, lbi, A_im, ALU.mult)
            TT(fre, t1, t2, ALU.add)
            TT(fre, fre, rden, ALU.mult)
            TT(t1, lbi, lre, ALU.mult)
            TT(t2, numr, A_im, ALU.mult)
            TT(fim, t1, t2, ALU.subtract)
            TT(fim, fim, rden, ALU.mult)
            Pre = [col() for _ in range(9)]
            Pim = [col() for _ in range(9)]
            NPim = [col() for _ in range(9)]
            V(lambda: nc.vector.memset(Pre[0], 1.0))
            V(lambda: nc.vector.memset(Pim[0], 0.0))
            for tau in range(8):
                cmul(Pre[tau + 1], Pim[tau + 1], Pre[tau], Pim[tau], lbr, lbi)
            for tau in range(9):
                TS(NPim[tau], Pim[tau], -1.0, ALU.mult)
            Gre = [col() for _ in range(8)]
            Gim = [col() for _ in range(8)]
            for tau in range(8):
                cmul(Gre[tau], Gim[tau], Pre[tau], Pim[tau], fre, fim)
            nlev = 0
            while (1 << nlev) < NCH:
                nlev += 1
            Hre = [col() for _ in range(nlev)]
            Him = [col() for _ in range(nlev)]
            NHim = [col() for _ in range(nlev)]
            V(lambda: nc.vector.tensor_copy(out=Hre[0], in_=Pre[8]))
            V(lambda: nc.vector.tensor_copy(out=Him[0], in_=Pim[8]))
            for k in range(1, nlev):
                cmul(Hre[k], Him[k], Hre[k - 1], Him[k - 1], Hre[k - 1], Him[k - 1])
            for k in range(nlev):
                TS(NHim[k], Him[k], -1.0, ALU.mult)
            KT_t = ph.enter_context(_sb("KT", [128, 2, 32, 8], F32))
            PT_t = ph.enter_context(_sb("PTt", [128, 3, 32, 9], F32))
            for i_ in range(8):
                for (ri, Garr) in ((0, Gre), (1, Gim)):
                    V(lambda: nc.vector.tensor_copy(out=KT_t[:, ri, 0:16, i_], in_=Garr[7 - i_][:, 0:16]))
                    V(lambda: nc.vector.tensor_copy(out=KT_t[:, ri, 16:32, i_], in_=Garr[i_][:, 16:32]))
            for tau_ in range(9):
                for (ri, Parr) in ((0, Pre), (1, Pim), (2, NPim)):
                    V(lambda: nc.vector.tensor_copy(out=PT_t[:, ri, :, tau_], in_=Parr[tau_]))
            ev_pr = V(lambda: nc.vector.tensor_copy(out=t1, in_=t2))
            for E in (ACT, POOL, PE):
                E.wait(ev_pr)
            if dbg:
                prdbg = nc.dram_tensor("prdbg", [128, NPR * 32], F32, kind="ExternalOutput").ap()
                dma(PRb.slot, prdbg, PR_t[:], R=[PRb])

            base_slot_ptr = slot_ptr[0]
            for hf in range(2):
              with ExitStack() as ph_ssm:
                uT_t = ph_ssm.enter_context(_sb(f"uT{hf}", [128, 2, 8, NCH], BF16))
                uTb = Buf(uT_t)
                with ExitStack() as ph:
                    slot_ptr[0] = base_slot_ptr
                    in_proj(ph, [("u", i, i * 128) for i in range(2)], hf * 256, 256, uTb=uTb)
                    barrier()
                slot_ptr[0] = base_slot_ptr
                ph = ph_ssm
                bzr = Ring([Buf(ph.enter_context(_sb(f"bz{i}", [128, 4, 128], F32)), new_slot()) for i in range(2)])
                Wst_r = Ring([Buf(ph.enter_context(_sb(f"Wst{i}", [128, 16, 128], BF16))) for i in range(1)])
                WAb = Buf(ph.enter_context(_sb("WA", [128, 16, 128], F32)))
                T1b = Buf(ph.enter_context(_sb("T1", [128, 16, 128], F32)))
                T2b = Buf(ph.enter_context(_sb("T2", [128, 16, 128], F32)))
                BbZ_t = ph.enter_context(_sb("BbZ", [128, 16, 128], BF16))
                BbZb = Buf(BbZ_t)
                CL_t = ph.enter_context(_sb("CL", [128, 8 * 18, 128], BF16))
                CLb = Buf(CL_t)
                BD_t = ph.enter_context(_sb("BD", [128, 16, 128], BF16))
                BDb = Buf(BD_t)
                XA_t = ph.enter_context(_sb("XA", [128, 2, NCH], F32))
                XB_t = ph.enter_context(_sb("XB", [128, 2, NCH], F32))
                XAb, XBb = Buf(XA_t), Buf(XB_t)
                Xb_t = ph.enter_context(_sb("Xb", [128, 8, 2, NCH], BF16))
                Xbb = Buf(Xb_t)
                zbuf_t = ph.enter_context(_sb("zbuf", [128, LP], BF16))
                zb = Buf(zbuf_t, new_slot())
                ytr = Ring([Buf(ph.enter_context(_sb(f"yt{i}", [128, 512], F32))) for i in range(2)])
                op(POOL, lambda: nc.gpsimd.memset(Xb_t[:].rearrange("p a b c -> p (a b c)"), 0.0), W=[Xbb])
                trr = Ring([banks[0], banks[3]])
                sring = Ring(banks[1:3])
                kring = Ring(banks[3:4])
                accr = Ring(banks[4:7])
                cbs = colblocks(NCH)

                for gcl in range(2):
                    gc = hf * 2 + gcl
                    uv = uT_t[:, gcl, :, :].rearrange("p i c -> p c i")
                    for gpl in range(4):
                        gp = gc * 4 + gpl
                        for d in range(2):
                            cidx = d * 16 + gp
                            slot8 = gpl * 2 + d
                            bzb = bzr.next()
                            dma(bzb.slot, bzb.t[:], bz[cidx].rearrange("p (a c) -> p a c", a=4), W=[bzb])
                            Bre, Bim, Cre, Cim = (bzb.t[:, a, :] for a in range(4))
                            Wst = Wst_r.next()

                            def sc(arr):
                                return arr[:, cidx:cidx + 1]

                            wa = WAb
                            kre = KT_t[:, 0, cidx, :].unsqueeze(2).unsqueeze(3).broadcast_to([128, 8, 2, 128])
                            kim2 = KT_t[:, 1, cidx, :].unsqueeze(2).broadcast_to([128, 8, 128])
                            bpair = bzb.t[:, 0:2, :].unsqueeze(1).broadcast_to([128, 8, 2, 128])
                            bre8 = Bre.unsqueeze(1).broadcast_to([128, 8, 128])
                            bim8 = Bim.unsqueeze(1).broadcast_to([128, 8, 128])
                            T1v = T1b.t[:].rearrange("p (i a) c -> p i a c", a=2)
                            T2v = T2b.t[:].rearrange("p (i a) c -> p i a c", a=2)
                            WAv = wa.t[:].rearrange("p (i a) c -> p i a c", a=2)
                            op(DVE, lambda: nc.vector.tensor_tensor(out=T1v, in0=bpair, in1=kre, op=ALU.mult), R=[bzb], W=[T1b])
                            op(DVE, lambda: nc.vector.tensor_tensor(out=T2v[:, :, 0, :], in0=bim8, in1=kim2, op=ALU.mult), R=[bzb], W=[T2b])
                            op(DVE, lambda: nc.vector.tensor_tensor(out=T2v[:, :, 1, :], in0=bre8, in1=kim2, op=ALU.mult), R=[bzb], W=[T2b])
                            op(DVE, lambda: nc.vector.tensor_tensor(out=WAv[:, :, 0, :], in0=T1v[:, :, 0, :], in1=T2v[:, :, 0, :], op=ALU.subtract), R=[T1b, T2b], W=[wa])
                            op(DVE, lambda: nc.vector.tensor_tensor(out=WAv[:, :, 1, :], in0=T1v[:, :, 1, :], in1=T2v[:, :, 1, :], op=ALU.add), R=[T1b, T2b], W=[wa])
                            i0 = 7 if d == 0 else 0
                            op(DVE, lambda: nc.vector.tensor_copy(out=BbZ_t[:, slot8 * 2:slot8 * 2 + 2, :], in_=wa.t[:, i0 * 2:i0 * 2 + 2, :]), R=[wa], W=[BbZb])
                            for q4 in range(4):
                                tb = trr.next()
                                for m4 in range(4):
                                    idx = q4 * 4 + m4
                                    op(PE, lambda: nc.tensor.transpose(out=tb.t[:, m4 * 128:(m4 + 1) * 128], in_=wa.t[:, idx, :], identity=id_f), R=[wa, cmf], W=[tb], mark=(m4 == 3))
                                op(ACT, lambda: nc.scalar.copy(out=Wst.t[:, q4 * 4:(q4 + 1) * 4, :], in_=tb.t[:].rearrange("p (m c) -> p m c", m=4)), R=[tb], W=[Wst])
                            pre9 = PT_t[:, 0, cidx, :].unsqueeze(2).broadcast_to([128, 9, 128])
                            pim9 = PT_t[:, 1, cidx, :].unsqueeze(2).broadcast_to([128, 9, 128])
                            npim9 = PT_t[:, 2, cidx, :].unsqueeze(2).broadcast_to([128, 9, 128])
                            cre9 = Cre.unsqueeze(1).broadcast_to([128, 9, 128])
                            cim9 = Cim.unsqueeze(1).broadcast_to([128, 9, 128])
                            A1 = T1b.t[:, 0:9, :]
                            A2 = T2b.t[:, 0:9, :]
                            CLv = CL_t[:, slot8 * 18:(slot8 + 1) * 18, :].rearrange("p (t a) c -> p t a c", a=2)
                            op(DVE, lambda: nc.vector.tensor_tensor(out=A1, in0=cre9, in1=pre9, op=ALU.mult), R=[bzb], W=[T1b])
                            op(DVE, lambda: nc.vector.tensor_tensor(out=A2, in0=cim9, in1=pim9, op=ALU.mult), R=[bzb], W=[T2b])
                            op(DVE, lambda: nc.vector.tensor_tensor(out=CLv[:, :, 0, :], in0=A1, in1=A2, op=ALU.subtract), R=[T1b, T2b], W=[CLb])
                            op(DVE, lambda: nc.vector.tensor_tensor(out=A1, in0=cre9, in1=npim9, op=ALU.mult), R=[bzb], W=[T1b])
                            op(DVE, lambda: nc.vector.tensor_tensor(out=A2, in0=cim9, in1=pre9, op=ALU.mult), R=[bzb], W=[T2b])
                            op(DVE, lambda: nc.vector.tensor_tensor(out=CLv[:, :, 1, :], in0=A1, in1=A2, op=ALU.subtract), R=[T1b, T2b], W=[CLb])
                            for part in range(2):
                                for (cb0, n) in cbs:
                                    sbk = sring.next()
                                    mm_group(sbk, [(sbk.t[:, :n], Wst.t[:, i * 2 + part, :], uv[:, cb0:cb0 + n, i]) for i in range(8)], R=[Wst, uTb])
                                    op(ACT, lambda: nc.scalar.copy(out=XA_t[:, part, cb0:cb0 + n], in_=sbk.t[:, :n]), R=[sbk], W=[XAb])
                            src, dst = XAb, XBb
                            for k in range(nlev):
                                s_ = 1 << k
                                a_, b_, nb_ = sc(Hre[k]), sc(Him[k]), sc(NHim[k])
                                X, Y = src.t, dst.t
                                if d == 0:
                                    lo, hi = slice(0, NCH - s_), slice(s_, NCH)
                                    keep = slice(0, s_)
                                else:
                                    lo, hi = slice(s_, NCH), slice(0, NCH - s_)
                                    keep = slice(NCH - s_, NCH)
                                op(ACT, lambda: nc.scalar.copy(out=Y[:, :, keep], in_=X[:, :, keep]), R=[src], W=[dst])
                                op(DVE, lambda: nc.vector.scalar_tensor_tensor(out=Y[:, 0, hi], in0=X[:, 0, lo], scalar=a_, in1=X[:, 0, hi], op0=ALU.mult, op1=ALU.add), R=[src], W=[dst])
                                op(DVE, lambda: nc.vector.scalar_tensor_tensor(out=Y[:, 0, hi], in0=X[:, 1, lo], scalar=nb_, in1=Y[:, 0, hi], op0=ALU.mult, op1=ALU.add), R=[src], W=[dst])
                                op(DVE, lambda: nc.vector.scalar_tensor_tensor(out=Y[:, 1, hi], in0=X[:, 0, lo], scalar=b_, in1=X[:, 1, hi], op0=ALU.mult, op1=ALU.add), R=[src], W=[dst])
                                op(DVE, lambda: nc.vector.scalar_tensor_tensor(out=Y[:, 1, hi], in0=X[:, 1, lo], scalar=a_, in1=Y[:, 1, hi], op0=ALU.mult, op1=ALU.add), R=[src], W=[dst])
                                src, dst = dst, src
                            Z = src.t
                            if d == 0:
                                op(ACT, lambda: nc.scalar.copy(out=Xb_t[:, slot8, :, 1:NCH], in_=Z[:, :, 0:NCH - 1]), R=[src], W=[Xbb])
                            else:
                                op(ACT, lambda: nc.scalar.copy(out=Xb_t[:, slot8, :, 0:NCH - 1], in_=Z[:, :, 1:NCH]), R=[src], W=[Xbb])
                    for d in range(2):
                        for tau in range(8):
                            kb = kring.next()
                            mms = []
                            for gpl in range(4):
                                s8 = gpl * 2 + d
                                for part in range(2):
                                    mms.append((kb.t[:, 0:128], BbZ_t[:, s8 * 2 + part, :], CL_t[:, s8 * 18 + tau * 2 + part, :]))
                            mm_group(kb, mms, R=[BbZb, CLb])
                            op(ACT, lambda: nc.scalar.copy(out=BD_t[:, d * 8 + tau, :], in_=kb.t[:, 0:128]), R=[kb], W=[BDb])
                    zv = zbuf_t[:].rearrange("p (c i) -> p c i", i=8)
                    for j in range(8):
                        for (cb0, n) in cbs:
                            ab = accr.next()
                            mms = []
                            for i in range(0, j + 1):
                                mms.append((ab.t[:, :n], BD_t[:, 0 * 8 + (j - i), :], uv[:, cb0:cb0 + n, i]))
                            for i in range(j, 8):
                                mms.append((ab.t[:, :n], BD_t[:, 1 * 8 + (i - j), :], uv[:, cb0:cb0 + n, i]))
                            for gpl in range(4):
                                for d in range(2):
                                    s8 = gpl * 2 + d
                                    tau = (j + 1) if d == 0 else (8 - j)
                                    for part in range(2):
                                        mms.append((ab.t[:, :n], CL_t[:, s8 * 18 + tau * 2 + part, :], Xb_t[:, s8, part, cb0:cb0 + n]))
                            mm_group(ab, mms, R=[BDb, uTb, CLb, Xbb])
                            yt = ytr.next()
                            op(DVE, lambda: nc.vector.scalar_tensor_tensor(out=yt.t[:, :n], in0=uv[:, cb0:cb0 + n, j], scalar=gcol(GV_SSMD + gc), in1=ab.t[:, :n], op0=ALU.mult, op1=ALU.add), R=[ab, uTb], W=[yt])
                            op(ACT, lambda: nc.scalar.activation(out=zv[:, cb0:cb0 + n, j], in_=yt.t[:, :n], func=AF.Gelu), R=[yt], W=[zb])
                    dma(zb.slot, zT_d[gc * 128:(gc + 1) * 128, :], zbuf_t[:], R=[zb])
                barrier()

        conv_jobs = []
        for (src_, dst_, rows_, cols_, g0_) in [(w_glu, wglub, 512, 512, None), (w_sp, wspb, 512, D, None), (w_ap, wapb, D, D, None),
                                                  (w_out, woutb, D, D, None), (w1, w1b, D, DFF, GV_MLP), (w2, w2b, DFF, D, None)]:
            for k_ in range(rows_ // 128):
                for c0_ in range(0, cols_, 2048):
                    conv_jobs.append((src_, dst_, k_, c0_, min(2048, cols_ - c0_), g0_))
        conv_state = {"i": 0, "stg": None, "ost": None}

        def conv_step(nj):
            for _ in range(nj):
                if conv_state["i"] >= len(conv_jobs):
                    return
                (src_, dst_, k_, c0_, n_, g0_) = conv_jobs[conv_state["i"]]
                conv_state["i"] += 1
                st = conv_state["stg"].next()
                dma(st.slot, st.t[:, :n_], src_[k_ * 128:(k_ + 1) * 128, c0_:c0_ + n_], W=[st])
                ob = conv_state["ost"].next()
                if g0_ is None:
                    op(DVE, lambda: nc.vector.tensor_copy(out=ob.t[:, :n_], in_=st.t[:, :n_]), R=[st], W=[ob])
                else:
                    op(DVE, lambda: nc.vector.tensor_scalar(out=ob.t[:, :n_], in0=st.t[:, :n_], scalar1=gcol(g0_ + k_), scalar2=None, op0=ALU.mult), R=[st], W=[ob])
                dma(ob.slot, dst_[k_ * 128:(k_ + 1) * 128, c0_:c0_ + n_], ob.t[:, :n_], R=[ob])

        with ExitStack() as ph_att:
            kT_t = ph_att.enter_context(_sb("kT", [128, 2, LP], BF16))
            Vx_t = ph_att.enter_context(_sb("Vx", [128, NT, 4, 65], BF16))
            kTb, Vxb = Buf(kT_t), Buf(Vx_t)
            op(POOL, lambda: nc.gpsimd.memset(Vx_t[:].rearrange("p a b c -> p (a b c)"), 1.0), W=[Vxb])
            with ExitStack() as ph:
                slot_ptr[0] = base_slot_ptr
                chunks = [("q", i, i * 128) for i in range(8)] + [("k", i, 1024 + i * 128) for i in range(2)] + [("v", i, 1280 + i * 128) for i in range(2)]
                in_proj(ph, chunks, 512, 1536, kTb=kTb, Vxb=Vxb)
                barrier()
            slot_ptr[0] = base_slot_ptr
            ph = ph_att
            conv_state["stg"] = Ring([Buf(ph.enter_context(_sb(f"cst{i}", [128, 2048], F32)), new_slot()) for i in range(3)])
            conv_state["ost"] = Ring([Buf(ph.enter_context(_sb(f"cso{i}", [128, 2048], BF16)), new_slot()) for i in range(3)])
            Qr = Ring([Buf(ph.enter_context(_sb(f"Q{i}", [128, LP], BF16)), new_slot()) for i in range(2)])
            PTr = Ring([Buf(ph.enter_context(_sb(f"PT{i}", [128, 2, 512], BF16))) for i in range(3)])
            recb = Buf(ph.enter_context(_sb("rec", [128, 512], F32)))
            yor = Ring([Buf(ph.enter_context(_sb(f"yo{i}", [128, 512], BF16)), new_slot()) for i in range(2)])
            Sr = Ring([Buf(PSA[:, 0:2, :]), Buf(PSA[:, 2:4, :])])
            Or = Ring([banks[4], banks[6]])
            DENb = banks[5]
            pairs = [(h, h + 4) for h in (0, 1, 2, 3)] + [(h, h + 4) for h in (8, 9, 10, 11)]
            for (ha, hb) in pairs:
                ga, gb = ha // 4, hb // 4
                kc = ga // 2
                Qb = Qr.next()
                dma(Qb.slot, Qb.t[0:64, :], qT_d[ha * 64:(ha + 1) * 64, :], W=[Qb])
                dma(Qb.slot, Qb.t[64:128, :], qT_d[hb * 64:(hb + 1) * 64, :], upd=[Qb])
                for (q0, NQ) in colblocks(S):
                    qc0 = 128 + q0
                    Ob = Or.next()
                    pend = None
                    first_pv = True

                    def do_pv(pend, last):
                        nonlocal first_pv
                        PTb, kt = pend
                        f_ = first_pv
                        first_pv = False
                        for b in (PTb, Vxb, cmb):
                            PE.wait(b.wr)
                        if f_:
                            for bk in (Ob, DENb):
                                PE.wait(bk.wr)
                                PE.wait(list(bk.rd.values()))
                        nc.tensor.matmul(Ob.t[0:64, :NQ], lhsT=Vx_t[:, kt, ga, 0:64], rhs=PTb.t[:, 0, :NQ], start=f_, stop=last)
                        nc.tensor.matmul(Ob.t[64:128, :NQ], lhsT=Vx_t[:, kt, gb, 0:64], rhs=PTb.t[:, 1, :NQ], start=f_, stop=last)
                        nc.tensor.matmul(DENb.t[0:64, :NQ], lhsT=ones_b[:, 0:64], rhs=PTb.t[:, 0, :NQ], start=f_, stop=last)
                        inst = nc.tensor.matmul(DENb.t[64:128, :NQ], lhsT=ones_b[:, 0:64], rhs=PTb.t[:, 1, :NQ], start=f_, stop=last)
                        ev = PE.mark(inst)
                        PTb.rd[PE.name] = ev
                        if last:
                            for bk in (Ob, DENb):
                                bk.wr = ev
                                bk.rd = {}

                    for kt in range(NT):
                        Sb = Sr.next()
                        mms = [(Sb.t[:, 0, :NQ], kT_t[0:64, kc, kt * 128:(kt + 1) * 128], Qb.t[0:64, qc0:qc0 + NQ], True, True),
                               (Sb.t[:, 1, :NQ], kT_t[64:128, kc, kt * 128:(kt + 1) * 128], Qb.t[64:128, qc0:qc0 + NQ], True, True)]
                        mm_group(Sb, mms, R=[kTb, Qb])
                        if pend is not None:
                            do_pv(pend, False)
                        PTb = PTr.next()
                        if kt == 0:
                            op(ACT, lambda: nc.scalar.activation(out=PTb.t[:, :, :NQ], in_=Sb.t[:, :, :NQ], func=AF.Exp, scale=0.125, bias=gcol(GV_MASK)), R=[Sb], W=[PTb])
                        else:
                            op(ACT, lambda: nc.scalar.activation(out=PTb.t[:, :, :NQ], in_=Sb.t[:, :, :NQ], func=AF.Exp, scale=0.125), R=[Sb], W=[PTb])
                        pend = (PTb, kt)
                    do_pv(pend, True)
                    op(DVE, lambda: nc.vector.reciprocal(out=recb.t[:, :NQ], in_=DENb.t[:, :NQ]), R=[DENb], W=[recb])
                    yo = yor.next()
                    op(DVE, lambda: nc.vector.tensor_tensor(out=yo.t[:, :NQ], in0=Ob.t[:, :NQ], in1=recb.t[:, :NQ], op=ALU.mult), R=[Ob, recb], W=[yo])
                    dma(yo.slot, yaT_d[ha * 64:(ha + 1) * 64, q0:q0 + NQ], yo.t[0:64, :NQ], R=[yo])
                    dma(yo.slot, yaT_d[hb * 64:(hb + 1) * 64, q0:q0 + NQ], yo.t[64:128, :NQ], R=[yo])
                    conv_step(1)
            conv_step(len(conv_jobs))
            barrier()

        with ExitStack() as ph:
            slot_ptr[0] = base_slot_ptr

            def wload(name, src, kchunks, cols, c0=0):
                t = ph.enter_context(_sb(name, [128, kchunks, cols], BF16))
                b = Buf(t, new_slot())
                dma(b.slot, t[:], kview(src)[:, :, c0:c0 + cols], W=[b])
                return b
            Wg = wload("Wg", winb, 8, 2048, 2048)
            Wgl = wload("Wgl", wglub, 4, 512)
            Wsp = wload("Wsp", wspb, 4, D)
            Wap = wload("Wap", wapb, 8, D)
            Wo = wload("Wo", woutb, 8, D)
            xring = Ring([Buf(ph.enter_context(_sb(f"xt{i}", [128, 8, 512], F32)), new_slot()) for i in range(2)])
            sqb_ = Buf(ph.enter_context(_sb("sq3", [128, 8, 512], BF16)))
            xbb_ = Buf(ph.enter_context(_sb("xb3", [128, 8, 512], BF16)))
            ztr = Ring([Buf(ph.enter_context(_sb(f"zt{i}", [128, 4, 512], BF16)), new_slot()) for i in range(2)])
            yar = Ring([Buf(ph.enter_context(_sb(f"ya{i}", [128, 8, 512], BF16)), new_slot()) for i in range(2)])
            ysb = Buf(ph.enter_context(_sb("ys", [128, 4, 512], BF16)))
            mgb = Buf(ph.enter_context(_sb("mg", [128, 8, 512], BF16)))
            f32r = Ring([Buf(ph.enter_context(_sb(f"f3r{i}", [128, 512], F32))) for i in range(10)])
            rstdb = Buf(ph.enter_context(_sb("rstd3", [128, 512], F32)))
            h1r = Ring([Buf(ph.enter_context(_sb(f"h1s{i}", [128, 512], F32)), new_slot()) for i in range(3)])
            pr = Ring(banks)
            xTv = kview(xT)
            zTv = kview(zT_d)
            yaTv = kview(yaT_d)
            N = 512
            rstdr3 = Ring([rstdb, Buf(ph.enter_context(_sb("rstd3b", [128, 512], F32)))])
            tiles3 = colblocks(S)

            def prologue3(ti):
                q0 = tiles3[ti][0]
                c0 = 128 + q0
                xt = xring.next()
                dma(xt.slot, xt.t[:], xTv[:, :, c0:c0 + N], W=[xt])
                zt = ztr.next()
                dma(zt.slot, zt.t[:], zTv[:, :, c0:c0 + N], W=[zt])
                ya = yar.next()
                dma(ya.slot, ya.t[:], yaTv[:, :, q0:q0 + N], W=[ya])
                op(ACT, lambda: nc.scalar.activation(out=sqb_.t[:], in_=xt.t[:], func=AF.Square), R=[xt], W=[sqb_])
                op(DVE, lambda: nc.vector.tensor_copy(out=xbb_.t[:], in_=xt.t[:]), R=[xt], W=[xbb_])
                ssb = pr.next()
                mm_group(ssb, [(ssb.t[:], ones_b, sqb_.t[:, k, :]) for k in range(8)], R=[sqb_, cmb])
                rt = f32r.next()
                op(ACT, lambda: nc.scalar.activation(out=rt.t[:], in_=ssb.t[:], func=AF.Ln, scale=1.0 / D, bias=eps_c), R=[ssb], W=[rt])
                rs_ = rstdr3.next()
                op(ACT, lambda: nc.scalar.activation(out=rs_.t[:], in_=rt.t[:], func=AF.Exp, scale=-0.5), R=[rt], W=[rs_])
                return dict(q0=q0, xt=xt, zt=zt, ya=ya, rstd=rs_)

            def main3(P):
                zt, ya, rs_ = P["zt"], P["ya"], P["rstd"]
                for oc in range(4):
                    ps = pr.next()
                    mm_group(ps, [(ps.t[:], Wgl.t[:, k, oc * 128:(oc + 1) * 128], zt.t[:, k, :]) for k in range(4)], R=[Wgl, zt])
                    sg = f32r.next()
                    op(ACT, lambda: nc.scalar.activation(out=sg.t[:], in_=ps.t[:], func=AF.Sigmoid, bias=gcol(GV_BGLU + oc)), R=[ps], W=[sg])
                    op(DVE, lambda: nc.vector.tensor_tensor(out=ysb.t[:, oc, :], in0=zt.t[:, oc, :], in1=sg.t[:], op=ALU.mult), R=[zt, sg], W=[ysb])
                for oc in range(8):
                    sgs = []
                    for gi in range(2):
                        ps = pr.next()
                        wc = gi * 1024 + oc * 128
                        mm_group(ps, [(ps.t[:], Wg.t[:, k, wc:wc + 128], xbb_.t[:, k, :]) for k in range(8)], R=[Wg, xbb_])
                        tg = f32r.next()
                        op(DVE, lambda: nc.vector.tensor_tensor(out=tg.t[:], in0=ps.t[:], in1=rs_.t[:], op=ALU.mult), R=[ps, rs_], W=[tg])
                        sg = f32r.next()
                        op(ACT, lambda: nc.scalar.activation(out=sg.t[:], in_=tg.t[:], func=AF.Sigmoid), R=[tg], W=[sg])
                        sgs.append(sg)
                    ps = pr.next()
                    mm_group(ps, [(ps.t[:], Wsp.t[:, k, oc * 128:(oc + 1) * 128], ysb.t[:, k, :]) for k in range(4)], R=[Wsp, ysb])
                    m1 = f32r.next()
                    op(DVE, lambda: nc.vector.tensor_tensor(out=m1.t[:], in0=ps.t[:], in1=sgs[0].t[:], op=ALU.mult), R=[ps, sgs[0]], W=[m1])
                    ps2 = pr.next()
                    mm_group(ps2, [(ps2.t[:], Wap.t[:, k, oc * 128:(oc + 1) * 128], ya.t[:, k, :]) for k in range(8)], R=[Wap, ya])
                    m2 = f32r.next()
                    op(DVE, lambda: nc.vector.tensor_tensor(out=m2.t[:], in0=ps2.t[:], in1=sgs[1].t[:], op=ALU.mult), R=[ps2, sgs[1]], W=[m2])
                    op(DVE, lambda: nc.vector.tensor_tensor(out=mgb.t[:, oc, :], in0=m1.t[:], in1=m2.t[:], op=ALU.add), R=[m1, m2], W=[mgb])

            def outproj3(P):
                q0, xt = P["q0"], P["xt"]
                for oc in range(8):
                    ps = pr.next()
                    mm_group(ps, [(ps.t[:], Wo.t[:, k, oc * 128:(oc + 1) * 128], mgb.t[:, k, :]) for k in range(8)], R=[Wo, mgb])
                    h1 = h1r.next()
                    op(DVE, lambda: nc.vector.tensor_tensor(out=h1.t[:], in0=ps.t[:], in1=xt.t[:, oc, :], op=ALU.add), R=[ps, xt], W=[h1])
                    dma(h1.slot, h1T_d[oc * 128:(oc + 1) * 128, q0:q0 + N], h1.t[:], R=[h1])

            Pn = prologue3(0)
            for ti in range(len(tiles3)):
                P = Pn
                main3(P)
                if ti + 1 < len(tiles3):
                    Pn = prologue3(ti + 1)
                outproj3(P)
            barrier()

        with ExitStack() as ph:
            slot_ptr[0] = base_slot_ptr
            W1_t = ph.enter_context(_sb("W1", [128, 8, DFF], BF16))
            W2_t = ph.enter_context(_sb("W2", [128, 32, D], BF16))
            W1b_, W2b_ = Buf(W1_t, new_slot()), Buf(W2_t, new_slot())
            dma(W1b_.slot, W1_t[:], kview(w1b), W=[W1b_])
            dma(W2b_.slot, W2_t[:], kview(w2b), W=[W2b_])
            N = 256
            hr = Ring([Buf(ph.enter_context(_sb(f"h{i}", [128, 8, N], F32)), new_slot()) for i in range(2)])
            sqb_ = Buf(ph.enter_context(_sb("sq4", [128, 8, N], BF16)))
            hbb = Buf(ph.enter_context(_sb("hb4", [128, 8, N], BF16)))
            Ab = Buf(ph.enter_context(_sb("A4", [128, 32, N], BF16)))
            h2b = Buf(ph.enter_context(_sb("h2", [128, 8, N], F32)))
            osr = Ring([Buf(ph.enter_context(_sb(f"os{i}", [128, 8, N], F32)), new_slot()) for i in range(1)])
            f32r = Ring([Buf(ph.enter_context(_sb(f"f4r{i}", [128, N], F32))) for i in range(6)])
            rstdb = Buf(ph.enter_context(_sb("rstd4", [128, N], F32)))
            rstd5 = Buf(ph.enter_context(_sb("rstd5", [128, N], F32)))
            pr = Ring(banks)
            h1v = kview(h1T_d)
            outv = kview(outT)
            sq5b = Buf(ph.enter_context(_sb("sq5", [128, 8, N], BF16)))
            rstdr4 = Ring([rstdb, Buf(ph.enter_context(_sb("rstd4b", [128, N], F32)))])
            tiles4 = colblocks(S, N)

            def prologue4(ti):
                q0 = tiles4[ti][0]
                hb = hr.next()
                dma(hb.slot, hb.t[:], h1v[:, :, q0:q0 + N], W=[hb])
                op(ACT, lambda: nc.scalar.activation(out=sqb_.t[:], in_=hb.t[:], func=AF.Square), R=[hb], W=[sqb_])
                op(DVE, lambda: nc.vector.tensor_copy(out=hbb.t[:], in_=hb.t[:]), R=[hb], W=[hbb])
                ssb = pr.next()
                mm_group(ssb, [(ssb.t[:, :N], ones_b, sqb_.t[:, k, :]) for k in range(8)], R=[sqb_, cmb])
                rt = f32r.next()
                op(ACT, lambda: nc.scalar.activation(out=rt.t[:], in_=ssb.t[:, :N], func=AF.Ln, scale=1.0 / D, bias=eps_c), R=[ssb], W=[rt])
                rs_ = rstdr4.next()
                op(ACT, lambda: nc.scalar.activation(out=rs_.t[:], in_=rt.t[:], func=AF.Exp, scale=-0.5), R=[rt], W=[rs_])
                return dict(q0=q0, hb=hb, rstd=rs_)

            def stage1_4(P):
                rs_ = P["rstd"]
                for f in range(32):
                    ps = pr.next()
                    mm_group(ps, [(ps.t[:, :N], W1_t[:, k, f * 128:(f + 1) * 128], hbb.t[:, k, :]) for k in range(8)], R=[W1b_, hbb])
                    tr_ = f32r.next()
                    op(DVE, lambda: nc.vector.scalar_tensor_tensor(out=tr_.t[:], in0=ps.t[:, :N], scalar=0.0, in1=rs_.t[:], op0=ALU.max, op1=ALU.mult), R=[ps, rs_], W=[tr_])
                    op(ACT, lambda: nc.scalar.activation(out=Ab.t[:, f, :], in_=tr_.t[:], func=AF.Square), R=[tr_], W=[Ab])

            def stage2_4(P):
                hb = P["hb"]
                for oc in range(8):
                    ps = pr.next()
                    mm_group(ps, [(ps.t[:, :N], W2_t[:, f, oc * 128:(oc + 1) * 128], Ab.t[:, f, :]) for f in range(32)], R=[W2b_, Ab])
                    op(DVE, lambda: nc.vector.tensor_tensor(out=h2b.t[:, oc, :], in0=ps.t[:, :N], in1=hb.t[:, oc, :], op=ALU.add), R=[ps, hb], W=[h2b])

            def final4(P):
                q0 = P["q0"]
                op(ACT, lambda: nc.scalar.activation(out=sq5b.t[:], in_=h2b.t[:], func=AF.Square), R=[h2b], W=[sq5b])
                ssb = pr.next()
                mm_group(ssb, [(ssb.t[:, :N], ones_b, sq5b.t[:, k, :]) for k in range(8)], R=[sq5b, cmb])
                rt = f32r.next()
                op(ACT, lambda: nc.scalar.activation(out=rt.t[:], in_=ssb.t[:, :N], func=AF.Ln, scale=1.0 / D, bias=eps_c), R=[ssb], W=[rt])
                op(ACT, lambda: nc.scalar.activation(out=rstd5.t[:], in_=rt.t[:], func=AF.Exp, scale=-0.5), R=[rt], W=[rstd5])
                os_ = osr.next()
                for oc in range(8):
                    op(DVE, lambda: nc.vector.scalar_tensor_tensor(out=os_.t[:, oc, :], in0=h2b.t[:, oc, :], scalar=gcol(GV_FIN + oc), in1=rstd5.t[:], op0=ALU.mult, op1=ALU.mult), R=[h2b, rstd5], W=[os_])
                dma(os_.slot, outv[:, :, q0:q0 + N], os_.t[:], R=[os_])

            Pn = prologue4(0)
            Pprev = None
            for ti in range(len(tiles4)):
                P = Pn
                stage1_4(P)
                if ti + 1 < len(tiles4):
                    Pn = prologue4(ti + 1)
                if Pprev is not None:
                    final4(Pprev)
                stage2_4(P)
                Pprev = P
            final4(Pprev)
            barrier()
    return nc


def _const_tables(S):
    LP = S + 128
    cm = np.zeros((128, 512), np.float32)
    cm[:, 0:128] = 1.0
    cm[0:64, 128:192] = 1.0
    cm[64:128, 192:256] = 1.0
    for i in range(64):
        cm[2 * i + 1, 256 + 2 * i] = -1.0
        cm[2 * i, 256 + 2 * i + 1] = 1.0
    cm[:, 384:512] = np.eye(128, dtype=np.float32)
    rows = S // 64
    row_id = np.repeat(np.arange(rows, dtype=np.float32), 64)
    col_id = np.tile(np.arange(64, dtype=np.float32), rows)
    inv_freq = (np.float32(10000.0) ** (-np.arange(16, dtype=np.float32) / np.float32(16))).astype(np.float32)
    ang = np.concatenate([row_id[:, None] * inv_freq, col_id[:, None] * inv_freq], axis=-1).astype(np.float32)
    ang = np.concatenate([np.zeros((128, 32), np.float32), ang], axis=0)
    cos = np.cos(ang).astype(np.float32)
    sin = np.sin(ang).astype(np.float32)
    pair = (np.arange(128) % 64) // 2
    C = np.ascontiguousarray(cos[:, pair].T)
    Sn = np.ascontiguousarray(sin[:, pair].T)
    return cm, C, Sn


def _prep_shared(inp, S):
    f = lambda a: np.ascontiguousarray(np.asarray(a, dtype=np.float32))
    cm, C, Sn = _const_tables(S)
    gv = np.zeros((128, NG), np.float32)
    gv[:, GV_MIX:GV_MIX + 8] = f(inp["norm_mix_g"])[0].reshape(8, 128).T
    gv[:, GV_MLP:GV_MLP + 8] = f(inp["norm_mlp_g"])[0].reshape(8, 128).T
    gv[:, GV_FIN:GV_FIN + 8] = f(inp["norm_final_g"]).reshape(8, 128).T
    gv[:, GV_QG] = np.tile(f(inp["q_norm_g"])[0], 2)
    gv[:, GV_KG] = np.tile(f(inp["k_norm_g"])[0], 2)
    gv[:, GV_BGLU:GV_BGLU + 4] = f(inp["b_glu"])[0].reshape(4, 128).T
    gv[:, GV_SSMD:GV_SSMD + 4] = f(inp["ssm_d"])[0].reshape(4, 128).T
    gv[0:112, GV_MASK] = -30000.0
    a_re, a_im, ldt = f(inp["ssm_a_re"])[0], f(inp["ssm_a_im"])[0], f(inp["ssm_log_dt"])[0]
    b_re, b_im = f(inp["ssm_b_re"])[0], f(inp["ssm_b_im"])[0]
    c_re, c_im = f(inp["ssm_c_re"])[0], f(inp["ssm_c_im"])[0]
    ssmA = np.zeros((128, 96), np.float32)
    bz = np.zeros((32, 128, 4, 128), np.float32)
    for d in range(2):
        for gp in range(16):
            ci = d * 16 + gp
            for g2 in range(2):
                g = 2 * gp + g2
                gl = g % 8
                ps = slice(g2 * 64, g2 * 64 + 64)
                ssmA[ps, ci] = a_re[d, g]
                ssmA[ps, 32 + ci] = a_im[d, g]
                ssmA[ps, 64 + ci] = ldt[d, g]
                bz[ci, ps, 0, gl * 16:(gl + 1) * 16] = b_re[d, g]
                bz[ci, ps, 1, gl * 16:(gl + 1) * 16] = b_im[d, g]
                bz[ci, ps, 2, gl * 16:(gl + 1) * 16] = c_re[d, g].T
                bz[ci, ps, 3, gl * 16:(gl + 1) * 16] = c_im[d, g].T
    shared = {
        "w_in": f(inp["w_in"])[0], "w_glu": f(inp["w_glu"])[0], "w_sp": f(inp["w_ssm_proj"])[0],
        "w_ap": f(inp["w_attn_proj"])[0], "w_out": f(inp["w_out"])[0], "w1": f(inp["w_mlp_in"])[0],
        "w2": f(inp["w_mlp_out"])[0], "gv": gv, "cmat": cm, "ropeC": C, "ropeS": Sn, "ssmA": ssmA,
        "bz": bz.reshape(32, 128, 512),
    }
    return shared


def _make_xT(xb, meta):
    S = xb.shape[0]
    full = np.concatenate([np.zeros((112, D), np.float32), np.asarray(meta, np.float32), np.asarray(xb, np.float32)], axis=0)
    return np.ascontiguousarray(full.T)


_NC_CACHE = {}


def kernel(**inputs):
    x = np.asarray(inputs["x"], dtype=np.float32)
    B, S, _ = x.shape
    shared = _prep_shared(inputs, S)
    if S not in _NC_CACHE:
        _NC_CACHE[S] = build(S)
    nc = _NC_CACHE[S]
    in_maps = []
    for b in range(B):
        m = dict(shared)
        m["xT"] = _make_xT(x[b], inputs["meta_tokens"])
        in_maps.append(m)
    res = run_bass_kernel_spmd(nc, in_maps, core_ids=list(range(B)))
    out = np.stack([np.ascontiguousarray(r["outT"].T) for r in res.results], axis=0)
    return out.astype(np.float32)
```

```python
import numpy as np
from contextlib import ExitStack
import concourse.bass as bass
import concourse.mybir as mybir
from concourse.bass_utils import run_bass_kernel_spmd

F32 = mybir.dt.float32
BF16 = mybir.dt.bfloat16
AF = mybir.ActivationFunctionType
ALU = mybir.AluOpType

D = 1024
DFF = 4096
EPS = 1e-6
GV_MIX, GV_MLP, GV_FIN, GV_QG, GV_KG, GV_BGLU, GV_SSMD, GV_MASK, NG = 0, 8, 16, 24, 25, 26, 30, 34, 35


class Ev:
    __slots__ = ("sem", "val", "key")

    def __init__(s, sem, val, key):
        s.sem, s.val, s.key = sem, val, key


class Eng:
    def __init__(s, name, h, sem):
        s.name, s.h, s.sem = name, h, sem
        s.cnt = 0
        s.seen = {}
        s.last = None
        s.selfsync = True

    def wait(s, evs):
        if evs is None:
            return
        if isinstance(evs, Ev):
            evs = [evs]
        for ev in evs:
            if ev is None:
                continue
            if isinstance(ev, (list, tuple)):
                s.wait(ev)
                continue
            if ev.key == s.name and (s.name == "pe" or not s.selfsync):
                continue
            if s.seen.get(ev.key, 0) >= ev.val:
                continue
            s.h.wait_ge(ev.sem, ev.val)
            s.seen[ev.key] = ev.val

    def mark(s, inst):
        s.cnt += 1
        inst.then_inc(s.sem, 1)
        s.last = Ev(s.sem, s.cnt, s.name)
        return s.last


class Slot:
    def __init__(s, sem, key):
        s.sem, s.key = sem, key
        s.cnt = 0
        s.last = None


class Buf:
    def __init__(s, t, slot=None):
        s.t = t
        s.slot = slot
        s.wr = None
        s.rd = {}


class Ring:
    def __init__(s, bufs):
        s.bufs = bufs
        s.i = 0

    def next(s):
        b = s.bufs[s.i % len(s.bufs)]
        s.i += 1
        return b


def build(S, dbg=False):
    LP = S + 128
    NT = LP // 128
    NCH = LP // 8
    nc = bass.Bass("TRN2", target_bir_lowering=False)

    def din(name, shape, dt=F32):
        return nc.dram_tensor(name, shape, dt, kind="ExternalInput").ap()

    def dscr(name, shape, dt):
        return nc.dram_tensor(name, shape, dt, kind=("ExternalOutput" if dbg else "Internal")).ap()

    xT = din("xT", [D, LP])
    w_in = din("w_in", [D, 4096])
    w_glu = din("w_glu", [512, 512])
    w_sp = din("w_sp", [512, D])
    w_ap = din("w_ap", [D, D])
    w_out = din("w_out", [D, D])
    w1 = din("w1", [D, DFF])
    w2 = din("w2", [DFF, D])
    gv_d = din("gv", [128, NG])
    cmat_d = din("cmat", [128, 4 * 128])
    ropeC = din("ropeC", [128, LP])
    ropeS = din("ropeS", [128, LP])
    ssmA = din("ssmA", [128, 96])
    bz = din("bz", [32, 128, 512])
    outT = nc.dram_tensor("outT", [D, S], F32, kind="ExternalOutput").ap()

    winb = dscr("winb", [D, 4096], BF16)
    wglub = dscr("wglub", [512, 512], BF16)
    wspb = dscr("wspb", [512, D], BF16)
    wapb = dscr("wapb", [D, D], BF16)
    woutb = dscr("woutb", [D, D], BF16)
    w1b = dscr("w1b", [D, DFF], BF16)
    w2b = dscr("w2b", [DFF, D], BF16)
    qT_d = dscr("qT_d", [D, LP], BF16)
    zT_d = dscr("zT_d", [512, LP], BF16)
    yaT_d = dscr("yaT_d", [D, S], BF16)
    h1T_d = dscr("h1T_d", [D, S], F32)

    _uid = [0]

    def _sb(name, shape, dt):
        _uid[0] += 1
        return nc.sbuf_tensor(f"{name}_{_uid[0]}", shape, dt)

    def kview(ap):
        return ap.rearrange("(k p) l -> p k l", p=128)

    top = ExitStack()
    with top:
        sems = [top.enter_context(nc.semaphore(f"sem{i}")) for i in range(64)]
        PE = Eng("pe", nc.tensor, sems[0])
        ACT = Eng("act", nc.scalar, sems[1])
        DVE = Eng("dve", nc.vector, sems[2])
        POOL = Eng("pool", nc.gpsimd, sems[3])
        SP = Eng("sp", nc.sync, None)
        engines = [PE, ACT, DVE, POOL]
        slots = [Slot(sems[4 + i], f"dma{i}") for i in range(60)]
        slot_ptr = [0]

        def new_slot():
            s = slots[slot_ptr[0]]
            slot_ptr[0] += 1
            return s

        def op(E, fn, R=(), W=(), mark=True):
            for b in R:
                E.wait(b.wr)
            for b in W:
                E.wait(b.wr)
                E.wait(list(b.rd.values()))
            inst = fn()
            if mark:
                ev = E.mark(inst)
                for b in R:
                    b.rd[E.name] = ev
                for b in W:
                    b.wr = ev
                    b.rd = {}
                return ev
            return None

        def dma(slot, out, in_, R=(), W=(), upd=()):
            for b in R:
                SP.wait(b.wr)
            for b in W:
                SP.wait(b.wr)
                SP.wait(list(b.rd.values()))
            inst = nc.sync.dma_start(out=out, in_=in_)
            slot.cnt += 16
            inst.then_inc(slot.sem, 16)
            ev = Ev(slot.sem, slot.cnt, slot.key)
            slot.last = ev
            for b in R:
                b.rd[slot.key] = ev
            for b in list(W) + list(upd):
                b.wr = ev
                b.rd = {}
            return ev

        def mm_group(bank, mms, R=()):
            for b in R:
                PE.wait(b.wr)
            PE.wait(bank.wr)
            PE.wait(list(bank.rd.values()))
            n = len(mms)
            inst = None
            for i, m in enumerate(mms):
                if len(m) == 3:
                    st, sp_ = (i == 0), (i == n - 1)
                else:
                    st, sp_ = m[3], m[4]
                inst = nc.tensor.matmul(m[0], lhsT=m[1], rhs=m[2], start=st, stop=sp_)
            ev = PE.mark(inst)
            for b in R:
                b.rd[PE.name] = ev
            bank.wr = ev
            bank.rd = {}
            return ev

        def barrier():
            evs = [E.last for E in engines if E.last is not None]
            evs += [s.last for s in slots if s.last is not None]
            for E in engines + [SP]:
                E.wait(evs)

        def colblocks(n, w=512):
            return [(c, min(w, n - c)) for c in range(0, n, w)]

        gv_t = top.enter_context(_sb("gv", [128, NG], F32))
        cm_f = top.enter_context(_sb("cm_f", [128, 512], F32))
        cm_b = top.enter_context(_sb("cm_b", [128, 512], BF16))
        PSA = top.enter_context(nc.psum_tensor("psa", [128, 7, 512], F32))
        PSB = top.enter_context(nc.psum_tensor("psb", [128, 1024], BF16))
        gv = Buf(gv_t, new_slot())
        cmf = Buf(cm_f, new_slot())
        cmb = Buf(cm_b)
        dma(gv.slot, gv_t[:], gv_d, W=[gv])
        dma(cmf.slot, cm_f[:], cmat_d, W=[cmf])
        op(DVE, lambda: nc.vector.tensor_copy(out=cm_b[:], in_=cm_f[:]), R=[cmf], W=[cmb])
        ones_b = cm_b[:, 0:128]
        blk_b = cm_b[:, 128:256]
        rot_b = cm_b[:, 256:384]
        id_b = cm_b[:, 384:512]
        id_f = cm_f[:, 384:512]
        ones_f = cm_f[:, 0:128]
        DVE.wait(gv.wr)
        ACT.wait(gv.wr)
        POOL.wait(gv.wr)
        banks = [Buf(PSA[:, i, :]) for i in range(7)]
        bankB = Buf(PSB)
        base_slot_ptr = slot_ptr[0]

        def gcol(c):
            return gv_t[:, c:c + 1]

        with ExitStack() as ph:
            slot_ptr[0] = base_slot_ptr
            stg = Ring([Buf(ph.enter_context(_sb(f"wst{i}", [128, 2048], F32)), new_slot()) for i in range(3)])
            ost = Ring([Buf(ph.enter_context(_sb(f"wso{i}", [128, 2048], BF16)), new_slot()) for i in range(3)])
            rr = [0]

            def convert(src, dst, rows, cols, gain0=None):
                for k in range(rows // 128):
                    for c0 in range(0, cols, 2048):
                        n = min(2048, cols - c0)
                        st = stg.next()
                        dma(st.slot, st.t[:, :n], src[k * 128:(k + 1) * 128, c0:c0 + n], W=[st])
                        ob = ost.next()
                        which = rr[0] % 3
                        rr[0] += 1
                        if gain0 is None:
                            if which == 0:
                                op(DVE, lambda: nc.vector.tensor_copy(out=ob.t[:, :n], in_=st.t[:, :n]), R=[st], W=[ob])
                            elif which == 1:
                                op(POOL, lambda: nc.gpsimd.tensor_copy(out=ob.t[:, :n], in_=st.t[:, :n]), R=[st], W=[ob])
                            else:
                                op(ACT, lambda: nc.scalar.copy(out=ob.t[:, :n], in_=st.t[:, :n]), R=[st], W=[ob])
                        else:
                            g = gcol(gain0 + k)
                            if which == 0:
                                op(DVE, lambda: nc.vector.tensor_scalar(out=ob.t[:, :n], in0=st.t[:, :n], scalar1=g, scalar2=None, op0=ALU.mult), R=[st], W=[ob])
                            elif which == 1:
                                op(POOL, lambda: nc.gpsimd.tensor_scalar(out=ob.t[:, :n], in0=st.t[:, :n], scalar1=g, scalar2=None, op0=ALU.mult), R=[st], W=[ob])
                            else:
                                op(ACT, lambda: nc.scalar.activation(out=ob.t[:, :n], in_=st.t[:, :n], func=AF.Copy, scale=g), R=[st], W=[ob])
                        dma(ob.slot, dst[k * 128:(k + 1) * 128, c0:c0 + n], ob.t[:, :n], R=[ob])

            convert(w_in, winb, D, 4096, GV_MIX)
            barrier()

        def in_proj(ph, chunks, wcol0, wcols, uTb=None, kTb=None, Vxb=None):
            W_t = ph.enter_context(_sb("Wp", [128, 8, wcols], BF16))
            Wb = Buf(W_t, new_slot())
            dma(Wb.slot, W_t[:], kview(winb)[:, :, wcol0:wcol0 + wcols], W=[Wb])
            xring = Ring([Buf(ph.enter_context(_sb(f"xt{i}", [128, 8, 512], F32)), new_slot()) for i in range(2)])
            sqr = Ring([Buf(ph.enter_context(_sb(f"sq{i}", [128, 8, 512], BF16))) for i in range(1)])
            xbr = Ring([Buf(ph.enter_context(_sb(f"xb{i}", [128, 8, 512], BF16))) for i in range(2)])
            need_rope = any(c[0] in "qk" for c in chunks)
            if need_rope:
                csr = Ring([Buf(ph.enter_context(_sb(f"cs{i}", [128, 2, 512], F32)), new_slot()) for i in range(2)])
                qor = Ring([Buf(ph.enter_context(_sb(f"qo{i}", [128, 512], BF16)), new_slot()) for i in range(3)])
            f32r = Ring([Buf(ph.enter_context(_sb(f"f32r{i}", [128, 512], F32))) for i in range(12)])
            bfr = Ring([Buf(ph.enter_context(_sb(f"bfr{i}", [128, 512], BF16))) for i in range(6)])
            rstdr = Ring([Buf(ph.enter_context(_sb(f"rstd{i}", [128, 512], F32))) for i in range(2)])
            mainr = Ring(banks[0:3])
            ssb = banks[3]
            hsr = Ring(banks[4:5])
            rotr = Ring(banks[5:7])
            trb = bankB
            xTv = kview(xT)
            tiles_ = colblocks(LP)

            def prologue(ti):
                (c0, N) = tiles_[ti]
                xt = xring.next()
                dma(xt.slot, xt.t[:, :, :N], xTv[:, :, c0:c0 + N], W=[xt])
                cs = None
                if need_rope:
                    cs = csr.next()
                    dma(cs.slot, cs.t[:, 0, :N], ropeC[:, c0:c0 + N], W=[cs])
                    dma(cs.slot, cs.t[:, 1, :N], ropeS[:, c0:c0 + N], upd=[cs])
                sq = sqr.next()
                op(ACT, lambda: nc.scalar.activation(out=sq.t[:, :, :N], in_=xt.t[:, :, :N], func=AF.Square), R=[xt], W=[sq])
                xb = xbr.next()
                op(DVE, lambda: nc.vector.tensor_copy(out=xb.t[:, :, :N], in_=xt.t[:, :, :N]), R=[xt], W=[xb])
                mm_group(ssb, [(ssb.t[:, :N], ones_b, sq.t[:, k, :N]) for k in range(8)], R=[sq, cmb])
                rt = f32r.next()
                op(ACT, lambda: nc.scalar.activation(out=rt.t[:, :N], in_=ssb.t[:, :N], func=AF.Ln, scale=1.0 / D, bias=eps_c), R=[ssb], W=[rt])
                rstd = rstdr.next()
                op(ACT, lambda: nc.scalar.activation(out=rstd.t[:, :N], in_=rt.t[:, :N], func=AF.Exp, scale=-0.5), R=[rt], W=[rstd])
                return dict(c0=c0, N=N, cs=cs, xb=xb, rstd=rstd)

            def stageA(P, ch, stt):
                (kind, idx, wc) = ch
                c0, N, xb, rstd = P["c0"], P["N"], P["xb"], P["rstd"]
                ps = mainr.next()
                mm_group(ps, [(ps.t[:, :N], W_t[:, k, wc:wc + 128], xb.t[:, k, :N]) for k in range(8)], R=[Wb, xb])
                if kind == "u":
                    op(DVE, lambda: nc.vector.tensor_tensor(out=uTb.t[:, idx, :, c0 // 8:(c0 + N) // 8].rearrange("p i c -> p c i"), in0=ps.t[:, :N].rearrange("p (c i) -> p c i", i=8), in1=rstd.t[:, :N].rearrange("p (c i) -> p c i", i=8), op=ALU.mult), R=[ps, rstd], W=[uTb])
                elif kind == "v":
                    vt = bfr.next()
                    op(DVE, lambda: nc.vector.tensor_tensor(out=vt.t[:, :N], in0=ps.t[:, :N], in1=rstd.t[:, :N], op=ALU.mult), R=[ps, rstd], W=[vt])
                    stt["vt"] = vt
                else:
                    tq = f32r.next()
                    op(DVE, lambda: nc.vector.tensor_tensor(out=tq.t[:, :N], in0=ps.t[:, :N], in1=rstd.t[:, :N], op=ALU.mult), R=[ps, rstd], W=[tq])
                    stt["tq"] = tq

            def stageB(P, ch, stt):
                (kind, idx, wc) = ch
                c0, N = P["c0"], P["N"]
                if kind == "u":
                    return
                if kind == "v":
                    vt = stt["vt"]
                    for s_ in range(N // 128):
                        tt = c0 // 128 + s_
                        op(PE, lambda: nc.tensor.transpose(out=trb.t[:, s_ * 128:(s_ + 1) * 128], in_=vt.t[:, s_ * 128:(s_ + 1) * 128], identity=id_b), R=[vt, cmb], W=[trb])
                        op(ACT, lambda: nc.scalar.copy(out=Vxb.t[:, tt, 2 * idx:2 * idx + 2, 0:64],
                                                       in_=trb.t[:, s_ * 128:(s_ + 1) * 128].rearrange("p (g d) -> p g d", g=2)), R=[trb], W=[Vxb])
                    return
                tq = stt["tq"]
                sq2 = bfr.next()
                op(ACT, lambda: nc.scalar.activation(out=sq2.t[:, :N], in_=tq.t[:, :N], func=AF.Square), R=[tq], W=[sq2])
                hs = hsr.next()
                mm_group(hs, [(hs.t[:, :N], blk_b, sq2.t[:, :N])], R=[sq2, cmb])
                rt2 = f32r.next()
                op(ACT, lambda: nc.scalar.activation(out=rt2.t[:, :N], in_=hs.t[:, :N], func=AF.Ln, scale=1.0 / 64, bias=eps_c), R=[hs], W=[rt2])
                rq = f32r.next()
                op(ACT, lambda: nc.scalar.activation(out=rq.t[:, :N], in_=rt2.t[:, :N], func=AF.Exp, scale=-0.5), R=[rt2], W=[rq])
                tq2 = bfr.next()
                gc_ = gcol(GV_QG if kind == "q" else GV_KG)
                op(DVE, lambda: nc.vector.scalar_tensor_tensor(out=tq2.t[:, :N], in0=tq.t[:, :N], scalar=gc_, in1=rq.t[:, :N], op0=ALU.mult, op1=ALU.mult), R=[tq, rq], W=[tq2])
                stt["tq2"] = tq2

            def stageC(P, ch, stt):
                (kind, idx, wc) = ch
                c0, N, cs = P["c0"], P["N"], P["cs"]
                if kind in "uv":
                    return
                tq2 = stt["tq2"]
                rot = rotr.next()
                mm_group(rot, [(rot.t[:, :N], rot_b, tq2.t[:, :N])], R=[tq2, cmb])
                t1 = f32r.next()
                op(DVE, lambda: nc.vector.tensor_tensor(out=t1.t[:, :N], in0=tq2.t[:, :N], in1=cs.t[:, 0, :N], op=ALU.mult), R=[tq2, cs], W=[t1])
                t2 = f32r.next()
                op(DVE, lambda: nc.vector.tensor_tensor(out=t2.t[:, :N], in0=rot.t[:, :N], in1=cs.t[:, 1, :N], op=ALU.mult), R=[rot, cs], W=[t2])
                if kind == "k":
                    op(DVE, lambda: nc.vector.tensor_tensor(out=kTb.t[:, idx, c0:c0 + N], in0=t1.t[:, :N], in1=t2.t[:, :N], op=ALU.add), R=[t1, t2], W=[kTb])
                else:
                    qo = qor.next()
                    op(DVE, lambda: nc.vector.tensor_tensor(out=qo.t[:, :N], in0=t1.t[:, :N], in1=t2.t[:, :N], op=ALU.add), R=[t1, t2], W=[qo])
                    dma(qo.slot, qT_d[idx * 128:(idx + 1) * 128, c0:c0 + N], qo.t[:, :N], R=[qo])

            nch = len(chunks)
            Pn = prologue(0)
            for ti in range(len(tiles_)):
                P = Pn
                stts = [dict() for _ in range(nch)]
                for s_i in range(nch + 2):
                    if s_i < nch:
                        stageA(P, chunks[s_i], stts[s_i])
                        if s_i == nch - 1 and ti + 1 < len(tiles_):
                            Pn = prologue(ti + 1)
                    if 0 <= s_i - 1 < nch:
                        stageB(P, chunks[s_i - 1], stts[s_i - 1])
                    if 0 <= s_i - 2 < nch:
                        stageC(P, chunks[s_i - 2], stts[s_i - 2])

        eps_t = top.enter_context(_sb("eps", [128, 1], F32))
        epsb = Buf(eps_t)
        op(DVE, lambda: nc.vector.memset(eps_t[:], EPS), W=[epsb])
        ACT.wait(epsb.wr)
        eps_c = eps_t[:, 0:1]

        with ExitStack() as ph_prm:
            ph = ph_prm
            slot_ptr[0] = base_slot_ptr
            NPR = 108
            PR_t = ph.enter_context(_sb("PR", [128, NPR * 32], F32))
            PRb = Buf(PR_t, new_slot())
            pr_i = [0]

            def col():
                i = pr_i[0]
                pr_i[0] += 1
                assert i < NPR
                return PR_t[:, i * 32:(i + 1) * 32]

            A_re, A_im, LDT = col(), col(), col()
            dma(PRb.slot, PR_t[:, 0:96], ssmA, W=[PRb])

            def V(fn):
                return op(DVE, fn, R=[PRb], W=[PRb])

            def A(fn):
                return op(ACT, fn, R=[PRb], W=[PRb])

            def TT(o, a, b, o_):
                V(lambda: nc.vector.tensor_tensor(out=o, in0=a, in1=b, op=o_))

            def TS(o, a, s1, o1, s2=None, o2=None):
                if o2 is None:
                    V(lambda: nc.vector.tensor_scalar(out=o, in0=a, scalar1=s1, scalar2=None, op0=o1))
                else:
                    V(lambda: nc.vector.tensor_scalar(out=o, in0=a, scalar1=s1, scalar2=s2, op0=o1, op1=o2))

            t1, t2 = col(), col()

            def cmul(orr, oi, ar, ai, br, bi):
                TT(t1, ar, br, ALU.mult)
                TT(t2, ai, bi, ALU.mult)
                TT(orr, t1, t2, ALU.subtract)
                TT(t1, ar, bi, ALU.mult)
                TT(t2, ai, br, ALU.mult)
                TT(oi, t1, t2, ALU.add)

            dt_, lre, mag, ang, cs_, sn_, c2, s2 = col(), col(), col(), col(), col(), col(), col(), col()
            pio2 = col()
            V(lambda: nc.vector.memset(pio2, float(np.pi / 2)))
            A(lambda: nc.scalar.activation(out=dt_, in_=LDT, func=AF.Exp))
            TS(lre, A_re, -1e-4, ALU.min)
            TT(t1, lre, dt_, ALU.mult)
            A(lambda: nc.scalar.activation(out=mag, in_=t1, func=AF.Exp))
            TT(ang, A_im, dt_, ALU.mult)
            ki_t = ph.enter_context(_sb("ki", [128, 32], mybir.dt.int32))
            kf, rr_, mm_ = col(), col(), col()
            TS(kf, ang, float(1.0 / (2 * np.pi)), ALU.mult)
            V(lambda: nc.vector.tensor_copy(out=ki_t[:], in_=kf))
            V(lambda: nc.vector.tensor_copy(out=kf, in_=ki_t[:]))
            V(lambda: nc.vector.scalar_tensor_tensor(out=rr_, in0=kf, scalar=float(-2 * np.pi), in1=ang, op0=ALU.mult, op1=ALU.add))
            pi_c, npi_c = col(), col()
            V(lambda: nc.vector.memset(pi_c, float(np.pi)))
            V(lambda: nc.vector.memset(npi_c, float(-np.pi)))
            TT(mm_, rr_, pi_c, ALU.is_gt)
            V(lambda: nc.vector.scalar_tensor_tensor(out=rr_, in0=mm_, scalar=float(-2 * np.pi), in1=rr_, op0=ALU.mult, op1=ALU.add))
            TT(mm_, rr_, npi_c, ALU.is_lt)
            V(lambda: nc.vector.scalar_tensor_tensor(out=rr_, in0=mm_, scalar=float(2 * np.pi), in1=rr_, op0=ALU.mult, op1=ALU.add))
            TS(s2, rr_, -1.0, ALU.mult)
            TT(c2, rr_, s2, ALU.max)
            A(lambda: nc.scalar.activation(out=sn_, in_=rr_, func=AF.Sin))
            A(lambda: nc.scalar.activation(out=cs_, in_=c2, func=AF.Sin, scale=-1.0, bias=pio2[:, 0:1]))
            lbr, lbi = col(), col()
            TT(lbr, mag, cs_, ALU.mult)
            TT(lbi, mag, sn_, ALU.mult)
            numr, den, rden, fre, fim = col(), col(), col(), col(), col()
            TS(numr, lbr, -1.0, ALU.add)
            TT(t1, lre, lre, ALU.mult)
            TT(t2, A_im, A_im, ALU.mult)
            TT(den, t1, t2, ALU.add)
            V(lambda: nc.vector.reciprocal(out=rden, in_=den))
            TT(t1, numr, lre, ALU.mult)
            TT(t2, lbi, A_im, ALU.mult)
            TT(fre, t1, t2, ALU.add)
            TT(fre, fre, rden, ALU.mult)
            TT(t1, lbi, lre, ALU.mult)
            TT(t2, numr, A_im, ALU.mult)
            TT(fim, t1, t2, ALU.subtract)
            TT(fim, fim, rden, ALU.mult)
            Pre = [col() for _ in range(9)]
            Pim = [col() for _ in range(9)]
            NPim = [col() for _ in range(9)]
            V(lambda: nc.vector.memset(Pre[0], 1.0))
            V(lambda: nc.vector.memset(Pim[0], 0.0))
            for tau in range(8):
                cmul(Pre[tau + 1], Pim[tau + 1], Pre[tau], Pim[tau], lbr, lbi)
            for tau in range(9):
                TS(NPim[tau], Pim[tau], -1.0, ALU.mult)
            Gre = [col() for _ in range(8)]
            Gim = [col() for _ in range(8)]
            for tau in range(8):
                cmul(Gre[tau], Gim[tau], Pre[tau], Pim[tau], fre, fim)
            nlev = 0
            while (1 << nlev) < NCH:
                nlev += 1
            Hre = [col() for _ in range(nlev)]
            Him = [col() for _ in range(nlev)]
            NHim = [col() for _ in range(nlev)]
            V(lambda: nc.vector.tensor_copy(out=Hre[0], in_=Pre[8]))
            V(lambda: nc.vector.tensor_copy(out=Him[0], in_=Pim[8]))
            for k in range(1, nlev):
                cmul(Hre[k], Him[k], Hre[k - 1], Him[k - 1], Hre[k - 1], Him[k - 1])
            for k in range(nlev):
                TS(NHim[k], Him[k], -1.0, ALU.mult)
            KT_t = ph.enter_context(_sb("KT", [128, 2, 32, 8], F32))
            PT_t = ph.enter_context(_sb("PTt", [128, 3, 32, 9], F32))
            for i_ in range(8):
                for (ri, Garr) in ((0, Gre), (1, Gim)):
                    V(lambda: nc.vector.tensor_copy(out=KT_t[:, ri, 0:16, i_], in_=Garr[7 - i_][:, 0:16]))
                    V(lambda: nc.vector.tensor_copy(out=KT_t[:, ri, 16:32, i_], in_=Garr[i_][:, 16:32]))
            for tau_ in range(9):
                for (ri, Parr) in ((0, Pre), (1, Pim), (2, NPim)):
                    V(lambda: nc.vector.tensor_copy(out=PT_t[:, ri, :, tau_], in_=Parr[tau_]))
            ET_t = ph.enter_context(_sb("ET", [128, 2, 32, 16], F32))
            e_a = (col(), col())
            e_b = (col(), col())
            V(lambda: nc.vector.tensor_copy(out=e_a[0], in_=Pre[8]))
            V(lambda: nc.vector.tensor_copy(out=e_a[1], in_=Pim[8]))
            for m_ in range(1, 17):
                for ri in range(2):
                    V(lambda: nc.vector.tensor_copy(out=ET_t[:, ri, 0:16, m_ - 1], in_=e_a[ri][:, 0:16]))
                    V(lambda: nc.vector.tensor_copy(out=ET_t[:, ri, 16:32, 16 - m_], in_=e_a[ri][:, 16:32]))
                if m_ < 16:
                    cmul(e_b[0], e_b[1], e_a[0], e_a[1], Pre[8], Pim[8])
                    e_a, e_b = e_b, e_a
            ev_pr = V(lambda: nc.vector.tensor_copy(out=t1, in_=t2))
            for E in (ACT, POOL, PE):
                E.wait(ev_pr)
            if dbg:
                prdbg = nc.dram_tensor("prdbg", [128, NPR * 32], F32, kind="ExternalOutput").ap()
                dma(PRb.slot, prdbg, PR_t[:], R=[PRb])

            base_slot_ptr = slot_ptr[0]
            for hf in range(2):
              with ExitStack() as ph_ssm:
                uT_t = ph_ssm.enter_context(_sb(f"uT{hf}", [128, 2, 8, NCH], BF16))
                uTb = Buf(uT_t)
                with ExitStack() as ph:
                    slot_ptr[0] = base_slot_ptr
                    in_proj(ph, [("u", i, i * 128) for i in range(2)], hf * 256, 256, uTb=uTb)
                    barrier()
                slot_ptr[0] = base_slot_ptr
                ph = ph_ssm
                bzr = Ring([Buf(ph.enter_context(_sb(f"bz{i}", [128, 4, 128], F32)), new_slot()) for i in range(1)])
                Wst_r = Ring([Buf(ph.enter_context(_sb(f"Wst{i}", [128, 16, 128], BF16))) for i in range(1)])
                WAb = Buf(ph.enter_context(_sb("WA", [128, 16, 128], F32)))
                T1b = Buf(ph.enter_context(_sb("T1", [128, 16, 128], F32)))
                T2b = Buf(ph.enter_context(_sb("T2", [128, 16, 128], F32)))
                BbZ_t = ph.enter_context(_sb("BbZ", [128, 16, 128], BF16))
                BbZb = Buf(BbZ_t)
                CL_t = ph.enter_context(_sb("CL", [128, 8 * 18, 128], BF16))
                CLb = Buf(CL_t)
                BD_t = ph.enter_context(_sb("BD", [128, 16, 128], BF16))
                BDb = Buf(BD_t)
                XA_t = ph.enter_context(_sb("XA", [128, 2, NCH], F32))
                XB_t = ph.enter_context(_sb("XB", [128, 2, NCH], F32))
                XAb, XBb = Buf(XA_t), Buf(XB_t)
                NB = NCH // 16
                XXa_t = ph.enter_context(_sb("XXa", [128, 2, NB], F32))
                XXb_t = ph.enter_context(_sb("XXb", [128, 2, NB], F32))
                XXpf_t = ph.enter_context(_sb("XXpf", [128, 2, NB], F32))
                XXpb_t = ph.enter_context(_sb("XXpb", [128, 2, NB], F32))
                XXab, XXbb, XXpfb, XXpbb = Buf(XXa_t), Buf(XXb_t), Buf(XXpf_t), Buf(XXpb_t)
                op(DVE, lambda: nc.vector.memset(XXpf_t[:].rearrange("p a c -> p (a c)"), 0.0), W=[XXpfb])
                op(DVE, lambda: nc.vector.memset(XXpb_t[:].rearrange("p a c -> p (a c)"), 0.0), W=[XXpbb])
                Xb_t = ph.enter_context(_sb("Xb", [128, 8, 2, NCH], BF16))
                Xbb = Buf(Xb_t)
                zbuf_t = ph.enter_context(_sb("zbuf", [128, LP], BF16))
                zb = Buf(zbuf_t, new_slot())
                ytr = Ring([Buf(ph.enter_context(_sb(f"yt{i}", [128, 512], F32))) for i in range(2)])
                op(POOL, lambda: nc.gpsimd.memset(Xb_t[:].rearrange("p a b c -> p (a b c)"), 0.0), W=[Xbb])
                trr = Ring([banks[0], banks[3]])
                sring = Ring(banks[1:3])
                kring = Ring(banks[3:4])
                accr = Ring(banks[4:7])
                cbs = colblocks(NCH)

                for gcl in range(2):
                    gc = hf * 2 + gcl
                    uv = uT_t[:, gcl, :, :].rearrange("p i c -> p c i")
                    for gpl in range(4):
                        gp = gc * 4 + gpl
                        for d in range(2):
                            cidx = d * 16 + gp
                            slot8 = gpl * 2 + d
                            bzb = bzr.next()
                            dma(bzb.slot, bzb.t[:], bz[cidx].rearrange("p (a c) -> p a c", a=4), W=[bzb])
                            Bre, Bim, Cre, Cim = (bzb.t[:, a, :] for a in range(4))
                            Wst = Wst_r.next()

                            def sc(arr):
                                return arr[:, cidx:cidx + 1]

                            wa = WAb
                            kre = KT_t[:, 0, cidx, :].unsqueeze(2).unsqueeze(3).broadcast_to([128, 8, 2, 128])
                            kim2 = KT_t[:, 1, cidx, :].unsqueeze(2).broadcast_to([128, 8, 128])
                            bpair = bzb.t[:, 0:2, :].unsqueeze(1).broadcast_to([128, 8, 2, 128])
                            bre8 = Bre.unsqueeze(1).broadcast_to([128, 8, 128])
                            bim8 = Bim.unsqueeze(1).broadcast_to([128, 8, 128])
                            T1v = T1b.t[:].rearrange("p (i a) c -> p i a c", a=2)
                            T2v = T2b.t[:].rearrange("p (i a) c -> p i a c", a=2)
                            WAv = wa.t[:].rearrange("p (i a) c -> p i a c", a=2)
                            op(DVE, lambda: nc.vector.tensor_tensor(out=T1v, in0=bpair, in1=kre, op=ALU.mult), R=[bzb], W=[T1b])
                            op(DVE, lambda: nc.vector.tensor_tensor(out=T2v[:, :, 0, :], in0=bim8, in1=kim2, op=ALU.mult), R=[bzb], W=[T2b])
                            op(DVE, lambda: nc.vector.tensor_tensor(out=T2v[:, :, 1, :], in0=bre8, in1=kim2, op=ALU.mult), R=[bzb], W=[T2b])
                            op(DVE, lambda: nc.vector.tensor_tensor(out=WAv[:, :, 0, :], in0=T1v[:, :, 0, :], in1=T2v[:, :, 0, :], op=ALU.subtract), R=[T1b, T2b], W=[wa])
                            op(DVE, lambda: nc.vector.tensor_tensor(out=WAv[:, :, 1, :], in0=T1v[:, :, 1, :], in1=T2v[:, :, 1, :], op=ALU.add), R=[T1b, T2b], W=[wa])
                            i0 = 7 if d == 0 else 0
                            op(DVE, lambda: nc.vector.tensor_copy(out=BbZ_t[:, slot8 * 2:slot8 * 2 + 2, :], in_=wa.t[:, i0 * 2:i0 * 2 + 2, :]), R=[wa], W=[BbZb])
                            for q4 in range(4):
                                tb = trr.next()
                                for m4 in range(4):
                                    idx = q4 * 4 + m4
                                    op(PE, lambda: nc.tensor.transpose(out=tb.t[:, m4 * 128:(m4 + 1) * 128], in_=wa.t[:, idx, :], identity=id_f), R=[wa, cmf], W=[tb], mark=(m4 == 3))
                                op(ACT, lambda: nc.scalar.copy(out=Wst.t[:, q4 * 4:(q4 + 1) * 4, :], in_=tb.t[:].rearrange("p (m c) -> p m c", m=4)), R=[tb], W=[Wst])
                            pre9 = PT_t[:, 0, cidx, :].unsqueeze(2).broadcast_to([128, 9, 128])
                            pim9 = PT_t[:, 1, cidx, :].unsqueeze(2).broadcast_to([128, 9, 128])
                            npim9 = PT_t[:, 2, cidx, :].unsqueeze(2).broadcast_to([128, 9, 128])
                            cre9 = Cre.unsqueeze(1).broadcast_to([128, 9, 128])
                            cim9 = Cim.unsqueeze(1).broadcast_to([128, 9, 128])
                            A1 = T1b.t[:, 0:9, :]
                            A2 = T2b.t[:, 0:9, :]
                            CLv = CL_t[:, slot8 * 18:(slot8 + 1) * 18, :].rearrange("p (t a) c -> p t a c", a=2)
                            op(DVE, lambda: nc.vector.tensor_tensor(out=A1, in0=cre9, in1=pre9, op=ALU.mult), R=[bzb], W=[T1b])
                            op(DVE, lambda: nc.vector.tensor_tensor(out=A2, in0=cim9, in1=pim9, op=ALU.mult), R=[bzb], W=[T2b])
                            op(DVE, lambda: nc.vector.tensor_tensor(out=CLv[:, :, 0, :], in0=A1, in1=A2, op=ALU.subtract), R=[T1b, T2b], W=[CLb])
                            op(DVE, lambda: nc.vector.tensor_tensor(out=A1, in0=cre9, in1=npim9, op=ALU.mult), R=[bzb], W=[T1b])
                            op(DVE, lambda: nc.vector.tensor_tensor(out=A2, in0=cim9, in1=pre9, op=ALU.mult), R=[bzb], W=[T2b])
                            op(DVE, lambda: nc.vector.tensor_tensor(out=CLv[:, :, 1, :], in0=A1, in1=A2, op=ALU.subtract), R=[T1b, T2b], W=[CLb])
                            for part in range(2):
                                for (cb0, n) in cbs:
                                    sbk = sring.next()
                                    mm_group(sbk, [(sbk.t[:, :n], Wst.t[:, i * 2 + part, :], uv[:, cb0:cb0 + n, i]) for i in range(8)], R=[Wst, uTb])
                                    op(ACT, lambda: nc.scalar.copy(out=XA_t[:, part, cb0:cb0 + n], in_=sbk.t[:, :n]), R=[sbk], W=[XAb])
                            src, dst = XAb, XBb
                            for k in range(4):
                                s_ = 1 << k
                                a_, b_, nb_ = sc(Hre[k]), sc(Him[k]), sc(NHim[k])
                                X = src.t[:].rearrange("p a (c k) -> p a c k", k=16)
                                Y = dst.t[:].rearrange("p a (c k) -> p a c k", k=16)
                                if d == 0:
                                    lo, hi = slice(0, 16 - s_), slice(s_, 16)
                                    keep = slice(0, s_)
                                else:
                                    lo, hi = slice(s_, 16), slice(0, 16 - s_)
                                    keep = slice(16 - s_, 16)
                                op(ACT, lambda: nc.scalar.copy(out=Y[:, :, :, keep], in_=X[:, :, :, keep]), R=[src], W=[dst])
                                op(DVE, lambda: nc.vector.scalar_tensor_tensor(out=Y[:, 0, :, hi], in0=X[:, 0, :, lo], scalar=a_, in1=X[:, 0, :, hi], op0=ALU.mult, op1=ALU.add), R=[src], W=[dst])
                                op(DVE, lambda: nc.vector.scalar_tensor_tensor(out=Y[:, 0, :, hi], in0=X[:, 1, :, lo], scalar=nb_, in1=Y[:, 0, :, hi], op0=ALU.mult, op1=ALU.add), R=[src], W=[dst])
                                op(DVE, lambda: nc.vector.scalar_tensor_tensor(out=Y[:, 1, :, hi], in0=X[:, 0, :, lo], scalar=b_, in1=X[:, 1, :, hi], op0=ALU.mult, op1=ALU.add), R=[src], W=[dst])
                                op(DVE, lambda: nc.vector.scalar_tensor_tensor(out=Y[:, 1, :, hi], in0=X[:, 1, :, lo], scalar=a_, in1=Y[:, 1, :, hi], op0=ALU.mult, op1=ALU.add), R=[src], W=[dst])
                                src, dst = dst, src
                            L4 = src.t[:].rearrange("p a (c k) -> p a c k", k=16)
                            xs, xd = XXab, XXbb
                            op(DVE, lambda: nc.vector.tensor_copy(out=xs.t[:], in_=L4[:, :, :, 15 if d == 0 else 0]), R=[src], W=[xs])
                            k = 0
                            while (1 << k) < NB:
                                s_ = 1 << k
                                a_, b_, nb_ = sc(Hre[4 + k]), sc(Him[4 + k]), sc(NHim[4 + k])
                                X, Y = xs.t, xd.t
                                if d == 0:
                                    lo, hi = slice(0, NB - s_), slice(s_, NB)
                                    keep = slice(0, s_)
                                else:
                                    lo, hi = slice(s_, NB), slice(0, NB - s_)
                                    keep = slice(NB - s_, NB)
                                op(DVE, lambda: nc.vector.tensor_copy(out=Y[:, :, keep], in_=X[:, :, keep]), R=[xs], W=[xd])
                                op(DVE, lambda: nc.vector.scalar_tensor_tensor(out=Y[:, 0, hi], in0=X[:, 0, lo], scalar=a_, in1=X[:, 0, hi], op0=ALU.mult, op1=ALU.add), R=[xs], W=[xd])
                                op(DVE, lambda: nc.vector.scalar_tensor_tensor(out=Y[:, 0, hi], in0=X[:, 1, lo], scalar=nb_, in1=Y[:, 0, hi], op0=ALU.mult, op1=ALU.add), R=[xs], W=[xd])
                                op(DVE, lambda: nc.vector.scalar_tensor_tensor(out=Y[:, 1, hi], in0=X[:, 0, lo], scalar=b_, in1=X[:, 1, hi], op0=ALU.mult, op1=ALU.add), R=[xs], W=[xd])
                                op(DVE, lambda: nc.vector.scalar_tensor_tensor(out=Y[:, 1, hi], in0=X[:, 1, lo], scalar=a_, in1=Y[:, 1, hi], op0=ALU.mult, op1=ALU.add), R=[xs], W=[xd])
                                xs, xd = xd, xs
                                k += 1
                            if d == 0:
                                xpb = XXpfb
                                op(DVE, lambda: nc.vector.tensor_copy(out=xpb.t[:, :, 1:NB], in_=xs.t[:, :, 0:NB - 1]), R=[xs], W=[xpb])
                            else:
                                xpb = XXpbb
                                op(DVE, lambda: nc.vector.tensor_copy(out=xpb.t[:, :, 0:NB - 1], in_=xs.t[:, :, 1:NB]), R=[xs], W=[xpb])
                            etr = ET_t[:, 0, cidx, :].unsqueeze(1).broadcast_to([128, NB, 16])
                            eti = ET_t[:, 1, cidx, :].unsqueeze(1).broadcast_to([128, NB, 16])
                            xpr = xpb.t[:, 0, :].unsqueeze(2).broadcast_to([128, NB, 16])
                            xpi = xpb.t[:, 1, :].unsqueeze(2).broadcast_to([128, NB, 16])
                            e1 = T1b.t[:].rearrange("p a c -> p (a c)")[:, 0:NCH].rearrange("p (c k) -> p c k", k=16)
                            e2 = T2b.t[:].rearrange("p a c -> p (a c)")[:, 0:NCH].rearrange("p (c k) -> p c k", k=16)
                            op(DVE, lambda: nc.vector.tensor_tensor(out=e1, in0=etr, in1=xpr, op=ALU.mult), R=[xpb], W=[T1b])
                            op(DVE, lambda: nc.vector.tensor_tensor(out=e2, in0=eti, in1=xpi, op=ALU.mult), R=[xpb], W=[T2b])
                            op(DVE, lambda: nc.vector.tensor_tensor(out=e1, in0=e1, in1=e2, op=ALU.subtract), R=[T2b], W=[T1b])
                            op(DVE, lambda: nc.vector.tensor_tensor(out=L4[:, 0, :, :], in0=L4[:, 0, :, :], in1=e1, op=ALU.add), R=[T1b], W=[src])
                            op(DVE, lambda: nc.vector.tensor_tensor(out=e1, in0=etr, in1=xpi, op=ALU.mult), R=[xpb], W=[T1b])
                            op(DVE, lambda: nc.vector.tensor_tensor(out=e2, in0=eti, in1=xpr, op=ALU.mult), R=[xpb], W=[T2b])
                            op(DVE, lambda: nc.vector.tensor_tensor(out=e1, in0=e1, in1=e2, op=ALU.add), R=[T2b], W=[T1b])
                            op(DVE, lambda: nc.vector.tensor_tensor(out=L4[:, 1, :, :], in0=L4[:, 1, :, :], in1=e1, op=ALU.add), R=[T1b], W=[src])
                            Z = src.t
                            if d == 0:
                                op(ACT, lambda: nc.scalar.copy(out=Xb_t[:, slot8, :, 1:NCH], in_=Z[:, :, 0:NCH - 1]), R=[src], W=[Xbb])
                            else:
                                op(ACT, lambda: nc.scalar.copy(out=Xb_t[:, slot8, :, 0:NCH - 1], in_=Z[:, :, 1:NCH]), R=[src], W=[Xbb])
                    for d in range(2):
                        for tau in range(8):
                            kb = kring.next()
                            mms = []
                            for gpl in range(4):
                                s8 = gpl * 2 + d
                                for part in range(2):
                                    mms.append((kb.t[:, 0:128], BbZ_t[:, s8 * 2 + part, :], CL_t[:, s8 * 18 + tau * 2 + part, :]))
                            mm_group(kb, mms, R=[BbZb, CLb])
                            op(ACT, lambda: nc.scalar.copy(out=BD_t[:, d * 8 + tau, :], in_=kb.t[:, 0:128]), R=[kb], W=[BDb])
                    zv = zbuf_t[:].rearrange("p (c i) -> p c i", i=8)
                    for j in range(8):
                        for (cb0, n) in cbs:
                            ab = accr.next()
                            mms = []
                            for i in range(0, j + 1):
                                mms.append((ab.t[:, :n], BD_t[:, 0 * 8 + (j - i), :], uv[:, cb0:cb0 + n, i]))
                            for i in range(j, 8):
                                mms.append((ab.t[:, :n], BD_t[:, 1 * 8 + (i - j), :], uv[:, cb0:cb0 + n, i]))
                            for gpl in range(4):
                                for d in range(2):
                                    s8 = gpl * 2 + d
                                    tau = (j + 1) if d == 0 else (8 - j)
                                    for part in range(2):
                                        mms.append((ab.t[:, :n], CL_t[:, s8 * 18 + tau * 2 + part, :], Xb_t[:, s8, part, cb0:cb0 + n]))
                            mm_group(ab, mms, R=[BDb, uTb, CLb, Xbb])
                            yt = ytr.next()
                            op(DVE, lambda: nc.vector.scalar_tensor_tensor(out=yt.t[:, :n], in0=uv[:, cb0:cb0 + n, j], scalar=gcol(GV_SSMD + gc), in1=ab.t[:, :n], op0=ALU.mult, op1=ALU.add), R=[ab, uTb], W=[yt])
                            op(ACT, lambda: nc.scalar.activation(out=zv[:, cb0:cb0 + n, j], in_=yt.t[:, :n], func=AF.Gelu), R=[yt], W=[zb])
                    dma(zb.slot, zT_d[gc * 128:(gc + 1) * 128, :], zbuf_t[:], R=[zb])
                barrier()

        conv_jobs = []
        for (src_, dst_, rows_, cols_, g0_) in [(w_glu, wglub, 512, 512, None), (w_sp, wspb, 512, D, None), (w_ap, wapb, D, D, None),
                                                  (w_out, woutb, D, D, None), (w1, w1b, D, DFF, GV_MLP), (w2, w2b, DFF, D, None)]:
            for k_ in range(rows_ // 128):
                for c0_ in range(0, cols_, 2048):
                    conv_jobs.append((src_, dst_, k_, c0_, min(2048, cols_ - c0_), g0_))
        conv_state = {"i": 0, "stg": None, "ost": None}

        def conv_step(nj):
            for _ in range(nj):
                if conv_state["i"] >= len(conv_jobs):
                    return
                (src_, dst_, k_, c0_, n_, g0_) = conv_jobs[conv_state["i"]]
                conv_state["i"] += 1
                st = conv_state["stg"].next()
                dma(st.slot, st.t[:, :n_], src_[k_ * 128:(k_ + 1) * 128, c0_:c0_ + n_], W=[st])
                ob = conv_state["ost"].next()
                if g0_ is None:
                    op(DVE, lambda: nc.vector.tensor_copy(out=ob.t[:, :n_], in_=st.t[:, :n_]), R=[st], W=[ob])
                else:
                    op(DVE, lambda: nc.vector.tensor_scalar(out=ob.t[:, :n_], in0=st.t[:, :n_], scalar1=gcol(g0_ + k_), scalar2=None, op0=ALU.mult), R=[st], W=[ob])
                dma(ob.slot, dst_[k_ * 128:(k_ + 1) * 128, c0_:c0_ + n_], ob.t[:, :n_], R=[ob])

        with ExitStack() as ph_att:
            kT_t = ph_att.enter_context(_sb("kT", [128, 2, LP], BF16))
            Vx_t = ph_att.enter_context(_sb("Vx", [128, NT, 4, 65], BF16))
            kTb, Vxb = Buf(kT_t), Buf(Vx_t)
            op(POOL, lambda: nc.gpsimd.memset(Vx_t[:].rearrange("p a b c -> p (a b c)"), 1.0), W=[Vxb])
            with ExitStack() as ph:
                slot_ptr[0] = base_slot_ptr
                chunks = [("q", i, i * 128) for i in range(8)] + [("k", i, 1024 + i * 128) for i in range(2)] + [("v", i, 1280 + i * 128) for i in range(2)]
                in_proj(ph, chunks, 512, 1536, kTb=kTb, Vxb=Vxb)
                barrier()
            slot_ptr[0] = base_slot_ptr
            ph = ph_att
            conv_state["stg"] = Ring([Buf(ph.enter_context(_sb(f"cst{i}", [128, 2048], F32)), new_slot()) for i in range(3)])
            conv_state["ost"] = Ring([Buf(ph.enter_context(_sb(f"cso{i}", [128, 2048], BF16)), new_slot()) for i in range(3)])
            Qr = Ring([Buf(ph.enter_context(_sb(f"Q{i}", [128, LP], BF16)), new_slot()) for i in range(2)])
            PTr = Ring([Buf(ph.enter_context(_sb(f"PT{i}", [128, 2, 512], BF16))) for i in range(3)])
            recb = Buf(ph.enter_context(_sb("rec", [128, 512], F32)))
            yor = Ring([Buf(ph.enter_context(_sb(f"yo{i}", [128, 512], BF16)), new_slot()) for i in range(2)])
            Sr = Ring([Buf(PSA[:, 0:2, :]), Buf(PSA[:, 2:4, :])])
            Or = Ring([banks[4], banks[6]])
            DENb = banks[5]
            pairs = [(h, h + 4) for h in (0, 1, 2, 3)] + [(h, h + 4) for h in (8, 9, 10, 11)]
            for (ha, hb) in pairs:
                ga, gb = ha // 4, hb // 4
                kc = ga // 2
                Qb = Qr.next()
                dma(Qb.slot, Qb.t[0:64, :], qT_d[ha * 64:(ha + 1) * 64, :], W=[Qb])
                dma(Qb.slot, Qb.t[64:128, :], qT_d[hb * 64:(hb + 1) * 64, :], upd=[Qb])
                for (q0, NQ) in colblocks(S):
                    qc0 = 128 + q0
                    Ob = Or.next()
                    pend = None
                    first_pv = True

                    def do_pv(pend, last):
                        nonlocal first_pv
                        PTb, kt = pend
                        f_ = first_pv
                        first_pv = False
                        for b in (PTb, Vxb, cmb):
                            PE.wait(b.wr)
                        if f_:
                            for bk in (Ob, DENb):
                                PE.wait(bk.wr)
                                PE.wait(list(bk.rd.values()))
                        nc.tensor.matmul(Ob.t[0:64, :NQ], lhsT=Vx_t[:, kt, ga, 0:64], rhs=PTb.t[:, 0, :NQ], start=f_, stop=last)
                        nc.tensor.matmul(Ob.t[64:128, :NQ], lhsT=Vx_t[:, kt, gb, 0:64], rhs=PTb.t[:, 1, :NQ], start=f_, stop=last)
                        nc.tensor.matmul(DENb.t[0:64, :NQ], lhsT=ones_b[:, 0:64], rhs=PTb.t[:, 0, :NQ], start=f_, stop=last)
                        inst = nc.tensor.matmul(DENb.t[64:128, :NQ], lhsT=ones_b[:, 0:64], rhs=PTb.t[:, 1, :NQ], start=f_, stop=last)
                        ev = PE.mark(inst)
                        PTb.rd[PE.name] = ev
                        if last:
                            for bk in (Ob, DENb):
                                bk.wr = ev
                                bk.rd = {}

                    for kt in range(NT):
                        Sb = Sr.next()
                        mms = [(Sb.t[:, 0, :NQ], kT_t[0:64, kc, kt * 128:(kt + 1) * 128], Qb.t[0:64, qc0:qc0 + NQ], True, True),
                               (Sb.t[:, 1, :NQ], kT_t[64:128, kc, kt * 128:(kt + 1) * 128], Qb.t[64:128, qc0:qc0 + NQ], True, True)]
                        mm_group(Sb, mms, R=[kTb, Qb])
                        if pend is not None:
                            do_pv(pend, False)
                        PTb = PTr.next()
                        if kt == 0:
                            op(ACT, lambda: nc.scalar.activation(out=PTb.t[:, :, :NQ], in_=Sb.t[:, :, :NQ], func=AF.Exp, scale=0.125, bias=gcol(GV_MASK)), R=[Sb], W=[PTb])
                        else:
                            op(ACT, lambda: nc.scalar.activation(out=PTb.t[:, :, :NQ], in_=Sb.t[:, :, :NQ], func=AF.Exp, scale=0.125), R=[Sb], W=[PTb])
                        pend = (PTb, kt)
                    do_pv(pend, True)
                    op(DVE, lambda: nc.vector.reciprocal(out=recb.t[:, :NQ], in_=DENb.t[:, :NQ]), R=[DENb], W=[recb])
                    yo = yor.next()
                    op(DVE, lambda: nc.vector.tensor_tensor(out=yo.t[:, :NQ], in0=Ob.t[:, :NQ], in1=recb.t[:, :NQ], op=ALU.mult), R=[Ob, recb], W=[yo])
                    dma(yo.slot, yaT_d[ha * 64:(ha + 1) * 64, q0:q0 + NQ], yo.t[0:64, :NQ], R=[yo])
                    dma(yo.slot, yaT_d[hb * 64:(hb + 1) * 64, q0:q0 + NQ], yo.t[64:128, :NQ], R=[yo])
                    conv_step(1)
            conv_step(len(conv_jobs))
            barrier()

        with ExitStack() as ph:
            slot_ptr[0] = base_slot_ptr

            def wload(name, src, kchunks, cols, c0=0):
                t = ph.enter_context(_sb(name, [128, kchunks, cols], BF16))
                b = Buf(t, new_slot())
                dma(b.slot, t[:], kview(src)[:, :, c0:c0 + cols], W=[b])
                return b
            Wg = wload("Wg", winb, 8, 2048, 2048)
            Wgl = wload("Wgl", wglub, 4, 512)
            Wsp = wload("Wsp", wspb, 4, D)
            Wap = wload("Wap", wapb, 8, D)
            Wo = wload("Wo", woutb, 8, D)
            xring = Ring([Buf(ph.enter_context(_sb(f"xt{i}", [128, 8, 512], F32)), new_slot()) for i in range(2)])
            sqb_ = Buf(ph.enter_context(_sb("sq3", [128, 8, 512], BF16)))
            xbb_ = Buf(ph.enter_context(_sb("xb3", [128, 8, 512], BF16)))
            ztr = Ring([Buf(ph.enter_context(_sb(f"zt{i}", [128, 4, 512], BF16)), new_slot()) for i in range(2)])
            yar = Ring([Buf(ph.enter_context(_sb(f"ya{i}", [128, 8, 512], BF16)), new_slot()) for i in range(2)])
            ysb = Buf(ph.enter_context(_sb("ys", [128, 4, 512], BF16)))
            mgb = Buf(ph.enter_context(_sb("mg", [128, 8, 512], BF16)))
            f32r = Ring([Buf(ph.enter_context(_sb(f"f3r{i}", [128, 512], F32))) for i in range(10)])
            rstdb = Buf(ph.enter_context(_sb("rstd3", [128, 512], F32)))
            h1r = Ring([Buf(ph.enter_context(_sb(f"h1s{i}", [128, 512], F32)), new_slot()) for i in range(3)])
            pr = Ring(banks)
            xTv = kview(xT)
            zTv = kview(zT_d)
            yaTv = kview(yaT_d)
            N = 512
            rstdr3 = Ring([rstdb, Buf(ph.enter_context(_sb("rstd3b", [128, 512], F32)))])
            tiles3 = colblocks(S)

            def prologue3(ti):
                q0 = tiles3[ti][0]
                c0 = 128 + q0
                xt = xring.next()
                dma(xt.slot, xt.t[:], xTv[:, :, c0:c0 + N], W=[xt])
                zt = ztr.next()
                dma(zt.slot, zt.t[:], zTv[:, :, c0:c0 + N], W=[zt])
                ya = yar.next()
                dma(ya.slot, ya.t[:], yaTv[:, :, q0:q0 + N], W=[ya])
                op(ACT, lambda: nc.scalar.activation(out=sqb_.t[:], in_=xt.t[:], func=AF.Square), R=[xt], W=[sqb_])
                op(DVE, lambda: nc.vector.tensor_copy(out=xbb_.t[:], in_=xt.t[:]), R=[xt], W=[xbb_])
                ssb = pr.next()
                mm_group(ssb, [(ssb.t[:], ones_b, sqb_.t[:, k, :]) for k in range(8)], R=[sqb_, cmb])
                rt = f32r.next()
                op(ACT, lambda: nc.scalar.activation(out=rt.t[:], in_=ssb.t[:], func=AF.Ln, scale=1.0 / D, bias=eps_c), R=[ssb], W=[rt])
                rs_ = rstdr3.next()
                op(ACT, lambda: nc.scalar.activation(out=rs_.t[:], in_=rt.t[:], func=AF.Exp, scale=-0.5), R=[rt], W=[rs_])
                return dict(q0=q0, xt=xt, zt=zt, ya=ya, rstd=rs_)

            def main3(P):
                zt, ya, rs_ = P["zt"], P["ya"], P["rstd"]
                for oc in range(4):
                    ps = pr.next()
                    mm_group(ps, [(ps.t[:], Wgl.t[:, k, oc * 128:(oc + 1) * 128], zt.t[:, k, :]) for k in range(4)], R=[Wgl, zt])
                    sg = f32r.next()
                    op(ACT, lambda: nc.scalar.activation(out=sg.t[:], in_=ps.t[:], func=AF.Sigmoid, bias=gcol(GV_BGLU + oc)), R=[ps], W=[sg])
                    op(DVE, lambda: nc.vector.tensor_tensor(out=ysb.t[:, oc, :], in0=zt.t[:, oc, :], in1=sg.t[:], op=ALU.mult), R=[zt, sg], W=[ysb])
                for oc in range(8):
                    sgs = []
                    for gi in range(2):
                        ps = pr.next()
                        wc = gi * 1024 + oc * 128
                        mm_group(ps, [(ps.t[:], Wg.t[:, k, wc:wc + 128], xbb_.t[:, k, :]) for k in range(8)], R=[Wg, xbb_])
                        tg = f32r.next()
                        op(DVE, lambda: nc.vector.tensor_tensor(out=tg.t[:], in0=ps.t[:], in1=rs_.t[:], op=ALU.mult), R=[ps, rs_], W=[tg])
                        sg = f32r.next()
                        op(ACT, lambda: nc.scalar.activation(out=sg.t[:], in_=tg.t[:], func=AF.Sigmoid), R=[tg], W=[sg])
                        sgs.append(sg)
                    ps = pr.next()
                    mm_group(ps, [(ps.t[:], Wsp.t[:, k, oc * 128:(oc + 1) * 128], ysb.t[:, k, :]) for k in range(4)], R=[Wsp, ysb])
                    m1 = f32r.next()
                    op(DVE, lambda: nc.vector.tensor_tensor(out=m1.t[:], in0=ps.t[:], in1=sgs[0].t[:], op=ALU.mult), R=[ps, sgs[0]], W=[m1])
                    ps2 = pr.next()
                    mm_group(ps2, [(ps2.t[:], Wap.t[:, k, oc * 128:(oc + 1) * 128], ya.t[:, k, :]) for k in range(8)], R=[Wap, ya])
                    m2 = f32r.next()
                    op(DVE, lambda: nc.vector.tensor_tensor(out=m2.t[:], in0=ps2.t[:], in1=sgs[1].t[:], op=ALU.mult), R=[ps2, sgs[1]], W=[m2])
                    op(DVE, lambda: nc.vector.tensor_tensor(out=mgb.t[:, oc, :], in0=m1.t[:], in1=m2.t[:], op=ALU.add), R=[m1, m2], W=[mgb])

            def outproj3(P):
                q0, xt = P["q0"], P["xt"]
                for oc in range(8):
                    ps = pr.next()
                    mm_group(ps, [(ps.t[:], Wo.t[:, k, oc * 128:(oc + 1) * 128], mgb.t[:, k, :]) for k in range(8)], R=[Wo, mgb])
                    h1 = h1r.next()
                    op(DVE, lambda: nc.vector.tensor_tensor(out=h1.t[:], in0=ps.t[:], in1=xt.t[:, oc, :], op=ALU.add), R=[ps, xt], W=[h1])
                    dma(h1.slot, h1T_d[oc * 128:(oc + 1) * 128, q0:q0 + N], h1.t[:], R=[h1])

            Pn = prologue3(0)
            for ti in range(len(tiles3)):
                P = Pn
                main3(P)
                if ti + 1 < len(tiles3):
                    Pn = prologue3(ti + 1)
                outproj3(P)
            barrier()

        with ExitStack() as ph:
            slot_ptr[0] = base_slot_ptr
            W1_t = ph.enter_context(_sb("W1", [128, 8, DFF], BF16))
            W2_t = ph.enter_context(_sb("W2", [128, 32, D], BF16))
            W1b_, W2b_ = Buf(W1_t, new_slot()), Buf(W2_t, new_slot())
            dma(W1b_.slot, W1_t[:], kview(w1b), W=[W1b_])
            dma(W2b_.slot, W2_t[:], kview(w2b), W=[W2b_])
            N = 256
            hr = Ring([Buf(ph.enter_context(_sb(f"h{i}", [128, 8, N], F32)), new_slot()) for i in range(2)])
            sqb_ = Buf(ph.enter_context(_sb("sq4", [128, 8, N], BF16)))
            hbb = Buf(ph.enter_context(_sb("hb4", [128, 8, N], BF16)))
            Ab = Buf(ph.enter_context(_sb("A4", [128, 32, N], BF16)))
            h2b = Buf(ph.enter_context(_sb("h2", [128, 8, N], F32)))
            osr = Ring([Buf(ph.enter_context(_sb(f"os{i}", [128, 8, N], F32)), new_slot()) for i in range(1)])
            f32r = Ring([Buf(ph.enter_context(_sb(f"f4r{i}", [128, N], F32))) for i in range(6)])
            rstdb = Buf(ph.enter_context(_sb("rstd4", [128, N], F32)))
            rstd5 = Buf(ph.enter_context(_sb("rstd5", [128, N], F32)))
            pr = Ring(banks)
            h1v = kview(h1T_d)
            outv = kview(outT)
            sq5b = Buf(ph.enter_context(_sb("sq5", [128, 8, N], BF16)))
            rstdr4 = Ring([rstdb, Buf(ph.enter_context(_sb("rstd4b", [128, N], F32)))])
            tiles4 = colblocks(S, N)

            def prologue4(ti):
                q0 = tiles4[ti][0]
                hb = hr.next()
                dma(hb.slot, hb.t[:], h1v[:, :, q0:q0 + N], W=[hb])
                op(ACT, lambda: nc.scalar.activation(out=sqb_.t[:], in_=hb.t[:], func=AF.Square), R=[hb], W=[sqb_])
                op(DVE, lambda: nc.vector.tensor_copy(out=hbb.t[:], in_=hb.t[:]), R=[hb], W=[hbb])
                ssb = pr.next()
                mm_group(ssb, [(ssb.t[:, :N], ones_b, sqb_.t[:, k, :]) for k in range(8)], R=[sqb_, cmb])
                rt = f32r.next()
                op(ACT, lambda: nc.scalar.activation(out=rt.t[:], in_=ssb.t[:, :N], func=AF.Ln, scale=1.0 / D, bias=eps_c), R=[ssb], W=[rt])
                rs_ = rstdr4.next()
                op(ACT, lambda: nc.scalar.activation(out=rs_.t[:], in_=rt.t[:], func=AF.Exp, scale=-0.5), R=[rt], W=[rs_])
                return dict(q0=q0, hb=hb, rstd=rs_)

            def stage1_4(P):
                rs_ = P["rstd"]
                for f in range(32):
                    ps = pr.next()
                    mm_group(ps, [(ps.t[:, :N], W1_t[:, k, f * 128:(f + 1) * 128], hbb.t[:, k, :]) for k in range(8)], R=[W1b_, hbb])
                    tr_ = f32r.next()
                    op(DVE, lambda: nc.vector.scalar_tensor_tensor(out=tr_.t[:], in0=ps.t[:, :N], scalar=0.0, in1=rs_.t[:], op0=ALU.max, op1=ALU.mult), R=[ps, rs_], W=[tr_])
                    op(ACT, lambda: nc.scalar.activation(out=Ab.t[:, f, :], in_=tr_.t[:], func=AF.Square), R=[tr_], W=[Ab])

            def stage2_4(P):
                hb = P["hb"]
                for oc in range(8):
                    ps = pr.next()
                    mm_group(ps, [(ps.t[:, :N], W2_t[:, f, oc * 128:(oc + 1) * 128], Ab.t[:, f, :]) for f in range(32)], R=[W2b_, Ab])
                    op(DVE, lambda: nc.vector.tensor_tensor(out=h2b.t[:, oc, :], in0=ps.t[:, :N], in1=hb.t[:, oc, :], op=ALU.add), R=[ps, hb], W=[h2b])

            def final4(P):
                q0 = P["q0"]
                op(ACT, lambda: nc.scalar.activation(out=sq5b.t[:], in_=h2b.t[:], func=AF.Square), R=[h2b], W=[sq5b])
                ssb = pr.next()
                mm_group(ssb, [(ssb.t[:, :N], ones_b, sq5b.t[:, k, :]) for k in range(8)], R=[sq5b, cmb])
                rt = f32r.next()
                op(ACT, lambda: nc.scalar.activation(out=rt.t[:], in_=ssb.t[:, :N], func=AF.Ln, scale=1.0 / D, bias=eps_c), R=[ssb], W=[rt])
                op(ACT, lambda: nc.scalar.activation(out=rstd5.t[:], in_=rt.t[:], func=AF.Exp, scale=-0.5), R=[rt], W=[rstd5])
                os_ = osr.next()
                for oc in range(8):
                    op(DVE, lambda: nc.vector.scalar_tensor_tensor(out=os_.t[:, oc, :], in0=h2b.t[:, oc, :], scalar=gcol(GV_FIN + oc), in1=rstd5.t[:], op0=ALU.mult, op1=ALU.mult), R=[h2b, rstd5], W=[os_])
                dma(os_.slot, outv[:, :, q0:q0 + N], os_.t[:], R=[os_])

            Pn = prologue4(0)
            Pprev = None
            for ti in range(len(tiles4)):
                P = Pn
                stage1_4(P)
                if ti + 1 < len(tiles4):
                    Pn = prologue4(ti + 1)
                if Pprev is not None:
                    final4(Pprev)
                stage2_4(P)
                Pprev = P
            final4(Pprev)
            barrier()
    return nc


def _const_tables(S):
    LP = S + 128
    cm = np.zeros((128, 512), np.float32)
    cm[:, 0:128] = 1.0
    cm[0:64, 128:192] = 1.0
    cm[64:128, 192:256] = 1.0
    for i in range(64):
        cm[2 * i + 1, 256 + 2 * i] = -1.0
        cm[2 * i, 256 + 2 * i + 1] = 1.0
    cm[:, 384:512] = np.eye(128, dtype=np.float32)
    rows = S // 64
    row_id = np.repeat(np.arange(rows, dtype=np.float32), 64)
    col_id = np.tile(np.arange(64, dtype=np.float32), rows)
    inv_freq = (np.float32(10000.0) ** (-np.arange(16, dtype=np.float32) / np.float32(16))).astype(np.float32)
    ang = np.concatenate([row_id[:, None] * inv_freq, col_id[:, None] * inv_freq], axis=-1).astype(np.float32)
    ang = np.concatenate([np.zeros((128, 32), np.float32), ang], axis=0)
    cos = np.cos(ang).astype(np.float32)
    sin = np.sin(ang).astype(np.float32)
    pair = (np.arange(128) % 64) // 2
    C = np.ascontiguousarray(cos[:, pair].T)
    Sn = np.ascontiguousarray(sin[:, pair].T)
    return cm, C, Sn


def _prep_shared(inp, S):
    f = lambda a: np.ascontiguousarray(np.asarray(a, dtype=np.float32))
    cm, C, Sn = _const_tables(S)
    gv = np.zeros((128, NG), np.float32)
    gv[:, GV_MIX:GV_MIX + 8] = f(inp["norm_mix_g"])[0].reshape(8, 128).T
    gv[:, GV_MLP:GV_MLP + 8] = f(inp["norm_mlp_g"])[0].reshape(8, 128).T
    gv[:, GV_FIN:GV_FIN + 8] = f(inp["norm_final_g"]).reshape(8, 128).T
    gv[:, GV_QG] = np.tile(f(inp["q_norm_g"])[0], 2)
    gv[:, GV_KG] = np.tile(f(inp["k_norm_g"])[0], 2)
    gv[:, GV_BGLU:GV_BGLU + 4] = f(inp["b_glu"])[0].reshape(4, 128).T
    gv[:, GV_SSMD:GV_SSMD + 4] = f(inp["ssm_d"])[0].reshape(4, 128).T
    gv[0:112, GV_MASK] = -30000.0
    a_re, a_im, ldt = f(inp["ssm_a_re"])[0], f(inp["ssm_a_im"])[0], f(inp["ssm_log_dt"])[0]
    b_re, b_im = f(inp["ssm_b_re"])[0], f(inp["ssm_b_im"])[0]
    c_re, c_im = f(inp["ssm_c_re"])[0], f(inp["ssm_c_im"])[0]
    ssmA = np.zeros((128, 96), np.float32)
    bz = np.zeros((32, 128, 4, 128), np.float32)
    for d in range(2):
        for gp in range(16):
            ci = d * 16 + gp
            for g2 in range(2):
                g = 2 * gp + g2
                gl = g % 8
                ps = slice(g2 * 64, g2 * 64 + 64)
                ssmA[ps, ci] = a_re[d, g]
                ssmA[ps, 32 + ci] = a_im[d, g]
                ssmA[ps, 64 + ci] = ldt[d, g]
                bz[ci, ps, 0, gl * 16:(gl + 1) * 16] = b_re[d, g]
                bz[ci, ps, 1, gl * 16:(gl + 1) * 16] = b_im[d, g]
                bz[ci, ps, 2, gl * 16:(gl + 1) * 16] = c_re[d, g].T
                bz[ci, ps, 3, gl * 16:(gl + 1) * 16] = c_im[d, g].T
    shared = {
        "w_in": f(inp["w_in"])[0], "w_glu": f(inp["w_glu"])[0], "w_sp": f(inp["w_ssm_proj"])[0],
        "w_ap": f(inp["w_attn_proj"])[0], "w_out": f(inp["w_out"])[0], "w1": f(inp["w_mlp_in"])[0],
        "w2": f(inp["w_mlp_out"])[0], "gv": gv, "cmat": cm, "ropeC": C, "ropeS": Sn, "ssmA": ssmA,
        "bz": bz.reshape(32, 128, 512),
    }
    return shared


def _make_xT(xb, meta):
    S = xb.shape[0]
    full = np.concatenate([np.zeros((112, D), np.float32), np.asarray(meta, np.float32), np.asarray(xb, np.float32)], axis=0)
    return np.ascontiguousarray(full.T)


_NC_CACHE = {}


def kernel(**inputs):
    x = np.asarray(inputs["x"], dtype=np.float32)
    B, S, _ = x.shape
    shared = _prep_shared(inputs, S)
    if S not in _NC_CACHE:
        _NC_CACHE[S] = build(S)
    nc = _NC_CACHE[S]
    in_maps = []
    for b in range(B):
        m = dict(shared)
        m["xT"] = _make_xT(x[b], inputs["meta_tokens"])
        in_maps.append(m)
    res = run_bass_kernel_spmd(nc, in_maps, core_ids=list(range(B)))
    out = np.stack([np.ascontiguousarray(r["outT"].T) for r in res.results], axis=0)
    return out.astype(np.float32)
```

```python
import numpy as np
from contextlib import ExitStack
import concourse.bass as bass
import concourse.mybir as mybir
from concourse.bass_utils import run_bass_kernel_spmd

F32 = mybir.dt.float32
BF16 = mybir.dt.bfloat16
AF = mybir.ActivationFunctionType
ALU = mybir.AluOpType

D = 1024
DFF = 4096
EPS = 1e-6
GV_MIX, GV_MLP, GV_FIN, GV_QG, GV_KG, GV_BGLU, GV_SSMD, GV_MASK, NG = 0, 8, 16, 24, 25, 26, 30, 34, 35


class Ev:
    __slots__ = ("sem", "val", "key")

    def __init__(s, sem, val, key):
        s.sem, s.val, s.key = sem, val, key


class Eng:
    def __init__(s, name, h, sem):
        s.name, s.h, s.sem = name, h, sem
        s.cnt = 0
        s.seen = {}
        s.last = None
        s.selfsync = True

    def wait(s, evs):
        if evs is None:
            return
        if isinstance(evs, Ev):
            evs = [evs]
        for ev in evs:
            if ev is None:
                continue
            if isinstance(ev, (list, tuple)):
                s.wait(ev)
                continue
            if ev.key == s.name and (s.name == "pe" or not s.selfsync):
                continue
            if s.seen.get(ev.key, 0) >= ev.val:
                continue
            s.h.wait_ge(ev.sem, ev.val)
            s.seen[ev.key] = ev.val

    def mark(s, inst):
        s.cnt += 1
        inst.then_inc(s.sem, 1)
        s.last = Ev(s.sem, s.cnt, s.name)
        return s.last


class Slot:
    def __init__(s, sem, key):
        s.sem, s.key = sem, key
        s.cnt = 0
        s.last = None


class Buf:
    def __init__(s, t, slot=None):
        s.t = t
        s.slot = slot
        s.wr = None
        s.rd = {}


class Ring:
    def __init__(s, bufs):
        s.bufs = bufs
        s.i = 0

    def next(s):
        b = s.bufs[s.i % len(s.bufs)]
        s.i += 1
        return b


def build(S, dbg=False):
    LP = S + 128
    NT = LP // 128
    NCH = LP // 8
    nc = bass.Bass("TRN2", target_bir_lowering=False)

    def din(name, shape, dt=F32):
        return nc.dram_tensor(name, shape, dt, kind="ExternalInput").ap()

    def dscr(name, shape, dt):
        return nc.dram_tensor(name, shape, dt, kind=("ExternalOutput" if dbg else "Internal")).ap()

    xT = din("xT", [D, LP])
    w_in = din("w_in", [D, 4096])
    w_glu = din("w_glu", [512, 512])
    w_sp = din("w_sp", [512, D])
    w_ap = din("w_ap", [D, D])
    w_out = din("w_out", [D, D])
    w1 = din("w1", [D, DFF])
    w2 = din("w2", [DFF, D])
    gv_d = din("gv", [128, NG])
    cmat_d = din("cmat", [128, 4 * 128])
    ropeC = din("ropeC", [128, LP])
    ropeS = din("ropeS", [128, LP])
    ssmA = din("ssmA", [128, 96])
    bz = din("bz", [32, 128, 512])
    outT = nc.dram_tensor("outT", [D, S], F32, kind="ExternalOutput").ap()

    winb = dscr("winb", [D, 4096], BF16)
    wglub = dscr("wglub", [512, 512], BF16)
    wspb = dscr("wspb", [512, D], BF16)
    wapb = dscr("wapb", [D, D], BF16)
    woutb = dscr("woutb", [D, D], BF16)
    w1b = dscr("w1b", [D, DFF], BF16)
    w2b = dscr("w2b", [DFF, D], BF16)
    qT_d = dscr("qT_d", [D, LP], BF16)
    zT_d = dscr("zT_d", [512, LP], BF16)
    yaT_d = dscr("yaT_d", [D, S], BF16)
    h1T_d = dscr("h1T_d", [D, S], F32)

    _uid = [0]

    def _sb(name, shape, dt):
        _uid[0] += 1
        return nc.sbuf_tensor(f"{name}_{_uid[0]}", shape, dt)

    def kview(ap):
        return ap.rearrange("(k p) l -> p k l", p=128)

    top = ExitStack()
    with top:
        sems = [top.enter_context(nc.semaphore(f"sem{i}")) for i in range(64)]
        PE = Eng("pe", nc.tensor, sems[0])
        ACT = Eng("act", nc.scalar, sems[1])
        DVE = Eng("dve", nc.vector, sems[2])
        POOL = Eng("pool", nc.gpsimd, sems[3])
        SP = Eng("sp", nc.sync, None)
        engines = [PE, ACT, DVE, POOL]
        slots = [Slot(sems[4 + i], f"dma{i}") for i in range(60)]
        slot_ptr = [0]

        def new_slot():
            s = slots[slot_ptr[0]]
            slot_ptr[0] += 1
            return s

        def op(E, fn, R=(), W=(), mark=True):
            for b in R:
                E.wait(b.wr)
            for b in W:
                E.wait(b.wr)
                E.wait(list(b.rd.values()))
            inst = fn()
            if mark:
                ev = E.mark(inst)
                for b in R:
                    b.rd[E.name] = ev
                for b in W:
                    b.wr = ev
                    b.rd = {}
                return ev
            return None

        def dma(slot, out, in_, R=(), W=(), upd=()):
            for b in R:
                SP.wait(b.wr)
            for b in W:
                SP.wait(b.wr)
                SP.wait(list(b.rd.values()))
            inst = nc.sync.dma_start(out=out, in_=in_)
            slot.cnt += 16
            inst.then_inc(slot.sem, 16)
            ev = Ev(slot.sem, slot.cnt, slot.key)
            slot.last = ev
            for b in R:
                b.rd[slot.key] = ev
            for b in list(W) + list(upd):
                b.wr = ev
                b.rd = {}
            return ev

        def mm_group(bank, mms, R=()):
            for b in R:
                PE.wait(b.wr)
            PE.wait(bank.wr)
            PE.wait(list(bank.rd.values()))
            n = len(mms)
            inst = None
            for i, m in enumerate(mms):
                if len(m) == 3:
                    st, sp_ = (i == 0), (i == n - 1)
                else:
                    st, sp_ = m[3], m[4]
                inst = nc.tensor.matmul(m[0], lhsT=m[1], rhs=m[2], start=st, stop=sp_)
            ev = PE.mark(inst)
            for b in R:
                b.rd[PE.name] = ev
            bank.wr = ev
            bank.rd = {}
            return ev

        def barrier():
            evs = [E.last for E in engines if E.last is not None]
            evs += [s.last for s in slots if s.last is not None]
            for E in engines + [SP]:
                E.wait(evs)

        def colblocks(n, w=512):
            return [(c, min(w, n - c)) for c in range(0, n, w)]

        gv_t = top.enter_context(_sb("gv", [128, NG], F32))
        cm_f = top.enter_context(_sb("cm_f", [128, 512], F32))
        cm_b = top.enter_context(_sb("cm_b", [128, 512], BF16))
        PSA = top.enter_context(nc.psum_tensor("psa", [128, 7, 512], F32))
        PSB = top.enter_context(nc.psum_tensor("psb", [128, 1024], BF16))
        gv = Buf(gv_t, new_slot())
        cmf = Buf(cm_f, new_slot())
        cmb = Buf(cm_b)
        dma(gv.slot, gv_t[:], gv_d, W=[gv])
        dma(cmf.slot, cm_f[:], cmat_d, W=[cmf])
        op(DVE, lambda: nc.vector.tensor_copy(out=cm_b[:], in_=cm_f[:]), R=[cmf], W=[cmb])
        ones_b = cm_b[:, 0:128]
        blk_b = cm_b[:, 128:256]
        rot_b = cm_b[:, 256:384]
        id_b = cm_b[:, 384:512]
        id_f = cm_f[:, 384:512]
        ones_f = cm_f[:, 0:128]
        DVE.wait(gv.wr)
        ACT.wait(gv.wr)
        POOL.wait(gv.wr)
        banks = [Buf(PSA[:, i, :]) for i in range(7)]
        bankB = Buf(PSB)
        base_slot_ptr = slot_ptr[0]

        def gcol(c):
            return gv_t[:, c:c + 1]

        with ExitStack() as ph:
            slot_ptr[0] = base_slot_ptr
            stg = Ring([Buf(ph.enter_context(_sb(f"wst{i}", [128, 2048], F32)), new_slot()) for i in range(3)])
            ost = Ring([Buf(ph.enter_context(_sb(f"wso{i}", [128, 2048], BF16)), new_slot()) for i in range(3)])
            rr = [0]

            def convert(src, dst, rows, cols, gain0=None):
                for k in range(rows // 128):
                    for c0 in range(0, cols, 2048):
                        n = min(2048, cols - c0)
                        st = stg.next()
                        dma(st.slot, st.t[:, :n], src[k * 128:(k + 1) * 128, c0:c0 + n], W=[st])
                        ob = ost.next()
                        which = rr[0] % 3
                        rr[0] += 1
                        if gain0 is None:
                            if which == 0:
                                op(DVE, lambda: nc.vector.tensor_copy(out=ob.t[:, :n], in_=st.t[:, :n]), R=[st], W=[ob])
                            elif which == 1:
                                op(POOL, lambda: nc.gpsimd.tensor_copy(out=ob.t[:, :n], in_=st.t[:, :n]), R=[st], W=[ob])
                            else:
                                op(ACT, lambda: nc.scalar.copy(out=ob.t[:, :n], in_=st.t[:, :n]), R=[st], W=[ob])
                        else:
                            g = gcol(gain0 + k)
                            if which == 0:
                                op(DVE, lambda: nc.vector.tensor_scalar(out=ob.t[:, :n], in0=st.t[:, :n], scalar1=g, scalar2=None, op0=ALU.mult), R=[st], W=[ob])
                            elif which == 1:
                                op(POOL, lambda: nc.gpsimd.tensor_scalar(out=ob.t[:, :n], in0=st.t[:, :n], scalar1=g, scalar2=None, op0=ALU.mult), R=[st], W=[ob])
                            else:
                                op(ACT, lambda: nc.scalar.activation(out=ob.t[:, :n], in_=st.t[:, :n], func=AF.Copy, scale=g), R=[st], W=[ob])
                        dma(ob.slot, dst[k * 128:(k + 1) * 128, c0:c0 + n], ob.t[:, :n], R=[ob])

            convert(w_in, winb, D, 4096, GV_MIX)
            barrier()

        def in_proj(ph, chunks, wcol0, wcols, uTb=None, kTb=None, Vxb=None):
            W_t = ph.enter_context(_sb("Wp", [128, 8, wcols], BF16))
            Wb = Buf(W_t, new_slot())
            dma(Wb.slot, W_t[:], kview(winb)[:, :, wcol0:wcol0 + wcols], W=[Wb])
            xring = Ring([Buf(ph.enter_context(_sb(f"xt{i}", [128, 8, 512], F32)), new_slot()) for i in range(2)])
            sqr = Ring([Buf(ph.enter_context(_sb(f"sq{i}", [128, 8, 512], BF16))) for i in range(1)])
            xbr = Ring([Buf(ph.enter_context(_sb(f"xb{i}", [128, 8, 512], BF16))) for i in range(2)])
            need_rope = any(c[0] in "qk" for c in chunks)
            if need_rope:
                csr = Ring([Buf(ph.enter_context(_sb(f"cs{i}", [128, 2, 512], F32)), new_slot()) for i in range(2)])
                qor = Ring([Buf(ph.enter_context(_sb(f"qo{i}", [128, 512], BF16)), new_slot()) for i in range(3)])
            f32r = Ring([Buf(ph.enter_context(_sb(f"f32r{i}", [128, 512], F32))) for i in range(12)])
            bfr = Ring([Buf(ph.enter_context(_sb(f"bfr{i}", [128, 512], BF16))) for i in range(6)])
            rstdr = Ring([Buf(ph.enter_context(_sb(f"rstd{i}", [128, 512], F32))) for i in range(2)])
            mainr = Ring(banks[0:3])
            ssb = banks[3]
            hsr = Ring(banks[4:5])
            rotr = Ring(banks[5:7])
            trb = bankB
            xTv = kview(xT)
            tiles_ = colblocks(LP)

            def prologue(ti):
                (c0, N) = tiles_[ti]
                xt = xring.next()
                dma(xt.slot, xt.t[:, :, :N], xTv[:, :, c0:c0 + N], W=[xt])
                cs = None
                if need_rope:
                    cs = csr.next()
                    dma(cs.slot, cs.t[:, 0, :N], ropeC[:, c0:c0 + N], W=[cs])
                    dma(cs.slot, cs.t[:, 1, :N], ropeS[:, c0:c0 + N], upd=[cs])
                sq = sqr.next()
                op(ACT, lambda: nc.scalar.activation(out=sq.t[:, :, :N], in_=xt.t[:, :, :N], func=AF.Square), R=[xt], W=[sq])
                xb = xbr.next()
                op(DVE, lambda: nc.vector.tensor_copy(out=xb.t[:, :, :N], in_=xt.t[:, :, :N]), R=[xt], W=[xb])
                mm_group(ssb, [(ssb.t[:, :N], ones_b, sq.t[:, k, :N]) for k in range(8)], R=[sq, cmb])
                rt = f32r.next()
                op(ACT, lambda: nc.scalar.activation(out=rt.t[:, :N], in_=ssb.t[:, :N], func=AF.Ln, scale=1.0 / D, bias=eps_c), R=[ssb], W=[rt])
                rstd = rstdr.next()
                op(ACT, lambda: nc.scalar.activation(out=rstd.t[:, :N], in_=rt.t[:, :N], func=AF.Exp, scale=-0.5), R=[rt], W=[rstd])
                return dict(c0=c0, N=N, cs=cs, xb=xb, rstd=rstd)

            def stageA(P, ch, stt):
                (kind, idx, wc) = ch
                c0, N, xb, rstd = P["c0"], P["N"], P["xb"], P["rstd"]
                ps = mainr.next()
                mm_group(ps, [(ps.t[:, :N], W_t[:, k, wc:wc + 128], xb.t[:, k, :N]) for k in range(8)], R=[Wb, xb])
                if kind == "u":
                    op(DVE, lambda: nc.vector.tensor_tensor(out=uTb.t[:, idx, :, c0 // 8:(c0 + N) // 8].rearrange("p i c -> p c i"), in0=ps.t[:, :N].rearrange("p (c i) -> p c i", i=8), in1=rstd.t[:, :N].rearrange("p (c i) -> p c i", i=8), op=ALU.mult), R=[ps, rstd], W=[uTb])
                elif kind == "v":
                    vt = bfr.next()
                    op(DVE, lambda: nc.vector.tensor_tensor(out=vt.t[:, :N], in0=ps.t[:, :N], in1=rstd.t[:, :N], op=ALU.mult), R=[ps, rstd], W=[vt])
                    stt["vt"] = vt
                else:
                    tq = f32r.next()
                    op(DVE, lambda: nc.vector.tensor_tensor(out=tq.t[:, :N], in0=ps.t[:, :N], in1=rstd.t[:, :N], op=ALU.mult), R=[ps, rstd], W=[tq])
                    stt["tq"] = tq

            def stageB(P, ch, stt):
                (kind, idx, wc) = ch
                c0, N = P["c0"], P["N"]
                if kind == "u":
                    return
                if kind == "v":
                    vt = stt["vt"]
                    for s_ in range(N // 128):
                        tt = c0 // 128 + s_
                        op(PE, lambda: nc.tensor.transpose(out=trb.t[:, s_ * 128:(s_ + 1) * 128], in_=vt.t[:, s_ * 128:(s_ + 1) * 128], identity=id_b), R=[vt, cmb], W=[trb])
                        op(ACT, lambda: nc.scalar.copy(out=Vxb.t[:, tt, 2 * idx:2 * idx + 2, 0:64],
                                                       in_=trb.t[:, s_ * 128:(s_ + 1) * 128].rearrange("p (g d) -> p g d", g=2)), R=[trb], W=[Vxb])
                    return
                tq = stt["tq"]
                sq2 = bfr.next()
                op(ACT, lambda: nc.scalar.activation(out=sq2.t[:, :N], in_=tq.t[:, :N], func=AF.Square), R=[tq], W=[sq2])
                hs = hsr.next()
                mm_group(hs, [(hs.t[:, :N], blk_b, sq2.t[:, :N])], R=[sq2, cmb])
                rt2 = f32r.next()
                op(ACT, lambda: nc.scalar.activation(out=rt2.t[:, :N], in_=hs.t[:, :N], func=AF.Ln, scale=1.0 / 64, bias=eps_c), R=[hs], W=[rt2])
                rq = f32r.next()
                op(ACT, lambda: nc.scalar.activation(out=rq.t[:, :N], in_=rt2.t[:, :N], func=AF.Exp, scale=-0.5), R=[rt2], W=[rq])
                tq2 = bfr.next()
                gc_ = gcol(GV_QG if kind == "q" else GV_KG)
                op(DVE, lambda: nc.vector.scalar_tensor_tensor(out=tq2.t[:, :N], in0=tq.t[:, :N], scalar=gc_, in1=rq.t[:, :N], op0=ALU.mult, op1=ALU.mult), R=[tq, rq], W=[tq2])
                stt["tq2"] = tq2

            def stageC(P, ch, stt):
                (kind, idx, wc) = ch
                c0, N, cs = P["c0"], P["N"], P["cs"]
                if kind in "uv":
                    return
                tq2 = stt["tq2"]
                rot = rotr.next()
                mm_group(rot, [(rot.t[:, :N], rot_b, tq2.t[:, :N])], R=[tq2, cmb])
                t1 = f32r.next()
                op(DVE, lambda: nc.vector.tensor_tensor(out=t1.t[:, :N], in0=tq2.t[:, :N], in1=cs.t[:, 0, :N], op=ALU.mult), R=[tq2, cs], W=[t1])
                t2 = f32r.next()
                op(DVE, lambda: nc.vector.tensor_tensor(out=t2.t[:, :N], in0=rot.t[:, :N], in1=cs.t[:, 1, :N], op=ALU.mult), R=[rot, cs], W=[t2])
                if kind == "k":
                    op(DVE, lambda: nc.vector.tensor_tensor(out=kTb.t[:, idx, c0:c0 + N], in0=t1.t[:, :N], in1=t2.t[:, :N], op=ALU.add), R=[t1, t2], W=[kTb])
                else:
                    qo = qor.next()
                    op(DVE, lambda: nc.vector.tensor_tensor(out=qo.t[:, :N], in0=t1.t[:, :N], in1=t2.t[:, :N], op=ALU.add), R=[t1, t2], W=[qo])
                    dma(qo.slot, qT_d[idx * 128:(idx + 1) * 128, c0:c0 + N], qo.t[:, :N], R=[qo])

            nch = len(chunks)
            Pn = prologue(0)
            for ti in range(len(tiles_)):
                P = Pn
                stts = [dict() for _ in range(nch)]
                for s_i in range(nch + 2):
                    if s_i < nch:
                        stageA(P, chunks[s_i], stts[s_i])
                        if s_i == nch - 1 and ti + 1 < len(tiles_):
                            Pn = prologue(ti + 1)
                    if 0 <= s_i - 1 < nch:
                        stageB(P, chunks[s_i - 1], stts[s_i - 1])
                    if 0 <= s_i - 2 < nch:
                        stageC(P, chunks[s_i - 2], stts[s_i - 2])

        eps_t = top.enter_context(_sb("eps", [128, 1], F32))
        epsb = Buf(eps_t)
        op(DVE, lambda: nc.vector.memset(eps_t[:], EPS), W=[epsb])
        ACT.wait(epsb.wr)
        eps_c = eps_t[:, 0:1]

        with ExitStack() as ph_prm:
            ph = ph_prm
            slot_ptr[0] = base_slot_ptr
            NPR = 108
            PR_t = ph.enter_context(_sb("PR", [128, NPR * 32], F32))
            PRb = Buf(PR_t, new_slot())
            pr_i = [0]

            def col():
                i = pr_i[0]
                pr_i[0] += 1
                assert i < NPR
                return PR_t[:, i * 32:(i + 1) * 32]

            A_re, A_im, LDT = col(), col(), col()
            dma(PRb.slot, PR_t[:, 0:96], ssmA, W=[PRb])

            def V(fn):
                return op(DVE, fn, R=[PRb], W=[PRb])

            def A(fn):
                return op(ACT, fn, R=[PRb], W=[PRb])

            def TT(o, a, b, o_):
                V(lambda: nc.vector.tensor_tensor(out=o, in0=a, in1=b, op=o_))

            def TS(o, a, s1, o1, s2=None, o2=None):
                if o2 is None:
                    V(lambda: nc.vector.tensor_scalar(out=o, in0=a, scalar1=s1, scalar2=None, op0=o1))
                else:
                    V(lambda: nc.vector.tensor_scalar(out=o, in0=a, scalar1=s1, scalar2=s2, op0=o1, op1=o2))

            t1, t2 = col(), col()

            def cmul(orr, oi, ar, ai, br, bi):
                TT(t1, ar, br, ALU.mult)
                TT(t2, ai, bi, ALU.mult)
                TT(orr, t1, t2, ALU.subtract)
                TT(t1, ar, bi, ALU.mult)
                TT(t2, ai, br, ALU.mult)
                TT(oi, t1, t2, ALU.add)

            dt_, lre, mag, ang, cs_, sn_, c2, s2 = col(), col(), col(), col(), col(), col(), col(), col()
            pio2 = col()
            V(lambda: nc.vector.memset(pio2, float(np.pi / 2)))
            A(lambda: nc.scalar.activation(out=dt_, in_=LDT, func=AF.Exp))
            TS(lre, A_re, -1e-4, ALU.min)
            TT(t1, lre, dt_, ALU.mult)
            A(lambda: nc.scalar.activation(out=mag, in_=t1, func=AF.Exp))
            TT(ang, A_im, dt_, ALU.mult)
            ki_t = ph.enter_context(_sb("ki", [128, 32], mybir.dt.int32))
            kf, rr_, mm_ = col(), col(), col()
            TS(kf, ang, float(1.0 / (2 * np.pi)), ALU.mult)
            V(lambda: nc.vector.tensor_copy(out=ki_t[:], in_=kf))
            V(lambda: nc.vector.tensor_copy(out=kf, in_=ki_t[:]))
            V(lambda: nc.vector.scalar_tensor_tensor(out=rr_, in0=kf, scalar=float(-2 * np.pi), in1=ang, op0=ALU.mult, op1=ALU.add))
            pi_c, npi_c = col(), col()
            V(lambda: nc.vector.memset(pi_c, float(np.pi)))
            V(lambda: nc.vector.memset(npi_c, float(-np.pi)))
            TT(mm_, rr_, pi_c, ALU.is_gt)
            V(lambda: nc.vector.scalar_tensor_tensor(out=rr_, in0=mm_, scalar=float(-2 * np.pi), in1=rr_, op0=ALU.mult, op1=ALU.add))
            TT(mm_, rr_, npi_c, ALU.is_lt)
            V(lambda: nc.vector.scalar_tensor_tensor(out=rr_, in0=mm_, scalar=float(2 * np.pi), in1=rr_, op0=ALU.mult, op1=ALU.add))
            TS(s2, rr_, -1.0, ALU.mult)
            TT(c2, rr_, s2, ALU.max)
            A(lambda: nc.scalar.activation(out=sn_, in_=rr_, func=AF.Sin))
            A(lambda: nc.scalar.activation(out=cs_, in_=c2, func=AF.Sin, scale=-1.0, bias=pio2[:, 0:1]))
            lbr, lbi = col(), col()
            TT(lbr, mag, cs_, ALU.mult)
            TT(lbi, mag, sn_, ALU.mult)
            numr, den, rden, fre, fim = col(), col(), col(), col(), col()
            TS(numr, lbr, -1.0, ALU.add)
            TT(t1, lre, lre, ALU.mult)
            TT(t2, A_im, A_im, ALU.mult)
            TT(den, t1, t2, ALU.add)
            V(lambda: nc.vector.reciprocal(out=rden, in_=den))
            TT(t1, numr, lre, ALU.mult)
            TT(t2, lbi, A_im, ALU.mult)
            TT(fre, t1, t2, ALU.add)
            TT(fre, fre, rden, ALU.mult)
            TT(t1, lbi, lre, ALU.mult)
            TT(t2, numr, A_im, ALU.mult)
            TT(fim, t1, t2, ALU.subtract)
            TT(fim, fim, rden, ALU.mult)
            Pre = [col() for _ in range(9)]
            Pim = [col() for _ in range(9)]
            NPim = [col() for _ in range(9)]
            V(lambda: nc.vector.memset(Pre[0], 1.0))
            V(lambda: nc.vector.memset(Pim[0], 0.0))
            for tau in range(8):
                cmul(Pre[tau + 1], Pim[tau + 1], Pre[tau], Pim[tau], lbr, lbi)
            for tau in range(9):
                TS(NPim[tau], Pim[tau], -1.0, ALU.mult)
            Gre = [col() for _ in range(8)]
            Gim = [col() for _ in range(8)]
            for tau in range(8):
                cmul(Gre[tau], Gim[tau], Pre[tau], Pim[tau], fre, fim)
            nlev = 0
            while (1 << nlev) < NCH:
                nlev += 1
            Hre = [col() for _ in range(nlev)]
            Him = [col() for _ in range(nlev)]
            NHim = [col() for _ in range(nlev)]
            V(lambda: nc.vector.tensor_copy(out=Hre[0], in_=Pre[8]))
            V(lambda: nc.vector.tensor_copy(out=Him[0], in_=Pim[8]))
            for k in range(1, nlev):
                cmul(Hre[k], Him[k], Hre[k - 1], Him[k - 1], Hre[k - 1], Him[k - 1])
            for k in range(nlev):
                TS(NHim[k], Him[k], -1.0, ALU.mult)
            KT_t = ph.enter_context(_sb("KT", [128, 2, 32, 8], F32))
            PT_t = ph.enter_context(_sb("PTt", [128, 3, 32, 9], F32))
            for i_ in range(8):
                for (ri, Garr) in ((0, Gre), (1, Gim)):
                    V(lambda: nc.vector.tensor_copy(out=KT_t[:, ri, 0:16, i_], in_=Garr[7 - i_][:, 0:16]))
                    V(lambda: nc.vector.tensor_copy(out=KT_t[:, ri, 16:32, i_], in_=Garr[i_][:, 16:32]))
            for tau_ in range(9):
                for (ri, Parr) in ((0, Pre), (1, Pim), (2, NPim)):
                    V(lambda: nc.vector.tensor_copy(out=PT_t[:, ri, :, tau_], in_=Parr[tau_]))
            ET_t = ph.enter_context(_sb("ET", [128, 2, 32, 16], F32))
            e_a = (col(), col())
            e_b = (col(), col())
            V(lambda: nc.vector.tensor_copy(out=e_a[0], in_=Pre[8]))
            V(lambda: nc.vector.tensor_copy(out=e_a[1], in_=Pim[8]))
            for m_ in range(1, 17):
                for ri in range(2):
                    V(lambda: nc.vector.tensor_copy(out=ET_t[:, ri, 0:16, m_ - 1], in_=e_a[ri][:, 0:16]))
                    V(lambda: nc.vector.tensor_copy(out=ET_t[:, ri, 16:32, 16 - m_], in_=e_a[ri][:, 16:32]))
                if m_ < 16:
                    cmul(e_b[0], e_b[1], e_a[0], e_a[1], Pre[8], Pim[8])
                    e_a, e_b = e_b, e_a
            ev_pr = V(lambda: nc.vector.tensor_copy(out=t1, in_=t2))
            for E in (ACT, POOL, PE):
                E.wait(ev_pr)
            if dbg:
                prdbg = nc.dram_tensor("prdbg", [128, NPR * 32], F32, kind="ExternalOutput").ap()
                dma(PRb.slot, prdbg, PR_t[:], R=[PRb])

            base_slot_ptr = slot_ptr[0]
            for hf in range(2):
              with ExitStack() as ph_ssm:
                uT_t = ph_ssm.enter_context(_sb(f"uT{hf}", [128, 2, 8, NCH], BF16))
                uTb = Buf(uT_t)
                with ExitStack() as ph:
                    slot_ptr[0] = base_slot_ptr
                    in_proj(ph, [("u", i, i * 128) for i in range(2)], hf * 256, 256, uTb=uTb)
                    barrier()
                slot_ptr[0] = base_slot_ptr
                ph = ph_ssm
                bzr = Ring([Buf(ph.enter_context(_sb(f"bz{i}", [128, 4, 128], F32)), new_slot()) for i in range(1)])
                Wst_r = Ring([Buf(ph.enter_context(_sb(f"Wst{i}", [128, 16, 128], BF16))) for i in range(1)])
                WAb = Buf(ph.enter_context(_sb("WA", [128, 16, 128], F32)))
                T1b = Buf(ph.enter_context(_sb("T1", [128, 16, 128], F32)))
                T2b = Buf(ph.enter_context(_sb("T2", [128, 16, 128], F32)))
                BbZ_t = ph.enter_context(_sb("BbZ", [128, 16, 128], BF16))
                BbZb = Buf(BbZ_t)
                CL_t = ph.enter_context(_sb("CL", [128, 8 * 18, 128], BF16))
                CLb = Buf(CL_t)
                BD_t = ph.enter_context(_sb("BD", [128, 16, 128], BF16))
                BDb = Buf(BD_t)
                XA_t = ph.enter_context(_sb("XA", [128, 2, NCH], F32))
                XB_t = ph.enter_context(_sb("XB", [128, 2, NCH], F32))
                XAb, XBb = Buf(XA_t), Buf(XB_t)
                NB = NCH // 16
                XXa_t = ph.enter_context(_sb("XXa", [128, 2, NB], F32))
                XXb_t = ph.enter_context(_sb("XXb", [128, 2, NB], F32))
                XXpf_t = ph.enter_context(_sb("XXpf", [128, 2, NB], F32))
                XXpb_t = ph.enter_context(_sb("XXpb", [128, 2, NB], F32))
                XXab, XXbb, XXpfb, XXpbb = Buf(XXa_t), Buf(XXb_t), Buf(XXpf_t), Buf(XXpb_t)
                op(DVE, lambda: nc.vector.memset(XXpf_t[:].rearrange("p a c -> p (a c)"), 0.0), W=[XXpfb])
                op(DVE, lambda: nc.vector.memset(XXpb_t[:].rearrange("p a c -> p (a c)"), 0.0), W=[XXpbb])
                Xb_t = ph.enter_context(_sb("Xb", [128, 8, 2, NCH], BF16))
                Xbb = Buf(Xb_t)
                zbuf_t = ph.enter_context(_sb("zbuf", [128, LP], BF16))
                zb = Buf(zbuf_t, new_slot())
                ytr = Ring([Buf(ph.enter_context(_sb(f"yt{i}", [128, 512], F32))) for i in range(2)])
                op(POOL, lambda: nc.gpsimd.memset(Xb_t[:].rearrange("p a b c -> p (a b c)"), 0.0), W=[Xbb])
                trr = Ring([banks[0], banks[3]])
                sring = Ring(banks[1:3])
                kring = Ring(banks[3:4])
                accr = Ring(banks[4:7])
                cbs = colblocks(NCH)

                for gcl in range(2):
                    gc = hf * 2 + gcl
                    uv = uT_t[:, gcl, :, :].rearrange("p i c -> p c i")
                    for gpl in range(4):
                        gp = gc * 4 + gpl
                        for d in range(2):
                            cidx = d * 16 + gp
                            slot8 = gpl * 2 + d
                            bzb = bzr.next()
                            dma(bzb.slot, bzb.t[:], bz[cidx].rearrange("p (a c) -> p a c", a=4), W=[bzb])
                            Bre, Bim, Cre, Cim = (bzb.t[:, a, :] for a in range(4))
                            Wst = Wst_r.next()

                            def sc(arr):
                                return arr[:, cidx:cidx + 1]

                            wa = WAb
                            kre = KT_t[:, 0, cidx, :].unsqueeze(2).unsqueeze(3).broadcast_to([128, 8, 2, 128])
                            kim2 = KT_t[:, 1, cidx, :].unsqueeze(2).broadcast_to([128, 8, 128])
                            bpair = bzb.t[:, 0:2, :].unsqueeze(1).broadcast_to([128, 8, 2, 128])
                            bre8 = Bre.unsqueeze(1).broadcast_to([128, 8, 128])
                            bim8 = Bim.unsqueeze(1).broadcast_to([128, 8, 128])
                            T1v = T1b.t[:].rearrange("p (i a) c -> p i a c", a=2)
                            T2v = T2b.t[:].rearrange("p (i a) c -> p i a c", a=2)
                            WAv = wa.t[:].rearrange("p (i a) c -> p i a c", a=2)
                            op(DVE, lambda: nc.vector.tensor_tensor(out=T1v, in0=bpair, in1=kre, op=ALU.mult), R=[bzb], W=[T1b])
                            op(DVE, lambda: nc.vector.tensor_tensor(out=T2v[:, :, 0, :], in0=bim8, in1=kim2, op=ALU.mult), R=[bzb], W=[T2b])
                            op(DVE, lambda: nc.vector.tensor_tensor(out=T2v[:, :, 1, :], in0=bre8, in1=kim2, op=ALU.mult), R=[bzb], W=[T2b])
                            op(DVE, lambda: nc.vector.tensor_tensor(out=WAv[:, :, 0, :], in0=T1v[:, :, 0, :], in1=T2v[:, :, 0, :], op=ALU.subtract), R=[T1b, T2b], W=[wa])
                            op(DVE, lambda: nc.vector.tensor_tensor(out=WAv[:, :, 1, :], in0=T1v[:, :, 1, :], in1=T2v[:, :, 1, :], op=ALU.add), R=[T1b, T2b], W=[wa])
                            i0 = 7 if d == 0 else 0
                            op(DVE, lambda: nc.vector.tensor_copy(out=BbZ_t[:, slot8 * 2:slot8 * 2 + 2, :], in_=wa.t[:, i0 * 2:i0 * 2 + 2, :]), R=[wa], W=[BbZb])
                            for q4 in range(4):
                                tb = trr.next()
                                for m4 in range(4):
                                    idx = q4 * 4 + m4
                                    op(PE, lambda: nc.tensor.transpose(out=tb.t[:, m4 * 128:(m4 + 1) * 128], in_=wa.t[:, idx, :], identity=id_f), R=[wa, cmf], W=[tb], mark=(m4 == 3))
                                op(ACT, lambda: nc.scalar.copy(out=Wst.t[:, q4 * 4:(q4 + 1) * 4, :], in_=tb.t[:].rearrange("p (m c) -> p m c", m=4)), R=[tb], W=[Wst])
                            pre9 = PT_t[:, 0, cidx, :].unsqueeze(2).broadcast_to([128, 9, 128])
                            pim9 = PT_t[:, 1, cidx, :].unsqueeze(2).broadcast_to([128, 9, 128])
                            npim9 = PT_t[:, 2, cidx, :].unsqueeze(2).broadcast_to([128, 9, 128])
                            cre9 = Cre.unsqueeze(1).broadcast_to([128, 9, 128])
                            cim9 = Cim.unsqueeze(1).broadcast_to([128, 9, 128])
                            A1 = T1b.t[:, 0:9, :]
                            A2 = T2b.t[:, 0:9, :]
                            CLv = CL_t[:, slot8 * 18:(slot8 + 1) * 18, :].rearrange("p (t a) c -> p t a c", a=2)
                            op(DVE, lambda: nc.vector.tensor_tensor(out=A1, in0=cre9, in1=pre9, op=ALU.mult), R=[bzb], W=[T1b])
                            op(DVE, lambda: nc.vector.tensor_tensor(out=A2, in0=cim9, in1=pim9, op=ALU.mult), R=[bzb], W=[T2b])
                            op(DVE, lambda: nc.vector.tensor_tensor(out=CLv[:, :, 0, :], in0=A1, in1=A2, op=ALU.subtract), R=[T1b, T2b], W=[CLb])
                            op(DVE, lambda: nc.vector.tensor_tensor(out=A1, in0=cre9, in1=npim9, op=ALU.mult), R=[bzb], W=[T1b])
                            op(DVE, lambda: nc.vector.tensor_tensor(out=A2, in0=cim9, in1=pre9, op=ALU.mult), R=[bzb], W=[T2b])
                            op(DVE, lambda: nc.vector.tensor_tensor(out=CLv[:, :, 1, :], in0=A1, in1=A2, op=ALU.subtract), R=[T1b, T2b], W=[CLb])
                            for part in range(2):
                                for (cb0, n) in cbs:
                                    sbk = sring.next()
                                    mm_group(sbk, [(sbk.t[:, :n], Wst.t[:, i * 2 + part, :], uv[:, cb0:cb0 + n, i]) for i in range(8)], R=[Wst, uTb])
                                    op(ACT, lambda: nc.scalar.copy(out=XA_t[:, part, cb0:cb0 + n], in_=sbk.t[:, :n]), R=[sbk], W=[XAb])
                            src, dst = XAb, XBb
                            for k in range(4):
                                s_ = 1 << k
                                a_, b_, nb_ = sc(Hre[k]), sc(Him[k]), sc(NHim[k])
                                X = src.t[:].rearrange("p a (c k) -> p a c k", k=16)
                                Y = dst.t[:].rearrange("p a (c k) -> p a c k", k=16)
                                if d == 0:
                                    lo, hi = slice(0, 16 - s_), slice(s_, 16)
                                    keep = slice(0, s_)
                                else:
                                    lo, hi = slice(s_, 16), slice(0, 16 - s_)
                                    keep = slice(16 - s_, 16)
                                op(ACT, lambda: nc.scalar.copy(out=Y[:, :, :, keep], in_=X[:, :, :, keep]), R=[src], W=[dst])
                                op(DVE, lambda: nc.vector.scalar_tensor_tensor(out=Y[:, 0, :, hi], in0=X[:, 0, :, lo], scalar=a_, in1=X[:, 0, :, hi], op0=ALU.mult, op1=ALU.add), R=[src], W=[dst])
                                op(DVE, lambda: nc.vector.scalar_tensor_tensor(out=Y[:, 0, :, hi], in0=X[:, 1, :, lo], scalar=nb_, in1=Y[:, 0, :, hi], op0=ALU.mult, op1=ALU.add), R=[src], W=[dst])
                                op(DVE, lambda: nc.vector.scalar_tensor_tensor(out=Y[:, 1, :, hi], in0=X[:, 0, :, lo], scalar=b_, in1=X[:, 1, :, hi], op0=ALU.mult, op1=ALU.add), R=[src], W=[dst])
                                op(DVE, lambda: nc.vector.scalar_tensor_tensor(out=Y[:, 1, :, hi], in0=X[:, 1, :, lo], scalar=a_, in1=Y[:, 1, :, hi], op0=ALU.mult, op1=ALU.add), R=[src], W=[dst])
                                src, dst = dst, src
                            L4 = src.t[:].rearrange("p a (c k) -> p a c k", k=16)
                            xs, xd = XXab, XXbb
                            op(DVE, lambda: nc.vector.tensor_copy(out=xs.t[:], in_=L4[:, :, :, 15 if d == 0 else 0]), R=[src], W=[xs])
                            k = 0
                            while (1 << k) < NB:
                                s_ = 1 << k
                                a_, b_, nb_ = sc(Hre[4 + k]), sc(Him[4 + k]), sc(NHim[4 + k])
                                X, Y = xs.t, xd.t
                                if d == 0:
                                    lo, hi = slice(0, NB - s_), slice(s_, NB)
                                    keep = slice(0, s_)
                                else:
                                    lo, hi = slice(s_, NB), slice(0, NB - s_)
                                    keep = slice(NB - s_, NB)
                                op(DVE, lambda: nc.vector.tensor_copy(out=Y[:, :, keep], in_=X[:, :, keep]), R=[xs], W=[xd])
                                op(DVE, lambda: nc.vector.scalar_tensor_tensor(out=Y[:, 0, hi], in0=X[:, 0, lo], scalar=a_, in1=X[:, 0, hi], op0=ALU.mult, op1=ALU.add), R=[xs], W=[xd])
                                op(DVE, lambda: nc.vector.scalar_tensor_tensor(out=Y[:, 0, hi], in0=X[:, 1, lo], scalar=nb_, in1=Y[:, 0, hi], op0=ALU.mult, op1=ALU.add), R=[xs], W=[xd])
                                op(DVE, lambda: nc.vector.scalar_tensor_tensor(out=Y[:, 1, hi], in0=X[:, 0, lo], scalar=b_, in1=X[:, 1, hi], op0=ALU.mult, op1=ALU.add), R=[xs], W=[xd])
                                op(DVE, lambda: nc.vector.scalar_tensor_tensor(out=Y[:, 1, hi], in0=X[:, 1, lo], scalar=a_, in1=Y[:, 1, hi], op0=ALU.mult, op1=ALU.add), R=[xs], W=[xd])
                                xs, xd = xd, xs
                                k += 1
                            if d == 0:
                                xpb = XXpfb
                                op(DVE, lambda: nc.vector.tensor_copy(out=xpb.t[:, :, 1:NB], in_=xs.t[:, :, 0:NB - 1]), R=[xs], W=[xpb])
                            else:
                                xpb = XXpbb
                                op(DVE, lambda: nc.vector.tensor_copy(out=xpb.t[:, :, 0:NB - 1], in_=xs.t[:, :, 1:NB]), R=[xs], W=[xpb])
                            etr = ET_t[:, 0, cidx, :].unsqueeze(1).broadcast_to([128, NB, 16])
                            eti = ET_t[:, 1, cidx, :].unsqueeze(1).broadcast_to([128, NB, 16])
                            xpr = xpb.t[:, 0, :].unsqueeze(2).broadcast_to([128, NB, 16])
                            xpi = xpb.t[:, 1, :].unsqueeze(2).broadcast_to([128, NB, 16])
                            e1 = T1b.t[:].rearrange("p a c -> p (a c)")[:, 0:NCH].rearrange("p (c k) -> p c k", k=16)
                            e2 = T2b.t[:].rearrange("p a c -> p (a c)")[:, 0:NCH].rearrange("p (c k) -> p c k", k=16)
                            op(DVE, lambda: nc.vector.tensor_tensor(out=e1, in0=etr, in1=xpr, op=ALU.mult), R=[xpb], W=[T1b])
                            op(DVE, lambda: nc.vector.tensor_tensor(out=e2, in0=eti, in1=xpi, op=ALU.mult), R=[xpb], W=[T2b])
                            op(DVE, lambda: nc.vector.tensor_tensor(out=e1, in0=e1, in1=e2, op=ALU.subtract), R=[T2b], W=[T1b])
                            op(DVE, lambda: nc.vector.tensor_tensor(out=L4[:, 0, :, :], in0=L4[:, 0, :, :], in1=e1, op=ALU.add), R=[T1b], W=[src])
                            op(DVE, lambda: nc.vector.tensor_tensor(out=e1, in0=etr, in1=xpi, op=ALU.mult), R=[xpb], W=[T1b])
                            op(DVE, lambda: nc.vector.tensor_tensor(out=e2, in0=eti, in1=xpr, op=ALU.mult), R=[xpb], W=[T2b])
                            op(DVE, lambda: nc.vector.tensor_tensor(out=e1, in0=e1, in1=e2, op=ALU.add), R=[T2b], W=[T1b])
                            op(DVE, lambda: nc.vector.tensor_tensor(out=L4[:, 1, :, :], in0=L4[:, 1, :, :], in1=e1, op=ALU.add), R=[T1b], W=[src])
                            Z = src.t
                            if d == 0:
                                op(ACT, lambda: nc.scalar.copy(out=Xb_t[:, slot8, :, 1:NCH], in_=Z[:, :, 0:NCH - 1]), R=[src], W=[Xbb])
                            else:
                                op(ACT, lambda: nc.scalar.copy(out=Xb_t[:, slot8, :, 0:NCH - 1], in_=Z[:, :, 1:NCH]), R=[src], W=[Xbb])
                    for d in range(2):
                        for tau in range(8):
                            kb = kring.next()
                            mms = []
                            for gpl in range(4):
                                s8 = gpl * 2 + d
                                for part in range(2):
                                    mms.append((kb.t[:, 0:128], BbZ_t[:, s8 * 2 + part, :], CL_t[:, s8 * 18 + tau * 2 + part, :]))
                            mm_group(kb, mms, R=[BbZb, CLb])
                            op(ACT, lambda: nc.scalar.copy(out=BD_t[:, d * 8 + tau, :], in_=kb.t[:, 0:128]), R=[kb], W=[BDb])
                    zv = zbuf_t[:].rearrange("p (c i) -> p c i", i=8)
                    for j in range(8):
                        for (cb0, n) in cbs:
                            ab = accr.next()
                            mms = []
                            for i in range(0, j + 1):
                                mms.append((ab.t[:, :n], BD_t[:, 0 * 8 + (j - i), :], uv[:, cb0:cb0 + n, i]))
                            for i in range(j, 8):
                                mms.append((ab.t[:, :n], BD_t[:, 1 * 8 + (i - j), :], uv[:, cb0:cb0 + n, i]))
                            for gpl in range(4):
                                for d in range(2):
                                    s8 = gpl * 2 + d
                                    tau = (j + 1) if d == 0 else (8 - j)
                                    for part in range(2):
                                        mms.append((ab.t[:, :n], CL_t[:, s8 * 18 + tau * 2 + part, :], Xb_t[:, s8, part, cb0:cb0 + n]))
                            mm_group(ab, mms, R=[BDb, uTb, CLb, Xbb])
                            yt = ytr.next()
                            op(DVE, lambda: nc.vector.scalar_tensor_tensor(out=yt.t[:, :n], in0=uv[:, cb0:cb0 + n, j], scalar=gcol(GV_SSMD + gc), in1=ab.t[:, :n], op0=ALU.mult, op1=ALU.add), R=[ab, uTb], W=[yt])
                            op(ACT, lambda: nc.scalar.activation(out=zv[:, cb0:cb0 + n, j], in_=yt.t[:, :n], func=AF.Gelu), R=[yt], W=[zb])
                    dma(zb.slot, zT_d[gc * 128:(gc + 1) * 128, :], zbuf_t[:], R=[zb])
                barrier()

        conv_jobs = []
        for (src_, dst_, rows_, cols_, g0_) in [(w_glu, wglub, 512, 512, None), (w_sp, wspb, 512, D, None), (w_ap, wapb, D, D, None),
                                                  (w_out, woutb, D, D, None), (w1, w1b, D, DFF, GV_MLP), (w2, w2b, DFF, D, None)]:
            for k_ in range(rows_ // 128):
                for c0_ in range(0, cols_, 2048):
                    conv_jobs.append((src_, dst_, k_, c0_, min(2048, cols_ - c0_), g0_))
        conv_state = {"i": 0, "stg": None, "ost": None}

        def conv_step(nj):
            for _ in range(nj):
                if conv_state["i"] >= len(conv_jobs):
                    return
                (src_, dst_, k_, c0_, n_, g0_) = conv_jobs[conv_state["i"]]
                conv_state["i"] += 1
                st = conv_state["stg"].next()
                dma(st.slot, st.t[:, :n_], src_[k_ * 128:(k_ + 1) * 128, c0_:c0_ + n_], W=[st])
                ob = conv_state["ost"].next()
                if g0_ is None:
                    op(DVE, lambda: nc.vector.tensor_copy(out=ob.t[:, :n_], in_=st.t[:, :n_]), R=[st], W=[ob])
                else:
                    op(DVE, lambda: nc.vector.tensor_scalar(out=ob.t[:, :n_], in0=st.t[:, :n_], scalar1=gcol(g0_ + k_), scalar2=None, op0=ALU.mult), R=[st], W=[ob])
                dma(ob.slot, dst_[k_ * 128:(k_ + 1) * 128, c0_:c0_ + n_], ob.t[:, :n_], R=[ob])

        with ExitStack() as ph_att:
            kT_t = ph_att.enter_context(_sb("kT", [128, 2, LP], BF16))
            Vx_t = ph_att.enter_context(_sb("Vx", [128, NT, 4, 65], BF16))
            kTb, Vxb = Buf(kT_t), Buf(Vx_t)
            op(POOL, lambda: nc.gpsimd.memset(Vx_t[:].rearrange("p a b c -> p (a b c)"), 1.0), W=[Vxb])
            with ExitStack() as ph:
                slot_ptr[0] = base_slot_ptr
                chunks = [("q", i, i * 128) for i in range(8)] + [("k", i, 1024 + i * 128) for i in range(2)] + [("v", i, 1280 + i * 128) for i in range(2)]
                in_proj(ph, chunks, 512, 1536, kTb=kTb, Vxb=Vxb)
                barrier()
            slot_ptr[0] = base_slot_ptr
            ph = ph_att
            conv_state["stg"] = Ring([Buf(ph.enter_context(_sb(f"cst{i}", [128, 2048], F32)), new_slot()) for i in range(3)])
            conv_state["ost"] = Ring([Buf(ph.enter_context(_sb(f"cso{i}", [128, 2048], BF16)), new_slot()) for i in range(3)])
            Qr = Ring([Buf(ph.enter_context(_sb(f"Q{i}", [128, LP], BF16)), new_slot()) for i in range(2)])
            PTr = Ring([Buf(ph.enter_context(_sb(f"PT{i}", [128, 2, 512], BF16))) for i in range(3)])
            recb = Buf(ph.enter_context(_sb("rec", [128, 512], F32)))
            densb = Buf(ph.enter_context(_sb("densb", [128, 512], F32)))
            yor = Ring([Buf(ph.enter_context(_sb(f"yo{i}", [128, 512], BF16)), new_slot()) for i in range(2)])
            Sr = Ring([Buf(PSA[:, 0:2, :]), Buf(PSA[:, 2:4, :])])
            Or = Ring([banks[4], banks[6]])
            DENb = banks[5]
            pairs = [(h, h + 4) for h in (0, 1, 2, 3)] + [(h, h + 4) for h in (8, 9, 10, 11)]
            for (ha, hb) in pairs:
                ga, gb = ha // 4, hb // 4
                kc = ga // 2
                Qb = Qr.next()
                dma(Qb.slot, Qb.t[0:64, :], qT_d[ha * 64:(ha + 1) * 64, :], W=[Qb])
                dma(Qb.slot, Qb.t[64:128, :], qT_d[hb * 64:(hb + 1) * 64, :], upd=[Qb])
                for (q0, NQ) in colblocks(S):
                    qc0 = 128 + q0
                    Ob = Or.next()
                    pend = None
                    first_pv = True

                    def do_pv(pend, last):
                        nonlocal first_pv
                        PTb, kt = pend
                        f_ = first_pv
                        first_pv = False
                        for b in (PTb, Vxb, cmb):
                            PE.wait(b.wr)
                        if f_:
                            for bk in (Ob, DENb):
                                PE.wait(bk.wr)
                                PE.wait(list(bk.rd.values()))
                        nc.tensor.matmul(Ob.t[0:64, :NQ], lhsT=Vx_t[:, kt, ga, 0:64], rhs=PTb.t[:, 0, :NQ], start=f_, stop=last)
                        nc.tensor.matmul(Ob.t[64:128, :NQ], lhsT=Vx_t[:, kt, gb, 0:64], rhs=PTb.t[:, 1, :NQ], start=f_, stop=last)
                        nc.tensor.matmul(DENb.t[0:64, :NQ], lhsT=ones_b[:, 0:64], rhs=PTb.t[:, 0, :NQ], start=f_, stop=last)
                        inst = nc.tensor.matmul(DENb.t[64:128, :NQ], lhsT=ones_b[:, 0:64], rhs=PTb.t[:, 1, :NQ], start=f_, stop=last)
                        ev = PE.mark(inst)
                        PTb.rd[PE.name] = ev
                        if last:
                            for bk in (Ob, DENb):
                                bk.wr = ev
                                bk.rd = {}

                    for kt in range(NT):
                        Sb = Sr.next()
                        mms = [(Sb.t[:, 0, :NQ], kT_t[0:64, kc, kt * 128:(kt + 1) * 128], Qb.t[0:64, qc0:qc0 + NQ], True, True),
                               (Sb.t[:, 1, :NQ], kT_t[64:128, kc, kt * 128:(kt + 1) * 128], Qb.t[64:128, qc0:qc0 + NQ], True, True)]
                        mm_group(Sb, mms, R=[kTb, Qb])
                        if pend is not None:
                            do_pv(pend, False)
                        PTb = PTr.next()
                        if kt == 0:
                            op(ACT, lambda: nc.scalar.activation(out=PTb.t[:, :, :NQ], in_=Sb.t[:, :, :NQ], func=AF.Exp, scale=0.125, bias=gcol(GV_MASK)), R=[Sb], W=[PTb])
                        else:
                            op(ACT, lambda: nc.scalar.activation(out=PTb.t[:, :, :NQ], in_=Sb.t[:, :, :NQ], func=AF.Exp, scale=0.125), R=[Sb], W=[PTb])
                        pend = (PTb, kt)
                    do_pv(pend, True)
                    op(DVE, lambda: nc.vector.tensor_copy(out=densb.t[:, :NQ], in_=DENb.t[:, :NQ]), R=[DENb], W=[densb])
                    op(DVE, lambda: nc.vector.reciprocal(out=recb.t[:, :NQ], in_=densb.t[:, :NQ]), R=[densb], W=[recb])
                    yo = yor.next()
                    op(DVE, lambda: nc.vector.tensor_tensor(out=yo.t[:, :NQ], in0=Ob.t[:, :NQ], in1=recb.t[:, :NQ], op=ALU.mult), R=[Ob, recb], W=[yo])
                    dma(yo.slot, yaT_d[ha * 64:(ha + 1) * 64, q0:q0 + NQ], yo.t[0:64, :NQ], R=[yo])
                    dma(yo.slot, yaT_d[hb * 64:(hb + 1) * 64, q0:q0 + NQ], yo.t[64:128, :NQ], R=[yo])
                    conv_step(1)
            conv_step(len(conv_jobs))
            barrier()

        with ExitStack() as ph:
            slot_ptr[0] = base_slot_ptr

            def wload(name, src, kchunks, cols, c0=0):
                t = ph.enter_context(_sb(name, [128, kchunks, cols], BF16))
                b = Buf(t, new_slot())
                dma(b.slot, t[:], kview(src)[:, :, c0:c0 + cols], W=[b])
                return b
            Wg = wload("Wg", winb, 8, 2048, 2048)
            Wgl = wload("Wgl", wglub, 4, 512)
            Wsp = wload("Wsp", wspb, 4, D)
            Wap = wload("Wap", wapb, 8, D)
            Wo = wload("Wo", woutb, 8, D)
            xring = Ring([Buf(ph.enter_context(_sb(f"xt{i}", [128, 8, 512], F32)), new_slot()) for i in range(2)])
            sqb_ = Buf(ph.enter_context(_sb("sq3", [128, 8, 512], BF16)))
            xbb_ = Buf(ph.enter_context(_sb("xb3", [128, 8, 512], BF16)))
            ztr = Ring([Buf(ph.enter_context(_sb(f"zt{i}", [128, 4, 512], BF16)), new_slot()) for i in range(2)])
            yar = Ring([Buf(ph.enter_context(_sb(f"ya{i}", [128, 8, 512], BF16)), new_slot()) for i in range(2)])
            ysb = Buf(ph.enter_context(_sb("ys", [128, 4, 512], BF16)))
            mgb = Buf(ph.enter_context(_sb("mg", [128, 8, 512], BF16)))
            f32r = Ring([Buf(ph.enter_context(_sb(f"f3r{i}", [128, 512], F32))) for i in range(10)])
            rstdb = Buf(ph.enter_context(_sb("rstd3", [128, 512], F32)))
            h1r = Ring([Buf(ph.enter_context(_sb(f"h1s{i}", [128, 512], F32)), new_slot()) for i in range(3)])
            pr = Ring(banks)
            xTv = kview(xT)
            zTv = kview(zT_d)
            yaTv = kview(yaT_d)
            N = 512
            rstdr3 = Ring([rstdb, Buf(ph.enter_context(_sb("rstd3b", [128, 512], F32)))])
            tiles3 = colblocks(S)

            def prologue3(ti):
                q0 = tiles3[ti][0]
                c0 = 128 + q0
                xt = xring.next()
                dma(xt.slot, xt.t[:], xTv[:, :, c0:c0 + N], W=[xt])
                zt = ztr.next()
                dma(zt.slot, zt.t[:], zTv[:, :, c0:c0 + N], W=[zt])
                ya = yar.next()
                dma(ya.slot, ya.t[:], yaTv[:, :, q0:q0 + N], W=[ya])
                op(ACT, lambda: nc.scalar.activation(out=sqb_.t[:], in_=xt.t[:], func=AF.Square), R=[xt], W=[sqb_])
                op(DVE, lambda: nc.vector.tensor_copy(out=xbb_.t[:], in_=xt.t[:]), R=[xt], W=[xbb_])
                ssb = pr.next()
                mm_group(ssb, [(ssb.t[:], ones_b, sqb_.t[:, k, :]) for k in range(8)], R=[sqb_, cmb])
                rt = f32r.next()
                op(ACT, lambda: nc.scalar.activation(out=rt.t[:], in_=ssb.t[:], func=AF.Ln, scale=1.0 / D, bias=eps_c), R=[ssb], W=[rt])
                rs_ = rstdr3.next()
                op(ACT, lambda: nc.scalar.activation(out=rs_.t[:], in_=rt.t[:], func=AF.Exp, scale=-0.5), R=[rt], W=[rs_])
                return dict(q0=q0, xt=xt, zt=zt, ya=ya, rstd=rs_)

            def main3(P):
                zt, ya, rs_ = P["zt"], P["ya"], P["rstd"]
                for oc in range(4):
                    ps = pr.next()
                    mm_group(ps, [(ps.t[:], Wgl.t[:, k, oc * 128:(oc + 1) * 128], zt.t[:, k, :]) for k in range(4)], R=[Wgl, zt])
                    sg = f32r.next()
                    op(ACT, lambda: nc.scalar.activation(out=sg.t[:], in_=ps.t[:], func=AF.Sigmoid, bias=gcol(GV_BGLU + oc)), R=[ps], W=[sg])
                    op(DVE, lambda: nc.vector.tensor_tensor(out=ysb.t[:, oc, :], in0=zt.t[:, oc, :], in1=sg.t[:], op=ALU.mult), R=[zt, sg], W=[ysb])
                for oc in range(8):
                    sgs = []
                    for gi in range(2):
                        ps = pr.next()
                        wc = gi * 1024 + oc * 128
                        mm_group(ps, [(ps.t[:], Wg.t[:, k, wc:wc + 128], xbb_.t[:, k, :]) for k in range(8)], R=[Wg, xbb_])
                        tg = f32r.next()
                        op(DVE, lambda: nc.vector.tensor_tensor(out=tg.t[:], in0=ps.t[:], in1=rs_.t[:], op=ALU.mult), R=[ps, rs_], W=[tg])
                        sg = f32r.next()
                        op(ACT, lambda: nc.scalar.activation(out=sg.t[:], in_=tg.t[:], func=AF.Sigmoid), R=[tg], W=[sg])
                        sgs.append(sg)
                    ps = pr.next()
                    mm_group(ps, [(ps.t[:], Wsp.t[:, k, oc * 128:(oc + 1) * 128], ysb.t[:, k, :]) for k in range(4)], R=[Wsp, ysb])
                    m1 = f32r.next()
                    op(DVE, lambda: nc.vector.tensor_tensor(out=m1.t[:], in0=ps.t[:], in1=sgs[0].t[:], op=ALU.mult), R=[ps, sgs[0]], W=[m1])
                    ps2 = pr.next()
                    mm_group(ps2, [(ps2.t[:], Wap.t[:, k, oc * 128:(oc + 1) * 128], ya.t[:, k, :]) for k in range(8)], R=[Wap, ya])
                    m2 = f32r.next()
                    op(DVE, lambda: nc.vector.tensor_tensor(out=m2.t[:], in0=ps2.t[:], in1=sgs[1].t[:], op=ALU.mult), R=[ps2, sgs[1]], W=[m2])
                    op(DVE, lambda: nc.vector.tensor_tensor(out=mgb.t[:, oc, :], in0=m1.t[:], in1=m2.t[:], op=ALU.add), R=[m1, m2], W=[mgb])

            def outproj3(P):
                q0, xt = P["q0"], P["xt"]
                for oc in range(8):
                    ps = pr.next()
                    mm_group(ps, [(ps.t[:], Wo.t[:, k, oc * 128:(oc + 1) * 128], mgb.t[:, k, :]) for k in range(8)], R=[Wo, mgb])
                    h1 = h1r.next()
                    op(DVE, lambda: nc.vector.tensor_tensor(out=h1.t[:], in0=ps.t[:], in1=xt.t[:, oc, :], op=ALU.add), R=[ps, xt], W=[h1])
                    dma(h1.slot, h1T_d[oc * 128:(oc + 1) * 128, q0:q0 + N], h1.t[:], R=[h1])

            Pn = prologue3(0)
            for ti in range(len(tiles3)):
                P = Pn
                main3(P)
                if ti + 1 < len(tiles3):
                    Pn = prologue3(ti + 1)
                outproj3(P)
            barrier()

        with ExitStack() as ph:
            slot_ptr[0] = base_slot_ptr
            W1_t = ph.enter_context(_sb("W1", [128, 8, DFF], BF16))
            W2_t = ph.enter_context(_sb("W2", [128, 32, D], BF16))
            W1b_, W2b_ = Buf(W1_t, new_slot()), Buf(W2_t, new_slot())
            dma(W1b_.slot, W1_t[:], kview(w1b), W=[W1b_])
            dma(W2b_.slot, W2_t[:], kview(w2b), W=[W2b_])
            N = 256
            hr = Ring([Buf(ph.enter_context(_sb(f"h{i}", [128, 8, N], F32)), new_slot()) for i in range(2)])
            sqb_ = Buf(ph.enter_context(_sb("sq4", [128, 8, N], BF16)))
            hbb = Buf(ph.enter_context(_sb("hb4", [128, 8, N], BF16)))
            Ab = Buf(ph.enter_context(_sb("A4", [128, 32, N], BF16)))
            h2b = Buf(ph.enter_context(_sb("h2", [128, 8, N], F32)))
            osr = Ring([Buf(ph.enter_context(_sb(f"os{i}", [128, 8, N], F32)), new_slot()) for i in range(1)])
            f32r = Ring([Buf(ph.enter_context(_sb(f"f4r{i}", [128, N], F32))) for i in range(6)])
            rstdb = Buf(ph.enter_context(_sb("rstd4", [128, N], F32)))
            rstd5 = Buf(ph.enter_context(_sb("rstd5", [128, N], F32)))
            pr = Ring(banks)
            h1v = kview(h1T_d)
            outv = kview(outT)
            sq5b = Buf(ph.enter_context(_sb("sq5", [128, 8, N], BF16)))
            rstdr4 = Ring([rstdb, Buf(ph.enter_context(_sb("rstd4b", [128, N], F32)))])
            tiles4 = colblocks(S, N)

            def prologue4(ti):
                q0 = tiles4[ti][0]
                hb = hr.next()
                dma(hb.slot, hb.t[:], h1v[:, :, q0:q0 + N], W=[hb])
                op(ACT, lambda: nc.scalar.activation(out=sqb_.t[:], in_=hb.t[:], func=AF.Square), R=[hb], W=[sqb_])
                op(DVE, lambda: nc.vector.tensor_copy(out=hbb.t[:], in_=hb.t[:]), R=[hb], W=[hbb])
                ssb = pr.next()
                mm_group(ssb, [(ssb.t[:, :N], ones_b, sqb_.t[:, k, :]) for k in range(8)], R=[sqb_, cmb])
                rt = f32r.next()
                op(ACT, lambda: nc.scalar.activation(out=rt.t[:], in_=ssb.t[:, :N], func=AF.Ln, scale=1.0 / D, bias=eps_c), R=[ssb], W=[rt])
                rs_ = rstdr4.next()
                op(ACT, lambda: nc.scalar.activation(out=rs_.t[:], in_=rt.t[:], func=AF.Exp, scale=-0.5), R=[rt], W=[rs_])
                return dict(q0=q0, hb=hb, rstd=rs_)

            def stage1_4(P):
                rs_ = P["rstd"]
                for f in range(32):
                    ps = pr.next()
                    mm_group(ps, [(ps.t[:, :N], W1_t[:, k, f * 128:(f + 1) * 128], hbb.t[:, k, :]) for k in range(8)], R=[W1b_, hbb])
                    tr_ = f32r.next()
                    op(DVE, lambda: nc.vector.scalar_tensor_tensor(out=tr_.t[:], in0=ps.t[:, :N], scalar=0.0, in1=rs_.t[:], op0=ALU.max, op1=ALU.mult), R=[ps, rs_], W=[tr_])
                    op(ACT, lambda: nc.scalar.activation(out=Ab.t[:, f, :], in_=tr_.t[:], func=AF.Square), R=[tr_], W=[Ab])

            def stage2_4(P):
                hb = P["hb"]
                for oc in range(8):
                    ps = pr.next()
                    mm_group(ps, [(ps.t[:, :N], W2_t[:, f, oc * 128:(oc + 1) * 128], Ab.t[:, f, :]) for f in range(32)], R=[W2b_, Ab])
                    op(DVE, lambda: nc.vector.tensor_tensor(out=h2b.t[:, oc, :], in0=ps.t[:, :N], in1=hb.t[:, oc, :], op=ALU.add), R=[ps, hb], W=[h2b])

            def final4(P):
                q0 = P["q0"]
                op(ACT, lambda: nc.scalar.activation(out=sq5b.t[:], in_=h2b.t[:], func=AF.Square), R=[h2b], W=[sq5b])
                ssb = pr.next()
                mm_group(ssb, [(ssb.t[:, :N], ones_b, sq5b.t[:, k, :]) for k in range(8)], R=[sq5b, cmb])
                rt = f32r.next()
                op(ACT, lambda: nc.scalar.activation(out=rt.t[:], in_=ssb.t[:, :N], func=AF.Ln, scale=1.0 / D, bias=eps_c), R=[ssb], W=[rt])
                op(ACT, lambda: nc.scalar.activation(out=rstd5.t[:], in_=rt.t[:], func=AF.Exp, scale=-0.5), R=[rt], W=[rstd5])
                os_ = osr.next()
                for oc in range(8):
                    op(DVE, lambda: nc.vector.scalar_tensor_tensor(out=os_.t[:, oc, :], in0=h2b.t[:, oc, :], scalar=gcol(GV_FIN + oc), in1=rstd5.t[:], op0=ALU.mult, op1=ALU.mult), R=[h2b, rstd5], W=[os_])
                dma(os_.slot, outv[:, :, q0:q0 + N], os_.t[:], R=[os_])

            Pn = prologue4(0)
            Pprev = None
            for ti in range(len(tiles4)):
                P = Pn
                stage1_4(P)
                if ti + 1 < len(tiles4):
                    Pn = prologue4(ti + 1)
                if Pprev is not None:
                    final4(Pprev)
                stage2_4(P)
                Pprev = P
            final4(Pprev)
            barrier()
    return nc


def _const_tables(S):
    LP = S + 128
    cm = np.zeros((128, 512), np.float32)
    cm[:, 0:128] = 1.0
    cm[0:64, 128:192] = 1.0
    cm[64:128, 192:256] = 1.0
    for i in range(64):
        cm[2 * i + 1, 256 + 2 * i] = -1.0
        cm[2 * i, 256 + 2 * i + 1] = 1.0
    cm[:, 384:512] = np.eye(128, dtype=np.float32)
    rows = S // 64
    row_id = np.repeat(np.arange(rows, dtype=np.float32), 64)
    col_id = np.tile(np.arange(64, dtype=np.float32), rows)
    inv_freq = (np.float32(10000.0) ** (-np.arange(16, dtype=np.float32) / np.float32(16))).astype(np.float32)
    ang = np.concatenate([row_id[:, None] * inv_freq, col_id[:, None] * inv_freq], axis=-1).astype(np.float32)
    ang = np.concatenate([np.zeros((128, 32), np.float32), ang], axis=0)
    cos = np.cos(ang).astype(np.float32)
    sin = np.sin(ang).astype(np.float32)
    pair = (np.arange(128) % 64) // 2
    C = np.ascontiguousarray(cos[:, pair].T)
    Sn = np.ascontiguousarray(sin[:, pair].T)
    return cm, C, Sn


def _prep_shared(inp, S):
    f = lambda a: np.ascontiguousarray(np.asarray(a, dtype=np.float32))
    cm, C, Sn = _const_tables(S)
    gv = np.zeros((128, NG), np.float32)
    gv[:, GV_MIX:GV_MIX + 8] = f(inp["norm_mix_g"])[0].reshape(8, 128).T
    gv[:, GV_MLP:GV_MLP + 8] = f(inp["norm_mlp_g"])[0].reshape(8, 128).T
    gv[:, GV_FIN:GV_FIN + 8] = f(inp["norm_final_g"]).reshape(8, 128).T
    gv[:, GV_QG] = np.tile(f(inp["q_norm_g"])[0], 2)
    gv[:, GV_KG] = np.tile(f(inp["k_norm_g"])[0], 2)
    gv[:, GV_BGLU:GV_BGLU + 4] = f(inp["b_glu"])[0].reshape(4, 128).T
    gv[:, GV_SSMD:GV_SSMD + 4] = f(inp["ssm_d"])[0].reshape(4, 128).T
    gv[0:112, GV_MASK] = -30000.0
    a_re, a_im, ldt = f(inp["ssm_a_re"])[0], f(inp["ssm_a_im"])[0], f(inp["ssm_log_dt"])[0]
    b_re, b_im = f(inp["ssm_b_re"])[0], f(inp["ssm_b_im"])[0]
    c_re, c_im = f(inp["ssm_c_re"])[0], f(inp["ssm_c_im"])[0]
    ssmA = np.zeros((128, 96), np.float32)
    bz = np.zeros((32, 128, 4, 128), np.float32)
    for d in range(2):
        for gp in range(16):
            ci = d * 16 + gp
            for g2 in range(2):
                g = 2 * gp + g2
                gl = g % 8
                ps = slice(g2 * 64, g2 * 64 + 64)
                ssmA[ps, ci] = a_re[d, g]
                ssmA[ps, 32 + ci] = a_im[d, g]
                ssmA[ps, 64 + ci] = ldt[d, g]
                bz[ci, ps, 0, gl * 16:(gl + 1) * 16] = b_re[d, g]
                bz[ci, ps, 1, gl * 16:(gl + 1) * 16] = b_im[d, g]
                bz[ci, ps, 2, gl * 16:(gl + 1) * 16] = c_re[d, g].T
                bz[ci, ps, 3, gl * 16:(gl + 1) * 16] = c_im[d, g].T
    shared = {
        "w_in": f(inp["w_in"])[0], "w_glu": f(inp["w_glu"])[0], "w_sp": f(inp["w_ssm_proj"])[0],
        "w_ap": f(inp["w_attn_proj"])[0], "w_out": f(inp["w_out"])[0], "w1": f(inp["w_mlp_in"])[0],
        "w2": f(inp["w_mlp_out"])[0], "gv": gv, "cmat": cm, "ropeC": C, "ropeS": Sn, "ssmA": ssmA,
        "bz": bz.reshape(32, 128, 512),
    }
    return shared


def _make_xT(xb, meta):
    S = xb.shape[0]
    full = np.concatenate([np.zeros((112, D), np.float32), np.asarray(meta, np.float32), np.asarray(xb, np.float32)], axis=0)
    return np.ascontiguousarray(full.T)


_NC_CACHE = {}


def kernel(**inputs):
    x = np.asarray(inputs["x"], dtype=np.float32)
    B, S, _ = x.shape
    shared = _prep_shared(inputs, S)
    if S not in _NC_CACHE:
        _NC_CACHE[S] = build(S)
    nc = _NC_CACHE[S]
    in_maps = []
    for b in range(B):
        m = dict(shared)
        m["xT"] = _make_xT(x[b], inputs["meta_tokens"])
        in_maps.append(m)
    res = run_bass_kernel_spmd(nc, in_maps, core_ids=list(range(B)))
    out = np.stack([np.ascontiguousarray(r["outT"].T) for r in res.results], axis=0)
    return out.astype(np.float32)
```
